# Optimizing a Trainium2 kernel written in Bass

```python
import math
import jax, jax.numpy as jnp
from jax import lax
import numpy as np

D_MODEL = 1024
BATCH = 4
SEQ = 8192
DEPTH = 2

CHUNK = 64
LEFT_CHUNKS = 8
BAND = (LEFT_CHUNKS + 1) * CHUNK
S5_WIDTH = D_MODEL // 2
S5_GROUP = 16
S5_GROUPS = S5_WIDTH // S5_GROUP
S5_STATE = 64
CONV_WIDTH = D_MODEL - S5_WIDTH
CONV_KERNEL = 31
ATT_HEADS = 16
ATT_HEAD_DIM = D_MODEL // ATT_HEADS
MAX_REL = 128
MEM_LEN = 256
XA_HEADS = 4
XA_HEAD_DIM = D_MODEL // XA_HEADS
EPS = 1e-6
N_EVEN = (DEPTH + 1) // 2
N_ODD = DEPTH // 2
EVEN_IN = 2 * S5_WIDTH + 3 * CONV_WIDTH
ODD_IN = 4 * D_MODEL

kernel_name = 'streaming_hybrid_s5_conv_chunkattn'


def rms_norm(x, g):
    xf = x.astype(jnp.float32)
    y = xf * lax.rsqrt(jnp.mean(xf * xf, axis=-1, keepdims=True) + EPS)
    return (y * g.astype(jnp.float32)).astype(x.dtype)


def _complex_affine_combine(e1, e2):
    a1r, a1i, b1r, b1i = e1
    a2r, a2i, b2r, b2i = e2
    ar = a2r * a1r - a2i * a1i
    ai = a2r * a1i + a2i * a1r
    br = a2r * b1r - a2i * b1i + b2r
    bi = a2r * b1i + a2i * b1r + b2i
    return (ar, ai, br, bi)


def s5_mixer(u, lam_re, lam_im, log_dt, b_re, b_im, c_re, c_im, d_skip, glu_w, glu_b):
    bsz, L, _ = u.shape
    f32 = jnp.float32
    uf = u.astype(f32)
    ug = uf.reshape(bsz, L, S5_GROUPS, S5_GROUP)
    lr = lam_re.astype(f32)
    li = lam_im.astype(f32)
    dt = jnp.exp(log_dt.astype(f32))[:, None]
    mag = jnp.exp(lr * dt)
    ab_re = mag * jnp.cos(li * dt)
    ab_im = mag * jnp.sin(li * dt)
    den = lr * lr + li * li
    nr = ab_re - 1.0
    coef_re = (nr * lr + ab_im * li) / den
    coef_im = (ab_im * lr - nr * li) / den
    br = b_re.astype(f32)
    bi = b_im.astype(f32)
    bb_re = coef_re[..., None] * br - coef_im[..., None] * bi
    bb_im = coef_re[..., None] * bi + coef_im[..., None] * br
    bu_re = jnp.einsum('blgc,gpc->blgp', ug, bb_re)
    bu_im = jnp.einsum('blgc,gpc->blgp', ug, bb_im)
    a_re = jnp.broadcast_to(ab_re, bu_re.shape)
    a_im = jnp.broadcast_to(ab_im, bu_im.shape)
    _, _, h_re, h_im = lax.associative_scan(_complex_affine_combine, (a_re, a_im, bu_re, bu_im), axis=1)
    y = (jnp.einsum('blgp,gcp->blgc', h_re, c_re.astype(f32))
         - jnp.einsum('blgp,gcp->blgc', h_im, c_im.astype(f32)))
    y = y.reshape(bsz, L, S5_WIDTH) + d_skip.astype(f32) * uf
    z = jax.nn.gelu(y)
    out = z * jax.nn.sigmoid(z @ glu_w.astype(f32) + glu_b.astype(f32))
    return out.astype(u.dtype)


def conv_module(val, glu_gate, conv_w, conv_b, ln_g, ln_b):
    v = val * jax.nn.sigmoid(glu_gate)
    vpad = jnp.pad(v, ((0, 0), (CONV_KERNEL - 1, 0), (0, 0)))
    c = lax.conv_general_dilated(vpad, conv_w[:, None, :].astype(v.dtype), window_strides=(1,),
                                 padding='VALID', dimension_numbers=('NWC', 'WIO', 'NWC'),
                                 feature_group_count=CONV_WIDTH) + conv_b.astype(v.dtype)
    cf = c.astype(jnp.float32)
    mu = jnp.mean(cf, axis=-1, keepdims=True)
    var = jnp.mean(jnp.square(cf - mu), axis=-1, keepdims=True)
    cn = (cf - mu) * lax.rsqrt(var + EPS) * ln_g.astype(jnp.float32) + ln_b.astype(jnp.float32)
    return jax.nn.silu(cn).astype(val.dtype)


def ssm_conv_layer(h, w_in, lam_re, lam_im, log_dt, b_re, b_im, c_re, c_im, d_skip, glu_w, glu_b,
                   conv_w, conv_b, ln_g, ln_b, w_out):
    z = h @ w_in
    cuts = [S5_WIDTH, 2 * S5_WIDTH, 2 * S5_WIDTH + CONV_WIDTH, 2 * S5_WIDTH + 2 * CONV_WIDTH]
    u_a, gate_a, val_b, glu_b_in, gate_b = jnp.split(z, cuts, axis=-1)
    y_a = s5_mixer(u_a, lam_re, lam_im, log_dt, b_re, b_im, c_re, c_im, d_skip, glu_w, glu_b) * jax.nn.silu(gate_a)
    y_b = conv_module(val_b, glu_b_in, conv_w, conv_b, ln_g, ln_b) * jax.nn.silu(gate_b)
    return jnp.concatenate([y_a, y_b], axis=-1) @ w_out


def chunk_attention_layer(h, w_in, rel_bias, w_out):
    bsz, L, _ = h.shape
    q, k, v, g = jnp.split(h @ w_in, 4, axis=-1)
    q = q.reshape(bsz, L, ATT_HEADS, ATT_HEAD_DIM)
    k = k.reshape(bsz, L, ATT_HEADS, ATT_HEAD_DIM)
    v = v.reshape(bsz, L, ATT_HEADS, ATT_HEAD_DIM)
    pad = BAND - CHUNK
    kp = jnp.pad(k, ((0, 0), (pad, 0), (0, 0), (0, 0)))
    vp = jnp.pad(v, ((0, 0), (pad, 0), (0, 0), (0, 0)))
    qi = jnp.arange(CHUNK)[:, None]
    kj = jnp.arange(BAND)[None, :]
    rel = jnp.clip(qi - kj + pad, -MAX_REL, MAX_REL) + MAX_REL
    bias = rel_bias.astype(jnp.float32)[:, rel]
    scale = ATT_HEAD_DIM ** -0.5

    def one_chunk(c):
        start = c * CHUNK
        qc = lax.dynamic_slice_in_dim(q, start, CHUNK, axis=1)
        kc = lax.dynamic_slice_in_dim(kp, start, BAND, axis=1)
        vc = lax.dynamic_slice_in_dim(vp, start, BAND, axis=1)
        s = jnp.einsum('bqhd,bkhd->bhqk', qc, kc).astype(jnp.float32) * scale + bias
        valid = (start - pad + jnp.arange(BAND)) >= 0
        s = jnp.where(valid[None, None, None, :], s, -1e30)
        p = jax.nn.softmax(s, axis=-1).astype(h.dtype)
        return jnp.einsum('bhqk,bkhd->bqhd', p, vc)

    o = lax.map(one_chunk, jnp.arange(L // CHUNK))
    o = jnp.moveaxis(o, 0, 1).reshape(bsz, L, D_MODEL)
    return (o * jax.nn.silu(g)) @ w_out


def mem_cross_attention(h, mem_n, w_qg, w_kv, w_o):
    bsz, L, _ = h.shape
    q, g = jnp.split(h @ w_qg, 2, axis=-1)
    k, v = jnp.split(mem_n @ w_kv, 2, axis=-1)
    q = q.reshape(bsz, L, XA_HEADS, XA_HEAD_DIM)
    k = k.reshape(bsz, -1, XA_HEADS, XA_HEAD_DIM)
    v = v.reshape(bsz, -1, XA_HEADS, XA_HEAD_DIM)
    s = jnp.einsum('bqhd,bkhd->bhqk', q, k).astype(jnp.float32) * (XA_HEAD_DIM ** -0.5)
    p = jax.nn.softmax(s, axis=-1).astype(h.dtype)
    o = jnp.einsum('bhqk,bkhd->bqhd', p, v).reshape(bsz, L, D_MODEL)
    return (o * jax.nn.silu(g)) @ w_o


def setup_inputs(seed: int = 0) -> dict:
    key = jax.random.key(seed)
    ks = iter(jax.random.split(key, 40))

    def nrm(shape, scale):
        return jax.random.normal(next(ks), shape, jnp.float32) * scale

    def gain(shape):
        return 1.0 + nrm(shape, 0.02)

    lam_im_base = jnp.pi * jnp.arange(S5_STATE, dtype=jnp.float32)
    return {
        'x': nrm((BATCH, SEQ, D_MODEL), 1.0),
        'mem': nrm((BATCH, MEM_LEN, D_MODEL), 1.0),
        'mem_norm_g': gain((D_MODEL,)),
        'ev_norm_g': gain((N_EVEN, D_MODEL)),
        'ev_w_in': nrm((N_EVEN, D_MODEL, EVEN_IN), D_MODEL ** -0.5),
        'ev_s5_lambda_re': -0.5 + nrm((N_EVEN, S5_GROUPS, S5_STATE), 0.01),
        'ev_s5_lambda_im': lam_im_base + nrm((N_EVEN, S5_GROUPS, S5_STATE), 0.01),
        'ev_s5_log_dt': jax.random.uniform(next(ks), (N_EVEN, S5_GROUPS), jnp.float32,
                                           math.log(1e-3), math.log(1e-1)),
        'ev_s5_b_re': nrm((N_EVEN, S5_GROUPS, S5_STATE, S5_GROUP), (2 * S5_GROUP) ** -0.5),
        'ev_s5_b_im': nrm((N_EVEN, S5_GROUPS, S5_STATE, S5_GROUP), (2 * S5_GROUP) ** -0.5),
        'ev_s5_c_re': nrm((N_EVEN, S5_GROUPS, S5_GROUP, S5_STATE), S5_STATE ** -0.5),
        'ev_s5_c_im': nrm((N_EVEN, S5_GROUPS, S5_GROUP, S5_STATE), S5_STATE ** -0.5),
        'ev_s5_d': nrm((N_EVEN, S5_WIDTH), 1.0),
        'ev_s5_glu_w': nrm((N_EVEN, S5_WIDTH, S5_WIDTH), S5_WIDTH ** -0.5),
        'ev_s5_glu_b': nrm((N_EVEN, S5_WIDTH), 0.01),
        'ev_conv_w': nrm((N_EVEN, CONV_KERNEL, CONV_WIDTH), CONV_KERNEL ** -0.5),
        'ev_conv_b': nrm((N_EVEN, CONV_WIDTH), 0.01),
        'ev_conv_ln_g': gain((N_EVEN, CONV_WIDTH)),
        'ev_conv_ln_b': nrm((N_EVEN, CONV_WIDTH), 0.01),
        'ev_w_out': nrm((N_EVEN, D_MODEL, D_MODEL), D_MODEL ** -0.5),
        'od_norm_g': gain((N_ODD, D_MODEL)),
        'od_w_in': nrm((N_ODD, D_MODEL, ODD_IN), D_MODEL ** -0.5),
        'od_rel_bias': nrm((N_ODD, ATT_HEADS, 2 * MAX_REL + 1), 0.1),
        'od_w_out': nrm((N_ODD, D_MODEL, D_MODEL), D_MODEL ** -0.5),
        'xa_norm_g': gain((DEPTH, D_MODEL)),
        'xa_w_qg': nrm((DEPTH, D_MODEL, 2 * D_MODEL), D_MODEL ** -0.5),
        'xa_w_kv': nrm((DEPTH, D_MODEL, 2 * D_MODEL), D_MODEL ** -0.5),
        'xa_w_o': nrm((DEPTH, D_MODEL, D_MODEL), D_MODEL ** -0.5),
        'final_norm_g': gain((D_MODEL,)),
    }


def reference(x, mem, mem_norm_g, ev_norm_g, ev_w_in, ev_s5_lambda_re, ev_s5_lambda_im, ev_s5_log_dt,
              ev_s5_b_re, ev_s5_b_im, ev_s5_c_re, ev_s5_c_im, ev_s5_d, ev_s5_glu_w, ev_s5_glu_b,
              ev_conv_w, ev_conv_b, ev_conv_ln_g, ev_conv_ln_b, ev_w_out,
              od_norm_g, od_w_in, od_rel_bias, od_w_out,
              xa_norm_g, xa_w_qg, xa_w_kv, xa_w_o, final_norm_g):
    mem_n = rms_norm(mem, mem_norm_g)
    for layer in range(DEPTH):
        i = layer // 2
        if layer % 2 == 0:
            x = x + ssm_conv_layer(rms_norm(x, ev_norm_g[i]), ev_w_in[i], ev_s5_lambda_re[i], ev_s5_lambda_im[i],
                                   ev_s5_log_dt[i], ev_s5_b_re[i], ev_s5_b_im[i], ev_s5_c_re[i], ev_s5_c_im[i],
                                   ev_s5_d[i], ev_s5_glu_w[i], ev_s5_glu_b[i], ev_conv_w[i], ev_conv_b[i],
                                   ev_conv_ln_g[i], ev_conv_ln_b[i], ev_w_out[i])
        else:
            x = x + chunk_attention_layer(rms_norm(x, od_norm_g[i]), od_w_in[i], od_rel_bias[i], od_w_out[i])
        x = x + mem_cross_attention(rms_norm(x, xa_norm_g[layer]), mem_n, xa_w_qg[layer], xa_w_kv[layer], xa_w_o[layer])
    return rms_norm(x, final_norm_g)
```

```python
import bisect
import contextlib
import itertools
import numpy as np
import concourse.bass as bass
import concourse.mybir as mybir
from concourse.bass_utils import run_bass_kernel_spmd

F32 = mybir.dt.float32
BF16 = mybir.dt.bfloat16
I32 = mybir.dt.int32
AF = mybir.ActivationFunctionType
ALU = mybir.AluOpType

D = 1024
SEQ = 8192
NB = 4
NCORES = 8
OWN = 4096
HALO = 512
EXT = OWN + HALO
PRE = SEQ // 2 - HALO
MEM = 256
EPS = 1e-6
NEG = -30000.0


class _Op:
    __slots__ = ("eng", "fn", "reads", "writes", "dma", "deps", "sig", "tick", "idx", "dcount",
                 "after")


class Sched:
    NDSEM = 40

    def __init__(self, nc, stack, same_engine_sync=True):
        self.nc = nc
        self.engs = {"pe": nc.tensor, "act": nc.scalar, "dve": nc.vector,
                     "pool": nc.gpsimd, "sp": nc.sync}
        self.same = same_engine_sync
        self.ops = []
        self.flushed = 0
        self.last_write = {}
        self.readers = {}
        self.tick = {e: 0 for e in self.engs}
        self.dcount = {}
        self.waited = {e: {} for e in self.engs}
        self.esem = {e: stack.enter_context(nc.semaphore("sem_" + e)) for e in self.engs}
        self.dsem_pool = [stack.enter_context(nc.semaphore("dsem_%d" % i))
                          for i in range(self.NDSEM)]
        self.dsem = {}
        self.last_eng_op = {}
        self.last_dma_op = {}
        self.n_wait = 0
        self.sigidx = {e: [] for e in self.engs}

    mute = False

    def op(self, eng, fn, reads=(), writes=(), dma=None, after=()):
        if self.mute:
            return None
        o = _Op()
        o.eng = eng
        o.fn = fn
        o.reads = tuple(reads)
        o.writes = tuple(writes)
        o.dma = dma
        o.sig = False
        o.after = tuple(after)
        o.idx = len(self.ops)
        self.ops.append(o)
        return o

    def barrier(self):
        self.flush()
        after = [o for o in self.last_eng_op.values()] + [o for o in self.last_dma_op.values()]
        sp = self.engs["sp"]
        b = self.op("sp", lambda: sp.nop(), after=after)
        for e in ("pe", "act", "dve", "pool"):
            en = self.engs[e]
            self.op(e, (lambda en=en: en.nop()), after=[b])
        self.flush()

    def flush(self, final_wait_keys=()):
        ops = self.ops
        new = ops[self.flushed:]
        for o in new:
            deps = set()
            for k in o.reads:
                if k in self.last_write:
                    deps.add(self.last_write[k])
            for k in o.writes:
                if k in self.last_write:
                    deps.add(self.last_write[k])
                for r in self.readers.get(k, ()):
                    deps.add(r)
            for a in o.after:
                deps.add(a.idx)
            deps.discard(o.idx)
            for k in o.reads:
                self.readers.setdefault(k, []).append(o.idx)
            for k in o.writes:
                self.last_write[k] = o.idx
                self.readers[k] = []
            best = {}
            dd = []
            for d in deps:
                od = ops[d]
                if od.dma is not None:
                    dd.append(d)
                    continue
                if od.eng == o.eng and (od.eng in ("pe", "sp") or not self.same):
                    continue
                if od.eng not in best or best[od.eng] < d:
                    best[od.eng] = d
            o.deps = dd
            for d in best.values():
                od = ops[d]
                if not od.sig:
                    if d >= self.flushed:
                        od.sig = True
                    else:
                        lst = self.sigidx[od.eng]
                        j = bisect.bisect_left(lst, d)
                        d = lst[j]
                o.deps.append(d)
            if o.dma is None:
                self.last_eng_op[o.eng] = o
            else:
                self.last_dma_op[o.dma] = o
        for e, o in self.last_eng_op.items():
            if o.idx >= self.flushed:
                o.sig = True
        for o in new:
            if o.dma is not None:
                if o.dma not in self.dsem:
                    self.dsem[o.dma] = self.dsem_pool[len(self.dsem)]
                self.dcount[o.dma] = self.dcount.get(o.dma, 0) + 1
                o.dcount = self.dcount[o.dma]
            elif o.sig:
                self.tick[o.eng] += 1
                o.tick = self.tick[o.eng]
                self.sigidx[o.eng].append(o.idx)
        for o in new:
            e = self.engs[o.eng]
            w = self.waited[o.eng]
            need = {}
            for d in o.deps:
                od = ops[d]
                if od.dma is not None:
                    s = self.dsem[od.dma]
                    v = 16 * od.dcount
                else:
                    s = self.esem[od.eng]
                    v = od.tick
                if need.get(s, 0) < v:
                    need[s] = v
            for s, v in need.items():
                if w.get(s, 0) < v:
                    e.wait_ge(s, v)
                    w[s] = v
                    self.n_wait += 1
            inst = o.fn()
            if o.dma is not None:
                inst.then_inc(self.dsem[o.dma], 16)
            elif o.sig:
                inst.then_inc(self.esem[o.eng], 1)
            o.fn = None
        self.flushed = len(ops)
        sp = self.engs["sp"]
        for k in final_wait_keys:
            sp.wait_ge(self.dsem[k], 16 * self.dcount[k])


class V:
    __slots__ = ("ap", "keys")

    def __init__(self, ap, keys):
        self.ap = ap
        self.keys = tuple(keys)


class Tile:
    def __init__(self, name, handle, shape, kaxes):
        self.name = name
        self.t = handle
        self.shape = list(shape)
        self.kaxes = kaxes

    def __getitem__(self, idx):
        if not isinstance(idx, tuple):
            idx = (idx,)
        ranges = []
        for a in range(self.kaxes):
            i = idx[1 + a] if len(idx) > 1 + a else slice(None)
            n = self.shape[1 + a]
            if isinstance(i, int):
                ranges.append((i,))
            else:
                ranges.append(tuple(range(*i.indices(n))))
        keys = [(self.name,) + c for c in itertools.product(*ranges)]
        return V(self.t[idx], keys)


def _k(x):
    return x.keys if isinstance(x, V) else ()


def _a(x):
    return x.ap if isinstance(x, V) else x


class Builder:
    def __init__(self, nc):
        self.nc = nc
        self.gstack = contextlib.ExitStack()
        self.S = Sched(nc, self.gstack)
        self.pstack = None
        self.uid = 0
        self.nkey = 0
        self.rr = {}

    def phase_begin(self):
        self.pstack = contextlib.ExitStack()
        self.nkey = 0

    def fresh(self):
        self.nkey += 1
        return "u%d" % self.nkey

    def phase_end(self):
        self.S.barrier()
        self.pstack.close()
        self.pstack = None

    def tile(self, name, shape, dtype=F32, kaxes=0, space="sbuf", persistent=False):
        self.uid += 1
        nm = "%s_%d" % (name, self.uid)
        st = self.gstack if persistent else self.pstack
        if space == "sbuf":
            h = st.enter_context(self.nc.sbuf_tensor(nm, list(shape), dtype))
        else:
            h = st.enter_context(self.nc.psum_tensor(nm, list(shape), dtype))
        return Tile(nm, h, shape, kaxes)

    def pool(self, name, n, shape, dtype=F32, kaxes=0, space="sbuf"):
        tiles = [self.tile("%s%d" % (name, i), shape, dtype, kaxes, space) for i in range(n)]
        self.rr[id(tiles)] = 0
        return tiles

    def nxt(self, tiles):
        i = self.rr[id(tiles)]
        self.rr[id(tiles)] = (i + 1) % len(tiles)
        return tiles[i]

    def mm(self, out, lhsT, rhs, start=True, stop=True, **kw):
        nc = self.nc
        rd = lhsT.keys + rhs.keys + (() if start else out.keys)
        self.S.op("pe", lambda: nc.tensor.matmul(out.ap, lhsT=lhsT.ap, rhs=rhs.ap, start=start,
                                                 stop=stop, **kw), rd, out.keys)

    def transpose(self, out, in_, ident):
        nc = self.nc
        self.S.op("pe", lambda: nc.tensor.transpose(out=out.ap, in_=in_.ap, identity=ident.ap),
                  in_.keys + ident.keys, out.keys)

    def act(self, out, in_, func, bias=None, scale=None, accum=None):
        nc = self.nc
        kw = {}
        if bias is not None:
            kw["bias"] = _a(bias)
        if scale is not None:
            kw["scale"] = _a(scale)
        if accum is not None:
            kw["accum_out"] = accum.ap
        self.S.op("act", lambda: nc.scalar.activation(out=out.ap, in_=in_.ap, func=func, **kw),
                  in_.keys + _k(bias) + _k(scale), out.keys + _k(accum))

    def _ve(self, eng):
        return self.nc.vector if eng == "dve" else self.nc.gpsimd

    def tt(self, eng, out, in0, in1, op):
        e = self._ve(eng)
        self.S.op(eng, lambda: e.tensor_tensor(out=out.ap, in0=in0.ap, in1=in1.ap, op=op),
                  in0.keys + in1.keys, out.keys)

    def ts(self, eng, out, in0, s1, s2=None, op0=ALU.mult, op1=None):
        e = self._ve(eng)
        kw = {}
        if op1 is not None:
            kw["op1"] = op1
        self.S.op(eng, lambda: e.tensor_scalar(out=out.ap, in0=in0.ap, scalar1=_a(s1),
                                               scalar2=_a(s2), op0=op0, **kw),
                  in0.keys + _k(s1) + _k(s2), out.keys)

    def stt(self, eng, out, in0, scalar, in1, op0, op1):
        e = self._ve(eng)
        self.S.op(eng, lambda: e.scalar_tensor_tensor(out=out.ap, in0=in0.ap, scalar=_a(scalar),
                                                      in1=in1.ap, op0=op0, op1=op1),
                  in0.keys + _k(scalar) + in1.keys, out.keys)

    def copy(self, eng, out, in_):
        if eng == "act":
            nc = self.nc
            self.S.op("act", lambda: nc.scalar.copy(out=out.ap, in_=in_.ap), in_.keys, out.keys)
        else:
            e = self._ve(eng)
            self.S.op(eng, lambda: e.tensor_copy(out=out.ap, in_=in_.ap), in_.keys, out.keys)

    def memset(self, eng, out, val):
        e = self._ve(eng)
        self.S.op(eng, lambda: e.memset(out.ap, val), (), out.keys)

    def recip(self, out, in_):
        nc = self.nc
        self.S.op("dve", lambda: nc.vector.reciprocal(out=out.ap, in_=in_.ap), in_.keys, out.keys)

    def scan(self, out, d0, d1, init, op0=ALU.mult, op1=ALU.add):
        nc = self.nc
        self.S.op("dve", lambda: nc.vector.tensor_tensor_scan(out=out.ap, data0=d0.ap, data1=d1.ap,
                                                              initial=_a(init), op0=op0, op1=op1),
                  d0.keys + d1.keys + _k(init), out.keys)

    def dma(self, eng, out, in_, key, **kw):
        e = self.S.engs[eng]
        self.S.op(eng, lambda: e.dma_start(out=_a(out), in_=_a(in_), **kw), _k(in_), _k(out),
                  dma=key)


def rows_view(dram, r0, nsub):
    return dram[r0:r0 + 128 * nsub, :].rearrange("(s p) d -> p s d", p=128)


def load_w_bf16(B, name, dram, kdim, ndim, key, c0=0, eng="pool"):
    kc = kdim // 128
    t = B.tile(name, [128, kc, ndim], BF16, kaxes=0)
    src = dram[:, c0:c0 + ndim].rearrange("(k p) n -> p k n", p=128)
    half = max(1, kc // 2)
    for i in range(0, kc, half):
        B.dma(eng, V(t.t[:, i:i + half, :], t[:].keys), src[:, i:i + half, :], key)
    return t


def bcast_row(B, name, dram_row, n, key, persistent=False):
    t = B.tile(name, [128, n], F32, persistent=persistent)
    B.dma("sp", t[:], dram_row.broadcast_to([128, n]), key)
    return t


def rmsnorm_tile(B, xt_v, ss_v, rstd_v, g_t, out_v, sq_scratch):
    B.act(sq_scratch, xt_v, AF.Square, accum=ss_v)
    B.act(rstd_v, ss_v, AF.Sqrt, bias=EPS, scale=1.0 / D)
    B.recip(rstd_v, rstd_v)
    B.stt("dve", out_v, xt_v, rstd_v, g_t[:], ALU.mult, ALU.mult)


def mem_kv(B, C, io, l, kt, vv):
    outer = B.pstack
    B.pstack = contextlib.ExitStack()
    ident = C["ident"]
    g = bcast_row(B, "memg", io["mem_norm_g"], D, B.fresh())
    mt = B.tile("memx", [128, 2, D], F32, kaxes=1)
    B.dma("sp", mt[:], rows_view(io["mem"], 0, 2), "ld_x0")
    sq = B.tile("memsq", [128, D], F32)
    ss = B.tile("memss", [128, 2], F32, kaxes=1)
    rstd = B.tile("memrs", [128, 2], F32, kaxes=1)
    mn = B.tile("memn", [128, 2, D], BF16, kaxes=1)
    memT = B.tile("memT", [128, 8, 2, 128], BF16, kaxes=2)
    pbig = B.pool("mpb", 4, [128, 512], F32, space="psum")
    ptr = B.pool("mptr", 2, [128, 4, 128], BF16, space="psum")
    for s in range(2):
        rmsnorm_tile(B, mt[:, s, :], ss[:, s:s + 1], rstd[:, s:s + 1], g, mn[:, s, :], sq[:])
        for kg in range(2):
            p = B.nxt(ptr)
            for j in range(4):
                kc = kg * 4 + j
                B.transpose(V(p.t[:, j, :], p[:].keys), mn[:, s, kc * 128:(kc + 1) * 128], ident[:])
            B.copy("act" if kg == 0 else "dve", memT[:, kg * 4:(kg + 1) * 4, s, :], p[:])
    wkv = load_w_bf16(B, "wkv%d" % l, io["xa_w_kv"][l], D, 2048, B.fresh())
    for j in range(8):
        p = B.nxt(pbig)
        for kc in range(8):
            B.mm(V(p.t[:, 0:256], p[:].keys), V(wkv.t[:, kc, j * 128:(j + 1) * 128], wkv[:].keys),
                 V(memT.t[:, kc, :, :], memT[:, kc].keys), start=(kc == 0), stop=(kc == 7))
        B.copy("act" if j % 2 == 0 else "dve", kt[:, j, :], V(p.t[:, 0:256], p[:].keys))
    for m in range(2):
        for n in range(2):
            p = B.nxt(pbig)
            for kc in range(8):
                B.mm(p[:], memT[:, kc, m, :],
                     V(wkv.t[:, kc, 1024 + n * 512:1024 + (n + 1) * 512], wkv[:].keys),
                     start=(kc == 0), stop=(kc == 7))
            B.copy("act" if n == 0 else "dve", V(vv.t[:, m, n * 512:(n + 1) * 512], vv[:, m].keys),
                   p[:])
    B.S.barrier()
    B.pstack.close()
    B.pstack = outer


def phase_xa(B, C, io, l, xin, xout, ntok, final):
    T = 512
    NS = T // 128
    nt = ntok // T
    B.phase_begin()
    ident, ones = C["ident"], C["ones"]
    KTl = B.tile("KT%d" % l, [128, 8, 256], BF16, kaxes=1)
    VVl = B.tile("VV%d" % l, [128, 2, 1024], BF16, kaxes=1)
    mem_kv(B, C, io, l, KTl, VVl)
    g = bcast_row(B, "xag", io["xa_norm_g"][l:l + 1, :], D, B.fresh())
    gf = bcast_row(B, "fng", io["final_norm_g"], D, B.fresh()) if final else None
    wqg = load_w_bf16(B, "wqg", io["xa_w_qg"][l], D, 2048, B.fresh())
    wo = load_w_bf16(B, "wo", io["xa_w_o"][l], D, D, B.fresh())
    xts = B.pool("xt", 2, [128, NS, D], F32, kaxes=1)
    sq = B.tile("sq", [128, D], F32)
    ss = B.tile("ss", [128, 2 * NS], F32, kaxes=1)
    rstd = B.tile("rstd", [128, 2 * NS], F32, kaxes=1)
    hn = B.pool("hn", 2, [128, D], BF16)
    hnT = B.tile("hnT", [128, 8, NS, 128], BF16, kaxes=2)
    qT = B.tile("qT", [128, 8, T], BF16, kaxes=1)
    sgT = B.tile("sgT", [128, 8, T], BF16, kaxes=1)
    Et = B.pool("E", 4, [128, T], BF16)
    rs = B.pool("rs", 2, [128, T], F32)
    wt = B.pool("wt", 2, [128, T], F32)
    ogT = B.tile("ogT", [128, 8, T], BF16, kaxes=1)
    pbig = B.pool("pb", 6, [128, 512], F32, space="psum")
    ptr = B.pool("ptr", 2, [128, 4, 128], BF16, space="psum")

    def load(i):
        xt = xts[i % 2]
        B.dma("sp", xt[:], rows_view(xin, i * T, NS), "ld_x%d" % (i % 2))

    load(0)
    for i in range(nt):
        if i + 1 < nt:
            load(i + 1)
        xt = xts[i % 2]
        for s in range(NS):
            h = B.nxt(hn)
            rmsnorm_tile(B, xt[:, s, :], ss[:, s:s + 1], rstd[:, s:s + 1], g, h[:], sq[:])
            for kg in range(2):
                p = B.nxt(ptr)
                for j in range(4):
                    kc = kg * 4 + j
                    B.transpose(V(p.t[:, j, :], p[:].keys), V(h.t[:, kc * 128:(kc + 1) * 128], h[:].keys),
                                ident[:])
                B.copy("act" if kg == 0 else "dve", hnT[:, kg * 4:(kg + 1) * 4, s, :], p[:])
        for j in range(16):
            p = B.nxt(pbig)
            for kc in range(8):
                B.mm(p[:], V(wqg.t[:, kc, j * 128:(j + 1) * 128], wqg[:].keys),
                     V(hnT.t[:, kc, :, :], hnT[:, kc].keys), start=(kc == 0), stop=(kc == 7))
            if j < 8:
                B.ts("dve", qT[:, j, :], p[:], 1.0 / 16.0, None, ALU.mult)
            else:
                B.act(sgT[:, j - 8, :], p[:], AF.Silu)
        for hh in range(4):
            es = []
            for m in range(2):
                p = B.nxt(pbig)
                for dc in range(2):
                    B.mm(p[:], V(KTl.t[:, 2 * hh + dc, m * 128:(m + 1) * 128], KTl[:, 2 * hh + dc].keys),
                         qT[:, 2 * hh + dc, :], start=(dc == 0), stop=(dc == 1))
                e = B.nxt(Et)
                B.act(e[:], p[:], AF.Exp)
                es.append(e)
            psum = B.nxt(pbig)
            for m in range(2):
                B.mm(psum[:], ones[:], es[m][:], start=(m == 0), stop=(m == 1))
            r = B.nxt(rs)
            B.recip(r[:], psum[:])
            for dc in range(2):
                p = B.nxt(pbig)
                for m in range(2):
                    c0 = hh * 256 + dc * 128
                    B.mm(p[:], V(VVl.t[:, m, c0:c0 + 128], VVl[:, m].keys), es[m][:],
                         start=(m == 0), stop=(m == 1))
                w = B.nxt(wt)
                B.tt("pool", w[:], r[:], sgT[:, 2 * hh + dc, :], ALU.mult)
                B.tt("dve", ogT[:, 2 * hh + dc, :], p[:], w[:], ALU.mult)
        for s in range(NS):
            for n in range(2):
                p = B.nxt(pbig)
                for kc in range(8):
                    B.mm(p[:], V(ogT.t[:, kc, s * 128:(s + 1) * 128], ogT[:, kc].keys),
                         V(wo.t[:, kc, n * 512:(n + 1) * 512], wo[:].keys), start=(kc == 0), stop=(kc == 7))
                xv = V(xt.t[:, s, n * 512:(n + 1) * 512], xt[:, s].keys)
                B.tt("dve", xv, xv, p[:], ALU.add)
            if final:
                rmsnorm_tile(B, xt[:, s, :], ss[:, NS + s:NS + s + 1], rstd[:, NS + s:NS + s + 1], gf,
                             xt[:, s, :], sq[:])
        B.dma("sp", rows_view(xout, i * T, NS), xt[:], "st_o%d" % (i % 2))
    B.phase_end()


def phase_l1(B, C, io, xin, xout):
    T = 256
    NS = 2
    nt = EXT // T
    NH = HALO // T
    B.phase_begin()
    ident, ones = C["ident"], C["ones"]
    g = bcast_row(B, "l1g", io["od_norm_g"], D, B.fresh())
    win = load_w_bf16(B, "l1win", io["od_w_in"][0], D, 4096, B.fresh())
    wo = load_w_bf16(B, "l1wo", io["od_w_out"][0], D, D, B.fresh())
    bias = B.tile("l1bias", [128, 80, 128], BF16)
    bsrc = io["c_bias"].rearrange("t k q -> k t q")
    bkey = B.fresh()
    for t0 in range(0, 80, 8):
        B.dma("pool", V(bias.t[:, t0:t0 + 8, :], bias[:].keys), bsrc[:, t0:t0 + 8, :], bkey)
    hmask = B.tile("l1hm", [128, 128], BF16)
    B.dma("pool", hmask[:], io["c_hmask"][:, :], B.fresh())
    xts = B.pool("l1xt", 2, [128, NS, D], F32, kaxes=1)
    sq = B.tile("l1sq", [128, D], F32)
    ss = B.tile("l1ss", [128, NS], F32, kaxes=1)
    rstd = B.tile("l1rstd", [128, NS], F32, kaxes=1)
    hn = B.pool("l1hn", 2, [128, D], BF16)
    hnT = B.tile("l1hnT", [128, 8, NS, 128], BF16, kaxes=2)
    KTr = B.tile("l1KT", [128, 8, 8, 128], BF16, kaxes=2)
    Vr = B.tile("l1V", [128, 8, D], BF16, kaxes=1)
    qTA = B.tile("l1qA", [128, 8, T], BF16, kaxes=1)
    qTB = B.tile("l1qB", [128, 8, T], BF16, kaxes=1)
    sgT = B.tile("l1sg", [128, 8, T], BF16, kaxes=1)
    ogT = B.tile("l1og", [128, 8, T], BF16, kaxes=1)
    Ep = B.pool("l1E", 4, [128, 640], BF16)
    rp = B.pool("l1r", 2, [128, 256], F32)
    wp = B.pool("l1w", 2, [128, 128], F32)
    Sall = B.tile("l1S", [128, 2, 1024], F32, kaxes=1, space="psum")
    scnt = [0]
    pb = B.pool("l1pb", 2, [128, 512], F32, space="psum")
    OSp = B.pool("l1OS", 1, [128, 512], F32, space="psum")
    ptr2 = B.tile("l1ptr", [128, 1, 4, 128], BF16, kaxes=1, space="psum")
    pcnt = [0]
    B.memset("pool", qTA[:], 0.0)
    B.memset("pool", qTB[:], 0.0)

    def load(i):
        B.dma("sp", xts[i % 2][:], rows_view(xin, i * T, NS), "ld_x%d" % (i % 2))

    load(0)
    cnt = 0
    for i in range(nt):
        if i + 1 < nt:
            load(i + 1)
        xt = xts[i % 2]
        own = i >= NH
        for s in range(NS):
            h = B.nxt(hn)
            rmsnorm_tile(B, xt[:, s, :], ss[:, s:s + 1], rstd[:, s:s + 1], g, h[:], sq[:])
            for kg in range(2):
                pi = 0
                pcnt[0] += 1
                for j in range(4):
                    kc = kg * 4 + j
                    B.transpose(V(ptr2.t[:, pi, j, :], ptr2[:, pi].keys),
                                V(h.t[:, kc * 128:(kc + 1) * 128], h[:].keys), ident[:])
                B.copy("act" if kg == 0 else "dve", hnT[:, kg * 4:(kg + 1) * 4, s, :], ptr2[:, pi])
        sl0 = (2 * i) % 8
        for j in range(8):
            p = B.nxt(pb)
            pv = V(p.t[:, 0:T], p[:].keys)
            for kc in range(8):
                B.mm(pv, V(win.t[:, kc, 1024 + j * 128:1024 + (j + 1) * 128], win[:].keys),
                     V(hnT.t[:, kc, :, :], hnT[:, kc].keys), start=(kc == 0), stop=(kc == 7))
            cnt += 1
            B.copy("act" if cnt % 2 else "dve",
                   V(KTr.t[:, j, sl0:sl0 + 2, :], KTr[:, j, sl0:sl0 + 2].keys), pv)
        for s in range(NS):
            for n in range(2):
                p = B.nxt(pb)
                for kc in range(8):
                    B.mm(p[:], hnT[:, kc, s, :],
                         V(win.t[:, kc, 2048 + n * 512:2048 + (n + 1) * 512], win[:].keys),
                         start=(kc == 0), stop=(kc == 7))
                cnt += 1
                B.copy("act" if cnt % 2 else "dve",
                       V(Vr.t[:, sl0 + s, n * 512:(n + 1) * 512], Vr[:, sl0 + s].keys), p[:])
        if not own:
            continue
        for j in range(8):
            p = B.nxt(pb)
            pv = V(p.t[:, 0:T], p[:].keys)
            for kc in range(8):
                B.mm(pv, V(win.t[:, kc, j * 128:(j + 1) * 128], win[:].keys),
                     V(hnT.t[:, kc, :, :], hnT[:, kc].keys), start=(kc == 0), stop=(kc == 7))
            B.ts("dve", V(qTA.t[0:64, j, :], qTA[:, j].keys), V(p.t[0:64, 0:T], p[:].keys), 0.125, None,
                 ALU.mult)
            B.ts("dve", V(qTB.t[64:128, j, :], qTB[:, j].keys), V(p.t[64:128, 0:T], p[:].keys), 0.125, None,
                 ALU.mult)
        for j in range(8):
            p = B.nxt(pb)
            pv = V(p.t[:, 0:T], p[:].keys)
            for kc in range(8):
                B.mm(pv, V(win.t[:, kc, 3072 + j * 128:3072 + (j + 1) * 128], win[:].keys),
                     V(hnT.t[:, kc, :, :], hnT[:, kc].keys), start=(kc == 0), stop=(kc == 7))
            B.act(sgT[:, j, :], pv, AF.Silu)
        for s in range(NS):
            u = 2 * i + s
            qc = slice(s * 128, (s + 1) * 128)
            for j in range(8):
                Es = []
                for hb in range(2):
                    hd = 2 * j + hb
                    qq = qTA if hb == 0 else qTB
                    sb_ = scnt[0] % 2
                    scnt[0] += 1
                    Skeys = Sall[:, sb_].keys
                    for dl in range(5):
                        uk = u - dl
                        slot = uk % 8
                        blk = V(Sall.t[:, sb_, dl * 128:(dl + 1) * 128], Skeys)
                        halo = uk < HALO // 128
                        B.mm(blk, KTr[:, j, slot, :], V(qq.t[:, j, qc], qq[:, j].keys), start=True, stop=False)
                        B.mm(blk, ident[:], V(bias.t[:, hd * 5 + dl, :], bias[:].keys), start=False,
                             stop=not halo)
                        if halo:
                            B.mm(blk, ident[:], hmask[:], start=False, stop=True)
                    E = B.nxt(Ep)
                    B.act(E[:], V(Sall.t[:, sb_, 0:640], Skeys), AF.Exp)
                    Es.append(E)
                OS = B.nxt(OSp)
                for hb in range(2):
                    for dl in range(5):
                        slot = (u - dl) % 8
                        B.mm(V(OS.t[:, hb * 128:(hb + 1) * 128], OS[:].keys),
                             V(Vr.t[:, slot, j * 128:(j + 1) * 128], Vr[:, slot].keys),
                             V(Es[hb].t[:, dl * 128:(dl + 1) * 128], Es[hb][:].keys),
                             start=(dl == 0), stop=(dl == 4))
                for hb in range(2):
                    for dl in range(5):
                        B.mm(V(OS.t[:, 256 + hb * 128:256 + (hb + 1) * 128], OS[:].keys), ones[:],
                             V(Es[hb].t[:, dl * 128:(dl + 1) * 128], Es[hb][:].keys),
                             start=(dl == 0), stop=(dl == 4))
                r = B.nxt(rp)
                B.recip(r[:], V(OS.t[:, 256:512], OS[:].keys))
                w = B.nxt(wp)
                for hb in range(2):
                    pr = slice(64 * hb, 64 * hb + 64)
                    B.tt("pool", V(w.t[pr, :], w[:].keys), V(r.t[pr, hb * 128:(hb + 1) * 128], r[:].keys),
                         V(sgT.t[pr, j, qc], sgT[:, j].keys), ALU.mult)
                for hb in range(2):
                    pr = slice(64 * hb, 64 * hb + 64)
                    B.tt("dve", V(ogT.t[pr, j, qc], ogT[:, j].keys),
                         V(OS.t[pr, hb * 128:(hb + 1) * 128], OS[:].keys), V(w.t[pr, :], w[:].keys), ALU.mult)
        for s in range(NS):
            for n in range(2):
                p = B.nxt(pb)
                for kc in range(8):
                    B.mm(p[:], V(ogT.t[:, kc, s * 128:(s + 1) * 128], ogT[:, kc].keys),
                         V(wo.t[:, kc, n * 512:(n + 1) * 512], wo[:].keys), start=(kc == 0), stop=(kc == 7))
                xv = V(xt.t[:, s, n * 512:(n + 1) * 512], xt[:, s].keys)
                B.tt("dve", xv, xv, p[:], ALU.add)
        B.dma("sp", rows_view(xout, (i - NH) * T, NS), xt[:], "st_o%d" % (i % 2))
    B.phase_end()


TWO_PI = 6.283185307179586
DEBUG_STOP = None
DEBUG_NT = None


def load_cols(B, name, dram_row, key):
    t = B.tile(name, [128, 4], F32)
    B.dma("sp", t[:], dram_row.rearrange("o (c p) -> p (o c)", p=128), key,
          allow_slow_non_contiguous=True)
    return t


def phase_l0(B, C, io, xin, xout):
    T = 256
    NS = 2
    T5 = 128
    npre = PRE // T
    nt = (PRE + EXT) // T
    B.phase_begin()
    ident, ones = C["ident"], C["ones"]
    g = bcast_row(B, "l0g", io["ev_norm_g"], D, B.fresh())
    win = load_w_bf16(B, "l0win", io["ev_w_in"][0], D, 2560, B.fresh())
    wo = load_w_bf16(B, "l0wo", io["ev_w_out"][0], D, D, B.fresh())
    gluw = load_w_bf16(B, "l0gluw", io["ev_s5_glu_w"][0], 512, 512, B.fresh())
    vecs = B.tile("l0vecs", [128, 20], F32)
    lrli = B.tile("l0lrli", [128, 32], F32)

    class _Sub:
        def __init__(self, t, c0, n):
            self.t = _SubT(t, c0)
            self._t = t

        def __getitem__(self, idx):
            return self._t[:]

    class _SubT:
        def __init__(self, t, c0):
            self.tt_, self.c0 = t, c0

        def __getitem__(self, idx):
            p, c = idx
            return self.tt_.t[p, self.c0 + c.start:self.c0 + c.stop]

    Dt, glub, convb, lng, lnb = [_Sub(vecs, 4 * i, 4) for i in range(5)]
    onesM = B.tile("l0onesM", [128, 128], BF16)
    B.memset("pool", onesM[:], 1.0 / 512.0)
    CS = B.tile("l0CS", [128, 16, 2, T5], F32)
    BT = [B.tile("l0BT%d" % ri, [128, 16, 128], BF16) for ri in range(2)]
    CT = [B.tile("l0CT%d" % ri, [128, 16, 128], BF16) for ri in range(3)]
    DG = B.tile("l0DG", [128, 4, 31, 128], BF16)
    mag = B.tile("l0mag", [128, 16], F32)
    cT = B.tile("l0cT", [128, 16], F32)
    sT = B.tile("l0sT", [128, 16], F32)
    init = [B.tile("l0init%d" % ri, [128, 16], F32, kaxes=1) for ri in range(2)]
    B.memset("pool", init[0][:], 0.0)
    B.memset("pool", init[1][:], 0.0)

    outer = B.pstack
    B.pstack = contextlib.ExitStack()
    pps = B.tile("l0pps", [128, 4, 128], BF16, space="psum")
    cnt = [0]

    def small(name):
        return B.tile(name, [128, 16], F32)

    ident32 = B.tile("l0id32", [128, 128], F32)
    B.dma("sp", ident32[:], io["c_ident"][:, :], B.fresh())
    Vst = B.tile("l0Vst", [32, 128], F32)
    vkey = B.fresh()
    for vi, nm in enumerate(("ev_s5_d", "ev_s5_glu_b", "ev_conv_b", "ev_conv_ln_g", "ev_conv_ln_b")):
        B.dma("sp", V(Vst.t[4 * vi:4 * vi + 4, :], Vst[:].keys),
              io[nm].rearrange("o (c p) -> (o c) p", p=128), vkey)
    LL = B.tile("l0LL", [32, 128], F32)
    lkey2 = B.fresh()
    B.dma("sp", V(LL.t[0:16, :], LL[:].keys),
          io["ev_s5_lambda_re"][0].rearrange("(k g) p -> k (g p)", g=2), lkey2)
    B.dma("sp", V(LL.t[16:32, :], LL[:].keys),
          io["ev_s5_lambda_im"][0].rearrange("(k g) p -> k (g p)", g=2), lkey2)
    pv32 = B.tile("l0pv32", [128, 64], F32, space="psum")
    B.S.op("pe", lambda: B.nc.tensor.transpose(out=pv32.t[:, 0:32], in_=LL.t[0:32, :],
                                               identity=ident32.t[0:32, 0:32]),
           LL[:].keys + ident32[:].keys, pv32[:].keys)
    B.S.op("pe", lambda: B.nc.tensor.transpose(out=pv32.t[:, 32:52], in_=Vst.t[0:20, :],
                                               identity=ident32.t[0:20, 0:20]),
           Vst[:].keys + ident32[:].keys, pv32[:].keys)
    B.copy("dve", lrli[:], V(pv32.t[:, 0:32], pv32[:].keys))
    B.copy("dve", vecs[:], V(pv32.t[:, 32:52], pv32[:].keys))
    lr = _Sub(lrli, 0, 16)
    li = _Sub(lrli, 16, 16)
    lr = V(lrli.t[:, 0:16], lrli[:].keys)
    li = V(lrli.t[:, 16:32], lrli[:].keys)
    LB = bcast_row(B, "l0LB", io["ev_s5_log_dt"], 32, B.fresh())
    ldt = small("ldt")
    for gg in range(2):
        pr = slice(64 * gg, 64 * gg + 64)
        B.copy("dve", V(ldt.t[pr, :], ldt[:].keys),
               V(LB.t[pr, :].rearrange("p (k g) -> p k g", g=2)[:, :, gg], LB[:].keys))
    if DEBUG_STOP == "loads":
        B.S.mute = True
    dt = small("dt")
    B.act(dt[:], ldt[:], AF.Exp)
    lrdt = small("lrdt")
    B.tt("dve", lrdt[:], lr, dt[:], ALU.mult)
    B.act(mag[:], lrdt[:], AF.Exp)
    th = small("th")
    B.tt("dve", th[:], li, dt[:], ALU.mult)
    tq = small("tq")
    B.ts("dve", tq[:], th[:], 1.0 / TWO_PI, None, ALU.mult)
    tqi = B.tile("tqi", [128, 16], I32)
    B.copy("dve", tqi[:], tq[:])
    tqf = small("tqf")
    B.copy("dve", tqf[:], tqi[:])
    thr = small("thr")
    B.stt("dve", thr[:], tqf[:], -TWO_PI, th[:], ALU.mult, ALU.add)
    ath = small("ath")
    B.ts("dve", ath[:], thr[:], -1.0, None, ALU.mult)
    B.tt("dve", ath[:], ath[:], thr[:], ALU.max)
    cm, sm = small("c1"), small("s1")
    B.act(cm[:], ath[:], AF.Sin, bias=C["halfpi"][:, 0:1], scale=-1.0)
    B.act(sm[:], thr[:], AF.Sin)
    if DEBUG_STOP == "trig":
        B.S.mute = True
    abre, abim = small("abre"), small("abim")
    B.tt("dve", abre[:], mag[:], cm[:], ALU.mult)
    B.tt("dve", abim[:], mag[:], sm[:], ALU.mult)
    den, t0, t1 = small("den"), small("t0"), small("t1")
    B.tt("dve", den[:], lr, lr, ALU.mult)
    B.tt("dve", t0[:], li, li, ALU.mult)
    B.tt("dve", den[:], den[:], t0[:], ALU.add)
    B.recip(den[:], den[:])
    nr = small("nr")
    B.ts("dve", nr[:], abre[:], -1.0, None, ALU.add)
    cre, cim = small("cre"), small("cim")
    B.tt("dve", t0[:], nr[:], lr, ALU.mult)
    B.tt("dve", t1[:], abim[:], li, ALU.mult)
    B.tt("dve", t0[:], t0[:], t1[:], ALU.add)
    B.tt("dve", cre[:], t0[:], den[:], ALU.mult)
    B.tt("dve", t0[:], abim[:], lr, ALU.mult)
    B.tt("dve", t1[:], nr[:], li, ALU.mult)
    B.tt("dve", t0[:], t0[:], t1[:], ALU.subtract)
    B.tt("dve", cim[:], t0[:], den[:], ALU.mult)
    if DEBUG_STOP == "abar":
        B.S.mute = True
    B.memset("pool", V(CS.t[:, :, 0, 0:1], CS[:].keys), 1.0)
    B.memset("pool", V(CS.t[:, :, 1, 0:1], CS[:].keys), 0.0)
    tb = [B.tile("l0tb%d" % i, [128, 16, 64], F32) for i in range(4)]

    def bc(t, n):
        return V(t.t[:, :].rearrange("p (k o) -> p k o", o=1).broadcast_to([128, 16, n]), t[:].keys)

    for m in range(7):
        n = 1 << m
        cosn = V(CS.t[:, :, 0, 0:n], CS[:].keys)
        sinn = V(CS.t[:, :, 1, 0:n], CS[:].keys)
        tv = [V(t.t[:, :, 0:n], t[:].keys) for t in tb]
        B.tt("dve", tv[0], cosn, bc(cm, n), ALU.mult)
        B.tt("dve", tv[1], sinn, bc(sm, n), ALU.mult)
        B.tt("dve", tv[2], cosn, bc(sm, n), ALU.mult)
        B.tt("dve", tv[3], sinn, bc(cm, n), ALU.mult)
        B.tt("dve", V(CS.t[:, :, 0, n:2 * n], CS[:].keys), tv[0], tv[1], ALU.subtract)
        B.tt("dve", V(CS.t[:, :, 1, n:2 * n], CS[:].keys), tv[2], tv[3], ALU.add)
        c2, s2, cs2 = small("c2_%d" % m), small("s2_%d" % m), small("cs_%d" % m)
        B.tt("dve", c2[:], cm[:], cm[:], ALU.mult)
        B.tt("dve", s2[:], sm[:], sm[:], ALU.mult)
        B.tt("dve", cs2[:], cm[:], sm[:], ALU.mult)
        if m < 6:
            cmn, smn = small("cm_%d" % m), small("sm_%d" % m)
        else:
            cmn, smn = cT, sT
        B.tt("dve", cmn[:], c2[:], s2[:], ALU.subtract)
        B.ts("dve", smn[:], cs2[:], 2.0, None, ALU.mult)
        cm, sm = cmn, smn
    if DEBUG_STOP == "tables":
        B.S.mute = True
    bre = B.tile("l0bre", [128, 16, 16], F32)
    bim = B.tile("l0bim", [128, 16, 16], F32)
    for bt_, nm in ((bre, "ev_s5_b_re"), (bim, "ev_s5_b_im")):
        bsrc = io[nm][0].rearrange("(k g) p c -> (g p) k c", g=2)
        bk = B.fresh()
        for k0 in range(0, 16, 4):
            B.dma("sp", V(bt_.t[:, k0:k0 + 4, :], bt_[:].keys), bsrc[:, k0:k0 + 4, :], bk)
    bbre = B.tile("l0bbre", [128, 16, 16], F32)
    bbim = B.tile("l0bbim", [128, 16, 16], F32)
    tb16 = [V(t.t[:, :, 0:16], t[:].keys) for t in tb]
    B.tt("dve", tb16[0], bre[:], bc(cre, 16), ALU.mult)
    B.tt("dve", tb16[1], bim[:], bc(cim, 16), ALU.mult)
    B.tt("dve", bbre[:], tb16[0], tb16[1], ALU.subtract)
    B.tt("dve", tb16[2], bim[:], bc(cre, 16), ALU.mult)
    B.tt("dve", tb16[3], bre[:], bc(cim, 16), ALU.mult)
    B.tt("dve", bbim[:], tb16[2], tb16[3], ALU.add)
    BD = [B.tile("l0BD%d" % ri, [128, 16, 128], BF16) for ri in range(2)]
    for ri, (bd, bb) in enumerate(zip(BD, (bbre, bbim))):
        B.memset("pool", bd[:], 0.0)
        for k in range(16):
            r = k % 4
            for gg in range(2):
                pr = slice(64 * gg, 64 * gg + 64)
                c0 = 32 * r + 16 * gg
                B.copy("pool" if gg else "dve", V(bd.t[pr, k, c0:c0 + 16], bd[:].keys),
                       V(bb.t[pr, k, :], bb[:].keys))
        for kg in range(4):
            for j in range(4):
                B.transpose(V(pps.t[:, j, :], pps[:].keys), V(bd.t[:, kg * 4 + j, :], bd[:].keys), ident[:])
            B.copy("act", V(BT[ri].t[:, kg * 4:kg * 4 + 4, :], BT[ri][:].keys), pps[:])
    if DEBUG_STOP == "bbar":
        B.S.mute = True
    for ct in CT:
        B.memset("pool", ct[:], 0.0)
    Zs = [B.tile("l0Z%d" % ri, [64, 16, 128], F32) for ri in range(2)]
    for ri, nm in enumerate(("ev_s5_c_re", "ev_s5_c_im")):
        B.memset("pool", Zs[ri][:], 0.0)
        src = io[nm][0].rearrange("(k g) co p -> g co k p", g=2)
        ckey = B.fresh()
        B.dma("sp", V(Zs[ri].t[0:16, :, 0:64], Zs[ri][:].keys), src[0], ckey)
        B.dma("sp", V(Zs[ri].t[32:48, :, 64:128], Zs[ri][:].keys), src[1], ckey)
        for k in range(16):
            r = k % 4
            Zk = Zs[ri]
            B.S.op("pe", (lambda Zk=Zk, k=k: B.nc.tensor.transpose(out=pv32.t[:, 0:64], in_=Zk.t[0:64, k, :],
                                                                  identity=ident32.t[0:64, 0:64])),
                   Zk[:].keys + ident32[:].keys, pv32[:].keys)
            for gg in range(2):
                srcv = V(pv32.t[:, 32 * gg:32 * gg + 16], pv32[:].keys)
                c0 = 32 * r + 16 * gg
                if ri == 0:
                    B.copy("act", V(CT[0].t[:, k, c0:c0 + 16], CT[0][:].keys), srcv)
                    B.ts("dve", V(CT[1].t[:, k, c0:c0 + 16], CT[1][:].keys), srcv, -1.0, None, ALU.mult)
                else:
                    B.ts("dve", V(CT[2].t[:, k, c0:c0 + 16], CT[2][:].keys), srcv, -1.0, None, ALU.mult)
    if DEBUG_STOP == "cmat":
        B.S.mute = True
    cw = B.tile("l0cw", [31, 512], F32)
    B.dma("sp", cw[:], io["ev_conv_w"][0], B.fresh())
    wk = B.tile("l0wk", [128, 4, 31], F32)
    for cc in range(4):
        B.S.op("pe", (lambda cc=cc: B.nc.tensor.transpose(out=pv32.t[:, 0:31], in_=cw.t[0:31, cc * 128:(cc + 1) * 128],
                                                          identity=ident32.t[0:31, 0:31])),
               cw[:].keys + ident32[:].keys, pv32[:].keys)
        B.copy("act", V(wk.t[:, cc, :], wk[:].keys), V(pv32.t[:, 0:31], pv32[:].keys))
    for cc in range(4):
        for k in range(31):
            B.ts("dve" if (k % 2) else "pool", V(DG.t[:, cc, k, :], DG[:].keys), ident[:],
                 V(wk.t[:, cc, k:k + 1], wk[:].keys), None, ALU.mult)
    if DEBUG_STOP in ("loads", "trig", "abar", "tables", "bbar", "cmat"):
        B.S.mute = False
        B.phase_end()
        B.pstack = outer
        B.phase_end()
        return
    B.S.barrier()
    B.pstack.close()
    B.pstack = outer

    if DEBUG_STOP == "prep":
        B.phase_end()
        return
    xts = B.pool("l0xt", 2, [128, NS, D], F32, kaxes=1)
    ss = B.tile("l0ss", [128, NS], F32, kaxes=1)
    rstd = B.tile("l0rstd", [128, NS], F32, kaxes=1)
    hn = B.pool("l0hn", 2, [128, D], BF16)
    hnT = B.tile("l0hnT", [128, 8, NS, 128], BF16, kaxes=2)
    u32 = B.tile("l0u32", [128, 4, T], F32, kaxes=1)
    ubf = B.tile("l0ubf", [128, 4, T], BF16, kaxes=1)
    sga = B.tile("l0sga", [128, 4, T], BF16, kaxes=1)
    sgb = B.tile("l0sgb", [128, 4, T], BF16, kaxes=1)
    vbf = B.tile("l0vbf", [128, 4, 30 + T], BF16, kaxes=1)
    B.memset("pool", vbf[:], 0.0)
    sgl = B.pool("l0sgl", 2, [128, T], F32)
    T12 = B.pool("l0T12", 2, [128, 2, T5], F32)
    T34 = B.pool("l0T34", 2, [128, 2, T5], F32)
    G = B.pool("l0G", 2, [128, 2, T5], F32)
    R = B.pool("l0R", 2, [128, 2, T5], F32)
    PA = B.pool("l0PA", 2, [128, 2, T5], BF16)
    PB = B.pool("l0PB", 2, [128, 2, T5], BF16)
    cry = B.pool("l0cry", 2, [128, 4], F32)
    y32 = B.pool("l0y32", 1, [128, T], F32)
    z32 = B.tile("l0z32", [128, 4, T], F32, kaxes=1)
    zbf = B.tile("l0zbf", [128, 4, T], BF16, kaxes=1)
    sig = sgl
    c32 = B.tile("l0c32", [128, 4, T], F32, kaxes=1)
    cbf = B.tile("l0cbf", [128, 4, T], BF16, kaxes=1)
    csq = B.tile("l0csq", [128, 4, T], BF16, kaxes=1)
    mus = B.tile("l0mus", [128, T], F32)
    var = B.tile("l0var", [128, T], F32)
    rsd = B.tile("l0rsd", [128, T], F32)
    cn = B.pool("l0cn", 2, [128, T], F32)
    yab = B.tile("l0yab", [128, 8, T], BF16, kaxes=1)
    pb = B.pool("l0pb", 2, [128, 512], F32, space="psum")
    pS = B.pool("l0pS", 2, [128, 4, T5], F32, space="psum")
    yps = B.pool("l0yps", 1, [128, T], F32, space="psum")
    pst = B.pool("l0pst", 2, [128, T], F32, space="psum")
    ptr = B.tile("l0ptr", [128, 4, 128], BF16, space="psum")

    def load(i):
        B.dma("sp", xts[i % 2][:], rows_view(xin, i * T, NS), "ld_x%d" % (i % 2))

    def proj(c):
        p = B.nxt(pb)
        pv = V(p.t[:, 0:T], p[:].keys)
        for kc in range(8):
            B.mm(pv, V(win.t[:, kc, c * 128:(c + 1) * 128], win[:].keys),
                 V(hnT.t[:, kc, :, :], hnT[:, kc].keys), start=(kc == 0), stop=(kc == 7))
        return pv

    load(0)
    for i in range(nt):
        if DEBUG_NT is not None and i >= DEBUG_NT:
            break
        if i + 1 < nt:
            load(i + 1)
        xt = xts[i % 2]
        pre = i < npre
        lastpre = i == npre - 1
        for s in range(NS):
            h = B.nxt(hn)
            rmsnorm_tile(B, xt[:, s, :], ss[:, s:s + 1], rstd[:, s:s + 1], g, h[:], h[:])
            for kg in range(2):
                for j in range(4):
                    kc = kg * 4 + j
                    B.transpose(V(ptr.t[:, j, :], ptr[:].keys), V(h.t[:, kc * 128:(kc + 1) * 128], h[:].keys),
                                ident[:])
                B.copy("act" if kg == 0 else "dve", hnT[:, kg * 4:(kg + 1) * 4, s, :], ptr[:])
        for cc in range(4):
            pv = proj(cc)
            if pre:
                B.copy("act", ubf[:, cc, :], pv)
            else:
                B.copy("act", u32[:, cc, :], pv)
                B.copy("pool", ubf[:, cc, :], u32[:, cc, :])
        if not pre:
            for cc in range(4):
                B.act(sga[:, cc, :], proj(4 + cc), AF.Silu)
        if (not pre) or lastpre:
            for cc in range(4):
                sg = B.nxt(sgl)
                B.act(sg[:], proj(12 + cc), AF.Sigmoid)
                B.tt("dve", V(vbf.t[:, cc, 30:30 + T], vbf[:, cc].keys), proj(8 + cc), sg[:], ALU.mult)
        if not pre:
            for cc in range(4):
                B.act(sgb[:, cc, :], proj(16 + cc), AF.Silu)
        if DEBUG_STOP == "e_proj" and not pre:
            B.S.mute = True
        for cc in range(4):
            yp = B.nxt(yps) if not pre else None
            for h5 in range(2):
                c5 = slice(h5 * T5, (h5 + 1) * T5)
                for r in range(4):
                    k = cc * 4 + r
                    P = B.nxt(pS)
                    uv = V(ubf.t[:, cc, c5], ubf[:, cc].keys)
                    for q_, ri in enumerate((0, 1, 1, 0)):
                        B.mm(V(P.t[:, q_, :], P[:].keys), V(BT[ri].t[:, k, :], BT[ri][:].keys), uv)
                    csk = V(CS.t[:, k, :, :], CS[:].keys)
                    a12, a34 = B.nxt(T12), B.nxt(T34)
                    B.tt("dve", a12[:], V(P.t[:, 0:2, :], P[:].keys), csk, ALU.mult)
                    B.tt("dve", a34[:], V(P.t[:, 2:4, :], P[:].keys), csk, ALU.mult)
                    gt = B.nxt(G)
                    B.tt("pool", V(gt.t[:, 0, :], gt[:].keys), V(a12.t[:, 0, :], a12[:].keys),
                         V(a12.t[:, 1, :], a12[:].keys), ALU.add)
                    B.tt("pool", V(gt.t[:, 1, :], gt[:].keys), V(a34.t[:, 0, :], a34[:].keys),
                         V(a34.t[:, 1, :], a34[:].keys), ALU.subtract)
                    rt = B.nxt(R)
                    magk = V(mag.t[:, k:k + 1].broadcast_to([128, T5]), mag[:].keys)
                    for ri in range(2):
                        B.scan(V(rt.t[:, ri, :], rt[:].keys), magk, V(gt.t[:, ri, :], gt[:].keys),
                               V(init[ri].t[:, k:k + 1], init[ri][:, k].keys))
                    cr = B.nxt(cry)
                    rl_re = V(rt.t[:, 0, T5 - 1:T5], rt[:].keys)
                    rl_im = V(rt.t[:, 1, T5 - 1:T5], rt[:].keys)
                    cTk = V(cT.t[:, k:k + 1], cT[:].keys)
                    sTk = V(sT.t[:, k:k + 1], sT[:].keys)
                    B.ts("pool", V(cr.t[:, 0:1], cr[:].keys), rl_im, sTk, None, ALU.mult)
                    B.ts("pool", V(cr.t[:, 1:2], cr[:].keys), rl_im, cTk, None, ALU.mult)
                    B.ts("pool", V(cr.t[:, 2:3], cr[:].keys), rl_re, cTk, None, ALU.mult)
                    B.ts("pool", V(cr.t[:, 3:4], cr[:].keys), rl_re, sTk, None, ALU.mult)
                    B.tt("pool", V(init[0].t[:, k:k + 1], init[0][:, k].keys), V(cr.t[:, 2:3], cr[:].keys),
                         V(cr.t[:, 0:1], cr[:].keys), ALU.subtract)
                    B.tt("pool", V(init[1].t[:, k:k + 1], init[1][:, k].keys), V(cr.t[:, 3:4], cr[:].keys),
                         V(cr.t[:, 1:2], cr[:].keys), ALU.add)
                    if pre:
                        continue
                    pa, pb_ = B.nxt(PA), B.nxt(PB)
                    B.tt("dve", pa[:], rt[:], csk, ALU.mult)
                    B.tt("pool", V(pb_.t[:, 0, :], pb_[:].keys), V(rt.t[:, 0, :], rt[:].keys),
                         V(CS.t[:, k, 1, :], CS[:].keys), ALU.mult)
                    B.tt("pool", V(pb_.t[:, 1, :], pb_[:].keys), V(rt.t[:, 1, :], rt[:].keys),
                         V(CS.t[:, k, 0, :], CS[:].keys), ALU.mult)
                    yv = V(yp.t[:, c5], yp[:].keys)
                    B.mm(yv, V(CT[0].t[:, k, :], CT[0][:].keys), V(pa.t[:, 0, :], pa[:].keys),
                         start=(r == 0), stop=False)
                    B.mm(yv, V(CT[1].t[:, k, :], CT[1][:].keys), V(pa.t[:, 1, :], pa[:].keys),
                         start=False, stop=False)
                    B.mm(yv, V(CT[2].t[:, k, :], CT[2][:].keys), V(pb_.t[:, 0, :], pb_[:].keys),
                         start=False, stop=False)
                    B.mm(yv, V(CT[2].t[:, k, :], CT[2][:].keys), V(pb_.t[:, 1, :], pb_[:].keys),
                         start=False, stop=(r == 3))
            if pre:
                continue
            y = B.nxt(y32)
            B.stt("dve", y[:], u32[:, cc, :], V(Dt.t[:, cc:cc + 1], Dt[:].keys), yp[:], ALU.mult, ALU.add)
            B.act(z32[:, cc, :], y[:], AF.Gelu_apprx_tanh)
            B.copy("pool", zbf[:, cc, :], z32[:, cc, :])
        if pre:
            if lastpre:
                B.copy("pool", V(vbf.t[:, :, 0:30], vbf[:].keys), V(vbf.t[:, :, T:T + 30], vbf[:].keys))
            continue
        if DEBUG_STOP == "e_s5" and not pre:
            B.S.mute = True
        for co in range(4):
            p = B.nxt(pb)
            pv = V(p.t[:, 0:T], p[:].keys)
            for ci in range(4):
                B.mm(pv, V(gluw.t[:, ci, co * 128:(co + 1) * 128], gluw[:].keys), zbf[:, ci, :],
                     start=(ci == 0), stop=(ci == 3))
            sg = B.nxt(sig)
            B.act(sg[:], pv, AF.Sigmoid, bias=V(glub.t[:, co:co + 1], glub[:].keys))
            B.tt("pool", sg[:], sg[:], sga[:, co, :], ALU.mult)
            B.tt("dve", yab[:, co, :], z32[:, co, :], sg[:], ALU.mult)
        if DEBUG_STOP == "e_glu" and not pre:
            B.S.mute = True
        for cc in range(4):
            p = B.nxt(pb)
            pv = V(p.t[:, 0:T], p[:].keys)
            for k in range(31):
                B.mm(pv, V(DG.t[:, cc, k, :], DG[:].keys), V(vbf.t[:, cc, k:k + T], vbf[:, cc].keys),
                     start=(k == 0), stop=(k == 30))
            bcc = V(convb.t[:, cc:cc + 1], convb[:].keys)
            B.act(c32[:, cc, :], pv, AF.Identity, bias=bcc)
            B.act(cbf[:, cc, :], pv, AF.Identity, bias=bcc)
            B.act(csq[:, cc, :], pv, AF.Square, bias=bcc)
        B.copy("pool", V(vbf.t[:, :, 0:30], vbf[:].keys), V(vbf.t[:, :, T:T + 30], vbf[:].keys))
        if DEBUG_STOP == "e_conv" and not pre:
            B.S.mute = True
        pmu, pm2 = B.nxt(pst), B.nxt(pst)
        for cc in range(4):
            B.mm(pmu[:], onesM[:], cbf[:, cc, :], start=(cc == 0), stop=(cc == 3))
        for cc in range(4):
            B.mm(pm2[:], onesM[:], csq[:, cc, :], start=(cc == 0), stop=(cc == 3))
        B.copy("act", mus[:], pmu[:])
        B.tt("dve", var[:], mus[:], mus[:], ALU.mult)
        B.tt("dve", var[:], pm2[:], var[:], ALU.subtract)
        B.act(rsd[:], var[:], AF.Sqrt, bias=EPS)
        B.recip(rsd[:], rsd[:])
        for cc in range(4):
            c_ = B.nxt(cn)
            B.tt("dve", c_[:], c32[:, cc, :], mus[:], ALU.subtract)
            B.tt("pool", c_[:], c_[:], rsd[:], ALU.mult)
            B.act(c_[:], c_[:], AF.Silu, bias=V(lnb.t[:, cc:cc + 1], lnb[:].keys),
                  scale=V(lng.t[:, cc:cc + 1], lng[:].keys))
            B.tt("dve", yab[:, 4 + cc, :], c_[:], sgb[:, cc, :], ALU.mult)
        if DEBUG_STOP == "e_ln" and not pre:
            B.S.mute = True
        for s in range(NS):
            for n in range(2):
                p = B.nxt(pb)
                for kc in range(8):
                    B.mm(p[:], V(yab.t[:, kc, s * 128:(s + 1) * 128], yab[:, kc].keys),
                         V(wo.t[:, kc, n * 512:(n + 1) * 512], wo[:].keys), start=(kc == 0), stop=(kc == 7))
                xv = V(xt.t[:, s, n * 512:(n + 1) * 512], xt[:, s].keys)
                B.tt("dve", xv, xv, p[:], ALU.add)
        if DEBUG_STOP != "nostore":
            B.dma("pool" if DEBUG_STOP == "poolstore" else "sp", rows_view(xout, (i - npre) * T, NS), xt[:],
                  "st_o%d" % (i % 2))
    B.S.mute = False
    B.phase_end()


W_NAMES = ["mem_norm_g", "ev_norm_g", "ev_w_in", "ev_s5_lambda_re", "ev_s5_lambda_im", "ev_s5_log_dt",
           "ev_s5_b_re", "ev_s5_b_im", "ev_s5_c_re", "ev_s5_c_im", "ev_s5_d", "ev_s5_glu_w",
           "ev_s5_glu_b", "ev_conv_w", "ev_conv_b", "ev_conv_ln_g", "ev_conv_ln_b", "ev_w_out",
           "od_norm_g", "od_w_in", "od_rel_bias", "od_w_out", "xa_norm_g", "xa_w_qg", "xa_w_kv",
           "xa_w_o", "final_norm_g"]


def setup_consts(B, io):
    C = {}
    ident = B.tile("ident", [128, 128], BF16, persistent=True)
    B.dma("pool", ident[:], io["c_ident"][:, :], "ld_ident")
    ones = B.tile("ones", [128, 128], BF16, persistent=True)
    B.memset("pool", ones[:], 1.0)
    C["ident"] = ident
    C["ones"] = ones
    halfpi = B.tile("halfpi", [128, 1], F32, persistent=True)
    B.memset("pool", halfpi[:], 1.5707963267948966)
    C["halfpi"] = halfpi
    return C


def build_program(mode="full"):
    nc = bass.Bass("TRN2", target_bir_lowering=False)
    io = {}

    def din(name, shape):
        io[name] = nc.dram_tensor(name, list(shape), F32, kind="ExternalInput").ap()

    def dout(name, shape):
        io[name] = nc.dram_tensor(name, list(shape), F32, kind="ExternalOutput").ap()

    def dint(name, shape):
        io[name] = nc.dram_tensor(name, list(shape), F32, kind="Internal").ap()

    din("c_ident", [128, 128])
    din("mem", [MEM, D])
    din("mem_norm_g", [1, D])
    din("xa_norm_g", [2, D])
    din("xa_w_qg", [2, D, 2048])
    din("xa_w_kv", [2, D, 2048])
    din("xa_w_o", [2, D, D])
    din("final_norm_g", [1, D])
    B = Builder(nc)
    outs = []
    if mode == "xa_test":
        din("xin", [OWN, D])
        dout("out", [OWN, D])
        C = setup_consts(B, io)
        phase_xa(B, C, io, 1, io["xin"], io["out"], OWN, final=True)
        outs = ["st_o0", "st_o1"]
    if mode == "l0_test":
        din("xin", [PRE + EXT, D])
        dout("out", [EXT, D])
        for nm, shp in (("ev_norm_g", [1, D]), ("ev_w_in", [1, D, 2560]), ("ev_s5_lambda_re", [1, 32, 64]),
                        ("ev_s5_lambda_im", [1, 32, 64]), ("ev_s5_log_dt", [1, 32]),
                        ("ev_s5_b_re", [1, 32, 64, 16]), ("ev_s5_b_im", [1, 32, 64, 16]),
                        ("ev_s5_c_re", [1, 32, 16, 64]), ("ev_s5_c_im", [1, 32, 16, 64]),
                        ("ev_s5_d", [1, 512]), ("ev_s5_glu_w", [1, 512, 512]), ("ev_s5_glu_b", [1, 512]),
                        ("ev_conv_w", [1, 31, 512]), ("ev_conv_b", [1, 512]), ("ev_conv_ln_g", [1, 512]),
                        ("ev_conv_ln_b", [1, 512]), ("ev_w_out", [1, D, D])):
            din(nm, shp)
        C = setup_consts(B, io)
        phase_l0(B, C, io, io["xin"], io["out"])
        outs = [k for k in ("st_o0", "st_o1") if k in B.S.dsem]
    if mode == "l1_test":
        din("xin", [EXT, D])
        dout("out", [OWN, D])
        din("od_norm_g", [1, D])
        din("od_w_in", [1, D, 4096])
        din("od_w_out", [1, D, D])
        din("c_bias", [80, 128, 128])
        din("c_hmask", [128, 128])
        C = setup_consts(B, io)
        phase_l1(B, C, io, io["xin"], io["out"])
        outs = ["st_o0", "st_o1"]
    if mode == "full":
        din("x_ext", [PRE + EXT, D])
        dout("out", [OWN, D])
        dint("xa", [EXT, D])
        dint("xb", [EXT, D])
        dint("xc", [OWN, D])
        for nm, shp in (("ev_norm_g", [1, D]), ("ev_w_in", [1, D, 2560]), ("ev_s5_lambda_re", [1, 32, 64]),
                        ("ev_s5_lambda_im", [1, 32, 64]), ("ev_s5_log_dt", [1, 32]),
                        ("ev_s5_b_re", [1, 32, 64, 16]), ("ev_s5_b_im", [1, 32, 64, 16]),
                        ("ev_s5_c_re", [1, 32, 16, 64]), ("ev_s5_c_im", [1, 32, 16, 64]),
                        ("ev_s5_d", [1, 512]), ("ev_s5_glu_w", [1, 512, 512]), ("ev_s5_glu_b", [1, 512]),
                        ("ev_conv_w", [1, 31, 512]), ("ev_conv_b", [1, 512]), ("ev_conv_ln_g", [1, 512]),
                        ("ev_conv_ln_b", [1, 512]), ("ev_w_out", [1, D, D]),
                        ("od_norm_g", [1, D]), ("od_w_in", [1, D, 4096]), ("od_w_out", [1, D, D]),
                        ("c_bias", [80, 128, 128]), ("c_hmask", [128, 128])):
            din(nm, shp)
        C = setup_consts(B, io)
        phase_l0(B, C, io, io["x_ext"], io["xa"])
        phase_xa(B, C, io, 0, io["xa"], io["xb"], EXT, final=False)
        phase_l1(B, C, io, io["xb"], io["xc"])
        phase_xa(B, C, io, 1, io["xc"], io["out"], OWN, final=True)
        outs = ["st_o0", "st_o1"]
    B.S.flush(final_wait_keys=outs)
    B.gstack.close()
    return nc


def host_bias_tiles(rel_bias):
    k = np.arange(128)[:, None]
    q = np.arange(128)[None, :]
    out = np.empty((16, 5, 128, 128), np.float32)
    for dl in range(5):
        rel = np.clip(q - k + 128 * dl, -128, 128) + 128
        kb = (k >= 64).astype(np.int64)
        qb = (q >= 64).astype(np.int64)
        dc = -2 * dl + kb - qb
        masked = (dc > 0) | (dc < -8)
        for h in range(16):
            out[h, dl] = np.where(masked, np.float32(NEG), rel_bias[h][rel])
    return out.reshape(80, 128, 128)


_PROGRAM = None


def kernel(**inputs):
    global _PROGRAM
    if _PROGRAM is None:
        _PROGRAM = build_program("full")
    nc = _PROGRAM
    f32 = np.float32
    x = np.asarray(inputs["x"], f32)
    mem = np.asarray(inputs["mem"], f32)
    shared = {"c_ident": np.eye(128, dtype=f32),
              "c_bias": host_bias_tiles(np.asarray(inputs["od_rel_bias"], f32)[0]),
              "mem_norm_g": np.asarray(inputs["mem_norm_g"], f32)[None, :],
              "final_norm_g": np.asarray(inputs["final_norm_g"], f32)[None, :]}
    for k in W_NAMES:
        if k not in shared and k != "od_rel_bias":
            shared[k] = np.ascontiguousarray(np.asarray(inputs[k], f32))
    in_maps = []
    for c in range(NCORES):
        b, hh = divmod(c, 2)
        m = dict(shared)
        if hh == 0:
            m["x_ext"] = np.concatenate([np.zeros((OWN, D), f32), x[b, :OWN]], 0)
            m["c_hmask"] = np.full((128, 128), NEG, f32)
        else:
            m["x_ext"] = np.ascontiguousarray(x[b])
            m["c_hmask"] = np.zeros((128, 128), f32)
        m["mem"] = np.ascontiguousarray(mem[b])
        in_maps.append(m)
    res = run_bass_kernel_spmd(nc, in_maps, core_ids=list(range(NCORES)))
    out = np.empty((NB, SEQ, D), f32)
    for c in range(NCORES):
        b, hh = divmod(c, 2)
        out[b, hh * OWN:(hh + 1) * OWN] = res.results[c]["out"]
    return out
```

```python
import bisect
import contextlib
import itertools
import numpy as np
import concourse.bass as bass
import concourse.mybir as mybir
from concourse.bass_utils import run_bass_kernel_spmd

F32 = mybir.dt.float32
BF16 = mybir.dt.bfloat16
I32 = mybir.dt.int32
AF = mybir.ActivationFunctionType
ALU = mybir.AluOpType

D = 1024
SEQ = 8192
NB = 4
NCORES = 8
OWN = 4096
HALO = 512
EXT = OWN + HALO
PRE = SEQ // 2 - HALO
MEM = 256
EPS = 1e-6
NEG = -30000.0


class _Op:
    __slots__ = ("eng", "fn", "reads", "writes", "dma", "deps", "sig", "tick", "idx", "dcount",
                 "after")


class Sched:
    NDSEM = 40

    def __init__(self, nc, stack, same_engine_sync=True):
        self.nc = nc
        self.engs = {"pe": nc.tensor, "act": nc.scalar, "dve": nc.vector,
                     "pool": nc.gpsimd, "sp": nc.sync}
        self.same = same_engine_sync
        self.ops = []
        self.flushed = 0
        self.last_write = {}
        self.readers = {}
        self.tick = {e: 0 for e in self.engs}
        self.dcount = {}
        self.waited = {e: {} for e in self.engs}
        self.esem = {e: stack.enter_context(nc.semaphore("sem_" + e)) for e in self.engs}
        self.dsem_pool = [stack.enter_context(nc.semaphore("dsem_%d" % i))
                          for i in range(self.NDSEM)]
        self.dsem = {}
        self.last_eng_op = {}
        self.last_dma_op = {}
        self.n_wait = 0
        self.sigidx = {e: [] for e in self.engs}

    mute = False

    def op(self, eng, fn, reads=(), writes=(), dma=None, after=()):
        if self.mute:
            return None
        o = _Op()
        o.eng = eng
        o.fn = fn
        o.reads = tuple(reads)
        o.writes = tuple(writes)
        o.dma = dma
        o.sig = False
        o.after = tuple(after)
        o.idx = len(self.ops)
        self.ops.append(o)
        return o

    def barrier(self):
        self.flush()
        after = [o for o in self.last_eng_op.values()] + [o for o in self.last_dma_op.values()]
        sp = self.engs["sp"]
        b = self.op("sp", lambda: sp.nop(), after=after)
        for e in ("pe", "act", "dve", "pool"):
            en = self.engs[e]
            self.op(e, (lambda en=en: en.nop()), after=[b])
        self.flush()

    def flush(self, final_wait_keys=()):
        ops = self.ops
        new = ops[self.flushed:]
        for o in new:
            deps = set()
            for k in o.reads:
                if k in self.last_write:
                    deps.add(self.last_write[k])
            for k in o.writes:
                if k in self.last_write:
                    deps.add(self.last_write[k])
                for r in self.readers.get(k, ()):
                    deps.add(r)
            for a in o.after:
                deps.add(a.idx)
            deps.discard(o.idx)
            for k in o.reads:
                self.readers.setdefault(k, []).append(o.idx)
            for k in o.writes:
                self.last_write[k] = o.idx
                self.readers[k] = []
            best = {}
            dd = []
            for d in deps:
                od = ops[d]
                if od.dma is not None:
                    dd.append(d)
                    continue
                if od.eng == o.eng and (od.eng in ("pe", "sp") or not self.same):
                    continue
                if od.eng not in best or best[od.eng] < d:
                    best[od.eng] = d
            o.deps = dd
            for d in best.values():
                od = ops[d]
                if not od.sig:
                    if d >= self.flushed:
                        od.sig = True
                    else:
                        lst = self.sigidx[od.eng]
                        j = bisect.bisect_left(lst, d)
                        d = lst[j]
                o.deps.append(d)
            if o.dma is None:
                self.last_eng_op[o.eng] = o
            else:
                self.last_dma_op[o.dma] = o
        for e, o in self.last_eng_op.items():
            if o.idx >= self.flushed:
                o.sig = True
        for o in new:
            if o.dma is not None:
                if o.dma not in self.dsem:
                    self.dsem[o.dma] = self.dsem_pool[len(self.dsem)]
                self.dcount[o.dma] = self.dcount.get(o.dma, 0) + 1
                o.dcount = self.dcount[o.dma]
            elif o.sig:
                self.tick[o.eng] += 1
                o.tick = self.tick[o.eng]
                self.sigidx[o.eng].append(o.idx)
        for o in new:
            e = self.engs[o.eng]
            w = self.waited[o.eng]
            need = {}
            for d in o.deps:
                od = ops[d]
                if od.dma is not None:
                    s = self.dsem[od.dma]
                    v = 16 * od.dcount
                else:
                    s = self.esem[od.eng]
                    v = od.tick
                if need.get(s, 0) < v:
                    need[s] = v
            for s, v in need.items():
                if w.get(s, 0) < v:
                    e.wait_ge(s, v)
                    w[s] = v
                    self.n_wait += 1
            inst = o.fn()
            if o.dma is not None:
                inst.then_inc(self.dsem[o.dma], 16)
            elif o.sig:
                inst.then_inc(self.esem[o.eng], 1)
            o.fn = None
        self.flushed = len(ops)
        sp = self.engs["sp"]
        for k in final_wait_keys:
            sp.wait_ge(self.dsem[k], 16 * self.dcount[k])


class V:
    __slots__ = ("ap", "keys")

    def __init__(self, ap, keys):
        self.ap = ap
        self.keys = tuple(keys)


class Tile:
    def __init__(self, name, handle, shape, kaxes):
        self.name = name
        self.t = handle
        self.shape = list(shape)
        self.kaxes = kaxes

    def __getitem__(self, idx):
        if not isinstance(idx, tuple):
            idx = (idx,)
        ranges = []
        for a in range(self.kaxes):
            i = idx[1 + a] if len(idx) > 1 + a else slice(None)
            n = self.shape[1 + a]
            if isinstance(i, int):
                ranges.append((i,))
            else:
                ranges.append(tuple(range(*i.indices(n))))
        keys = [(self.name,) + c for c in itertools.product(*ranges)]
        return V(self.t[idx], keys)


def _k(x):
    return x.keys if isinstance(x, V) else ()


def _a(x):
    return x.ap if isinstance(x, V) else x


class Builder:
    def __init__(self, nc):
        self.nc = nc
        self.gstack = contextlib.ExitStack()
        self.S = Sched(nc, self.gstack)
        self.pstack = None
        self.uid = 0
        self.nkey = 0
        self.rr = {}

    def phase_begin(self):
        self.pstack = contextlib.ExitStack()
        self.nkey = 0

    def fresh(self):
        self.nkey += 1
        return "u%d" % self.nkey

    def phase_end(self):
        self.S.barrier()
        self.pstack.close()
        self.pstack = None

    def tile(self, name, shape, dtype=F32, kaxes=0, space="sbuf", persistent=False):
        self.uid += 1
        nm = "%s_%d" % (name, self.uid)
        st = self.gstack if persistent else self.pstack
        if space == "sbuf":
            h = st.enter_context(self.nc.sbuf_tensor(nm, list(shape), dtype))
        else:
            h = st.enter_context(self.nc.psum_tensor(nm, list(shape), dtype))
        return Tile(nm, h, shape, kaxes)

    def pool(self, name, n, shape, dtype=F32, kaxes=0, space="sbuf"):
        tiles = [self.tile("%s%d" % (name, i), shape, dtype, kaxes, space) for i in range(n)]
        self.rr[id(tiles)] = 0
        return tiles

    def nxt(self, tiles):
        i = self.rr[id(tiles)]
        self.rr[id(tiles)] = (i + 1) % len(tiles)
        return tiles[i]

    def mm(self, out, lhsT, rhs, start=True, stop=True, **kw):
        nc = self.nc
        rd = lhsT.keys + rhs.keys + (() if start else out.keys)
        self.S.op("pe", lambda: nc.tensor.matmul(out.ap, lhsT=lhsT.ap, rhs=rhs.ap, start=start,
                                                 stop=stop, **kw), rd, out.keys)

    def transpose(self, out, in_, ident):
        nc = self.nc
        self.S.op("pe", lambda: nc.tensor.transpose(out=out.ap, in_=in_.ap, identity=ident.ap),
                  in_.keys + ident.keys, out.keys)

    def act(self, out, in_, func, bias=None, scale=None, accum=None):
        nc = self.nc
        kw = {}
        if bias is not None:
            kw["bias"] = _a(bias)
        if scale is not None:
            kw["scale"] = _a(scale)
        if accum is not None:
            kw["accum_out"] = accum.ap
        self.S.op("act", lambda: nc.scalar.activation(out=out.ap, in_=in_.ap, func=func, **kw),
                  in_.keys + _k(bias) + _k(scale), out.keys + _k(accum))

    def _ve(self, eng):
        return self.nc.vector if eng == "dve" else self.nc.gpsimd

    def tt(self, eng, out, in0, in1, op):
        e = self._ve(eng)
        self.S.op(eng, lambda: e.tensor_tensor(out=out.ap, in0=in0.ap, in1=in1.ap, op=op),
                  in0.keys + in1.keys, out.keys)

    def ts(self, eng, out, in0, s1, s2=None, op0=ALU.mult, op1=None):
        e = self._ve(eng)
        kw = {}
        if op1 is not None:
            kw["op1"] = op1
        self.S.op(eng, lambda: e.tensor_scalar(out=out.ap, in0=in0.ap, scalar1=_a(s1),
                                               scalar2=_a(s2), op0=op0, **kw),
                  in0.keys + _k(s1) + _k(s2), out.keys)

    def stt(self, eng, out, in0, scalar, in1, op0, op1):
        e = self._ve(eng)
        self.S.op(eng, lambda: e.scalar_tensor_tensor(out=out.ap, in0=in0.ap, scalar=_a(scalar),
                                                      in1=in1.ap, op0=op0, op1=op1),
                  in0.keys + _k(scalar) + in1.keys, out.keys)

    def copy(self, eng, out, in_):
        if eng == "act":
            nc = self.nc
            self.S.op("act", lambda: nc.scalar.copy(out=out.ap, in_=in_.ap), in_.keys, out.keys)
        else:
            e = self._ve(eng)
            self.S.op(eng, lambda: e.tensor_copy(out=out.ap, in_=in_.ap), in_.keys, out.keys)

    def memset(self, eng, out, val):
        e = self._ve(eng)
        self.S.op(eng, lambda: e.memset(out.ap, val), (), out.keys)

    def recip(self, out, in_):
        nc = self.nc
        self.S.op("dve", lambda: nc.vector.reciprocal(out=out.ap, in_=in_.ap), in_.keys, out.keys)

    def scan(self, out, d0, d1, init, op0=ALU.mult, op1=ALU.add):
        nc = self.nc
        self.S.op("dve", lambda: nc.vector.tensor_tensor_scan(out=out.ap, data0=d0.ap, data1=d1.ap,
                                                              initial=_a(init), op0=op0, op1=op1),
                  d0.keys + d1.keys + _k(init), out.keys)

    def dma(self, eng, out, in_, key, **kw):
        e = self.S.engs[eng]
        self.S.op(eng, lambda: e.dma_start(out=_a(out), in_=_a(in_), **kw), _k(in_), _k(out),
                  dma=key)


def rows_view(dram, r0, nsub):
    return dram[r0:r0 + 128 * nsub, :].rearrange("(s p) d -> p s d", p=128)


def load_w_bf16(B, name, dram, kdim, ndim, key, c0=0, eng="pool"):
    kc = kdim // 128
    t = B.tile(name, [128, kc, ndim], BF16, kaxes=0)
    src = dram[:, c0:c0 + ndim].rearrange("(k p) n -> p k n", p=128)
    half = max(1, kc // 2)
    for i in range(0, kc, half):
        B.dma(eng, V(t.t[:, i:i + half, :], t[:].keys), src[:, i:i + half, :], key)
    return t


def bcast_row(B, name, dram_row, n, key, persistent=False):
    t = B.tile(name, [128, n], F32, persistent=persistent)
    B.dma("sp", t[:], dram_row.broadcast_to([128, n]), key)
    return t


def rmsnorm_tile(B, xt_v, ss_v, rstd_v, g_t, out_v, sq_scratch):
    B.act(sq_scratch, xt_v, AF.Square, accum=ss_v)
    B.act(rstd_v, ss_v, AF.Sqrt, bias=EPS, scale=1.0 / D)
    B.recip(rstd_v, rstd_v)
    B.stt("dve", out_v, xt_v, rstd_v, g_t[:], ALU.mult, ALU.mult)


def mem_kv(B, C, io, l, kt, vv):
    outer = B.pstack
    B.pstack = contextlib.ExitStack()
    ident = C["ident"]
    g = bcast_row(B, "memg", io["mem_norm_g"], D, B.fresh())
    mt = B.tile("memx", [128, 2, D], F32, kaxes=1)
    B.dma("sp", mt[:], rows_view(io["mem"], 0, 2), "ld_x0")
    sq = B.tile("memsq", [128, D], F32)
    ss = B.tile("memss", [128, 2], F32, kaxes=1)
    rstd = B.tile("memrs", [128, 2], F32, kaxes=1)
    mn = B.tile("memn", [128, 2, D], BF16, kaxes=1)
    memT = B.tile("memT", [128, 8, 2, 128], BF16, kaxes=2)
    pbig = B.pool("mpb", 4, [128, 512], F32, space="psum")
    ptr = B.pool("mptr", 2, [128, 4, 128], BF16, space="psum")
    for s in range(2):
        rmsnorm_tile(B, mt[:, s, :], ss[:, s:s + 1], rstd[:, s:s + 1], g, mn[:, s, :], sq[:])
        for kg in range(2):
            p = B.nxt(ptr)
            for j in range(4):
                kc = kg * 4 + j
                B.transpose(V(p.t[:, j, :], p[:].keys), mn[:, s, kc * 128:(kc + 1) * 128], ident[:])
            B.copy("act" if kg == 0 else "dve", memT[:, kg * 4:(kg + 1) * 4, s, :], p[:])
    wkv = load_w_bf16(B, "wkv%d" % l, io["xa_w_kv"][l], D, 2048, B.fresh())
    for j in range(8):
        p = B.nxt(pbig)
        for kc in range(8):
            B.mm(V(p.t[:, 0:256], p[:].keys), V(wkv.t[:, kc, j * 128:(j + 1) * 128], wkv[:].keys),
                 V(memT.t[:, kc, :, :], memT[:, kc].keys), start=(kc == 0), stop=(kc == 7))
        B.copy("act" if j % 2 == 0 else "dve", kt[:, j, :], V(p.t[:, 0:256], p[:].keys))
    for m in range(2):
        for n in range(2):
            p = B.nxt(pbig)
            for kc in range(8):
                B.mm(p[:], memT[:, kc, m, :],
                     V(wkv.t[:, kc, 1024 + n * 512:1024 + (n + 1) * 512], wkv[:].keys),
                     start=(kc == 0), stop=(kc == 7))
            B.copy("act" if n == 0 else "dve", V(vv.t[:, m, n * 512:(n + 1) * 512], vv[:, m].keys),
                   p[:])
    B.S.barrier()
    B.pstack.close()
    B.pstack = outer


def phase_xa(B, C, io, l, xin, xout, ntok, final):
    T = 512
    NS = T // 128
    nt = ntok // T
    B.phase_begin()
    ident, ones = C["ident"], C["ones"]
    KTl = B.tile("KT%d" % l, [128, 8, 256], BF16, kaxes=1)
    VVl = B.tile("VV%d" % l, [128, 2, 1024], BF16, kaxes=1)
    mem_kv(B, C, io, l, KTl, VVl)
    g = bcast_row(B, "xag", io["xa_norm_g"][l:l + 1, :], D, B.fresh())
    gf = bcast_row(B, "fng", io["final_norm_g"], D, B.fresh()) if final else None
    wqg = load_w_bf16(B, "wqg", io["xa_w_qg"][l], D, 2048, B.fresh())
    wo = load_w_bf16(B, "wo", io["xa_w_o"][l], D, D, B.fresh())
    xts = B.pool("xt", 2, [128, NS, D], F32, kaxes=1)
    sq = B.tile("sq", [128, D], F32)
    ss = B.tile("ss", [128, 2 * NS], F32, kaxes=1)
    rstd = B.tile("rstd", [128, 2 * NS], F32, kaxes=1)
    hn = B.pool("hn", 2, [128, D], BF16)
    hnT = B.tile("hnT", [128, 8, NS, 128], BF16, kaxes=2)
    qT = B.tile("qT", [128, 8, T], BF16, kaxes=1)
    sgT = B.tile("sgT", [128, 8, T], BF16, kaxes=1)
    Et = B.pool("E", 4, [128, T], BF16)
    rs = B.pool("rs", 2, [128, T], F32)
    wt = B.pool("wt", 2, [128, T], F32)
    ogT = B.tile("ogT", [128, 8, T], BF16, kaxes=1)
    pbig = B.pool("pb", 6, [128, 512], F32, space="psum")
    ptr = B.pool("ptr", 2, [128, 4, 128], BF16, space="psum")

    def load(i):
        xt = xts[i % 2]
        B.dma("sp", xt[:], rows_view(xin, i * T, NS), "ld_x%d" % (i % 2))

    load(0)
    for i in range(nt):
        if i + 1 < nt:
            load(i + 1)
        xt = xts[i % 2]
        for s in range(NS):
            h = B.nxt(hn)
            rmsnorm_tile(B, xt[:, s, :], ss[:, s:s + 1], rstd[:, s:s + 1], g, h[:], sq[:])
            for kg in range(2):
                p = B.nxt(ptr)
                for j in range(4):
                    kc = kg * 4 + j
                    B.transpose(V(p.t[:, j, :], p[:].keys), V(h.t[:, kc * 128:(kc + 1) * 128], h[:].keys),
                                ident[:])
                B.copy("act" if kg == 0 else "dve", hnT[:, kg * 4:(kg + 1) * 4, s, :], p[:])
        for j in range(16):
            p = B.nxt(pbig)
            for kc in range(8):
                B.mm(p[:], V(wqg.t[:, kc, j * 128:(j + 1) * 128], wqg[:].keys),
                     V(hnT.t[:, kc, :, :], hnT[:, kc].keys), start=(kc == 0), stop=(kc == 7))
            if j < 8:
                B.ts("dve", qT[:, j, :], p[:], 1.0 / 16.0, None, ALU.mult)
            else:
                B.act(sgT[:, j - 8, :], p[:], AF.Silu)
        for hh in range(4):
            es = []
            for m in range(2):
                p = B.nxt(pbig)
                for dc in range(2):
                    B.mm(p[:], V(KTl.t[:, 2 * hh + dc, m * 128:(m + 1) * 128], KTl[:, 2 * hh + dc].keys),
                         qT[:, 2 * hh + dc, :], start=(dc == 0), stop=(dc == 1))
                e = B.nxt(Et)
                B.act(e[:], p[:], AF.Exp)
                es.append(e)
            psum = B.nxt(pbig)
            for m in range(2):
                B.mm(psum[:], ones[:], es[m][:], start=(m == 0), stop=(m == 1))
            r = B.nxt(rs)
            B.recip(r[:], psum[:])
            for dc in range(2):
                p = B.nxt(pbig)
                for m in range(2):
                    c0 = hh * 256 + dc * 128
                    B.mm(p[:], V(VVl.t[:, m, c0:c0 + 128], VVl[:, m].keys), es[m][:],
                         start=(m == 0), stop=(m == 1))
                w = B.nxt(wt)
                B.tt("pool", w[:], r[:], sgT[:, 2 * hh + dc, :], ALU.mult)
                B.tt("dve", ogT[:, 2 * hh + dc, :], p[:], w[:], ALU.mult)
        for s in range(NS):
            for n in range(2):
                p = B.nxt(pbig)
                for kc in range(8):
                    B.mm(p[:], V(ogT.t[:, kc, s * 128:(s + 1) * 128], ogT[:, kc].keys),
                         V(wo.t[:, kc, n * 512:(n + 1) * 512], wo[:].keys), start=(kc == 0), stop=(kc == 7))
                xv = V(xt.t[:, s, n * 512:(n + 1) * 512], xt[:, s].keys)
                B.tt("dve", xv, xv, p[:], ALU.add)
            if final:
                rmsnorm_tile(B, xt[:, s, :], ss[:, NS + s:NS + s + 1], rstd[:, NS + s:NS + s + 1], gf,
                             xt[:, s, :], sq[:])
        B.dma("sp", rows_view(xout, i * T, NS), xt[:], "st_o%d" % (i % 2))
    B.phase_end()


def phase_l1(B, C, io, xin, xout):
    T = 256
    NS = 2
    nt = EXT // T
    NH = HALO // T
    B.phase_begin()
    ident, ones = C["ident"], C["ones"]
    g = bcast_row(B, "l1g", io["od_norm_g"], D, B.fresh())
    win = load_w_bf16(B, "l1win", io["od_w_in"][0], D, 4096, B.fresh())
    wo = load_w_bf16(B, "l1wo", io["od_w_out"][0], D, D, B.fresh())
    bias = B.tile("l1bias", [128, 80, 128], BF16)
    bsrc = io["c_bias"].rearrange("t k q -> k t q")
    bkey = B.fresh()
    for t0 in range(0, 80, 8):
        B.dma("pool", V(bias.t[:, t0:t0 + 8, :], bias[:].keys), bsrc[:, t0:t0 + 8, :], bkey)
    hmask = B.tile("l1hm", [128, 128], BF16)
    B.dma("pool", hmask[:], io["c_hmask"][:, :], B.fresh())
    xts = B.pool("l1xt", 2, [128, NS, D], F32, kaxes=1)
    sq = B.tile("l1sq", [128, D], F32)
    ss = B.tile("l1ss", [128, NS], F32, kaxes=1)
    rstd = B.tile("l1rstd", [128, NS], F32, kaxes=1)
    hn = B.pool("l1hn", 2, [128, D], BF16)
    hnT = B.tile("l1hnT", [128, 8, NS, 128], BF16, kaxes=2)
    KTr = B.tile("l1KT", [128, 8, 8, 128], BF16, kaxes=2)
    Vr = B.tile("l1V", [128, 8, D], BF16, kaxes=1)
    qTA = B.tile("l1qA", [128, 8, T], BF16, kaxes=1)
    qTB = B.tile("l1qB", [128, 8, T], BF16, kaxes=1)
    sgT = B.tile("l1sg", [128, 8, T], BF16, kaxes=1)
    ogT = B.tile("l1og", [128, 8, T], BF16, kaxes=1)
    Ep = B.pool("l1E", 4, [128, 640], BF16)
    rp = B.pool("l1r", 2, [128, 256], F32)
    wp = B.pool("l1w", 2, [128, 128], F32)
    Sall = B.tile("l1S", [128, 2, 1024], F32, kaxes=1, space="psum")
    scnt = [0]
    pb = B.pool("l1pb", 2, [128, 512], F32, space="psum")
    OSp = B.pool("l1OS", 1, [128, 512], F32, space="psum")
    ptr2 = B.tile("l1ptr", [128, 1, 4, 128], BF16, kaxes=1, space="psum")
    pcnt = [0]
    B.memset("pool", qTA[:], 0.0)
    B.memset("pool", qTB[:], 0.0)

    def load(i):
        B.dma("sp", xts[i % 2][:], rows_view(xin, i * T, NS), "ld_x%d" % (i % 2))

    load(0)
    cnt = 0
    for i in range(nt):
        if i + 1 < nt:
            load(i + 1)
        xt = xts[i % 2]
        own = i >= NH
        for s in range(NS):
            h = B.nxt(hn)
            rmsnorm_tile(B, xt[:, s, :], ss[:, s:s + 1], rstd[:, s:s + 1], g, h[:], sq[:])
            for kg in range(2):
                pi = 0
                pcnt[0] += 1
                for j in range(4):
                    kc = kg * 4 + j
                    B.transpose(V(ptr2.t[:, pi, j, :], ptr2[:, pi].keys),
                                V(h.t[:, kc * 128:(kc + 1) * 128], h[:].keys), ident[:])
                B.copy("act" if kg == 0 else "dve", hnT[:, kg * 4:(kg + 1) * 4, s, :], ptr2[:, pi])
        sl0 = (2 * i) % 8
        for j in range(8):
            p = B.nxt(pb)
            pv = V(p.t[:, 0:T], p[:].keys)
            for kc in range(8):
                B.mm(pv, V(win.t[:, kc, 1024 + j * 128:1024 + (j + 1) * 128], win[:].keys),
                     V(hnT.t[:, kc, :, :], hnT[:, kc].keys), start=(kc == 0), stop=(kc == 7))
            cnt += 1
            B.copy("act" if cnt % 2 else "dve",
                   V(KTr.t[:, j, sl0:sl0 + 2, :], KTr[:, j, sl0:sl0 + 2].keys), pv)
        for s in range(NS):
            for n in range(2):
                p = B.nxt(pb)
                for kc in range(8):
                    B.mm(p[:], hnT[:, kc, s, :],
                         V(win.t[:, kc, 2048 + n * 512:2048 + (n + 1) * 512], win[:].keys),
                         start=(kc == 0), stop=(kc == 7))
                cnt += 1
                B.copy("act" if cnt % 2 else "dve",
                       V(Vr.t[:, sl0 + s, n * 512:(n + 1) * 512], Vr[:, sl0 + s].keys), p[:])
        if not own:
            continue
        for j in range(8):
            p = B.nxt(pb)
            pv = V(p.t[:, 0:T], p[:].keys)
            for kc in range(8):
                B.mm(pv, V(win.t[:, kc, j * 128:(j + 1) * 128], win[:].keys),
                     V(hnT.t[:, kc, :, :], hnT[:, kc].keys), start=(kc == 0), stop=(kc == 7))
            B.ts("dve", V(qTA.t[0:64, j, :], qTA[:, j].keys), V(p.t[0:64, 0:T], p[:].keys), 0.125, None,
                 ALU.mult)
            B.ts("dve", V(qTB.t[64:128, j, :], qTB[:, j].keys), V(p.t[64:128, 0:T], p[:].keys), 0.125, None,
                 ALU.mult)
        for j in range(8):
            p = B.nxt(pb)
            pv = V(p.t[:, 0:T], p[:].keys)
            for kc in range(8):
                B.mm(pv, V(win.t[:, kc, 3072 + j * 128:3072 + (j + 1) * 128], win[:].keys),
                     V(hnT.t[:, kc, :, :], hnT[:, kc].keys), start=(kc == 0), stop=(kc == 7))
            B.act(sgT[:, j, :], pv, AF.Silu)
        for s in range(NS):
            u = 2 * i + s
            qc = slice(s * 128, (s + 1) * 128)
            for j in range(8):
                Es = []
                for hb in range(2):
                    hd = 2 * j + hb
                    qq = qTA if hb == 0 else qTB
                    sb_ = scnt[0] % 2
                    scnt[0] += 1
                    Skeys = Sall[:, sb_].keys
                    for dl in range(5):
                        uk = u - dl
                        slot = uk % 8
                        blk = V(Sall.t[:, sb_, dl * 128:(dl + 1) * 128], Skeys)
                        halo = uk < HALO // 128
                        B.mm(blk, KTr[:, j, slot, :], V(qq.t[:, j, qc], qq[:, j].keys), start=True, stop=False)
                        B.mm(blk, ident[:], V(bias.t[:, hd * 5 + dl, :], bias[:].keys), start=False,
                             stop=not halo)
                        if halo:
                            B.mm(blk, ident[:], hmask[:], start=False, stop=True)
                    E = B.nxt(Ep)
                    B.act(E[:], V(Sall.t[:, sb_, 0:640], Skeys), AF.Exp)
                    Es.append(E)
                OS = B.nxt(OSp)
                for hb in range(2):
                    for dl in range(5):
                        slot = (u - dl) % 8
                        B.mm(V(OS.t[:, hb * 128:(hb + 1) * 128], OS[:].keys),
                             V(Vr.t[:, slot, j * 128:(j + 1) * 128], Vr[:, slot].keys),
                             V(Es[hb].t[:, dl * 128:(dl + 1) * 128], Es[hb][:].keys),
                             start=(dl == 0), stop=(dl == 4))
                for hb in range(2):
                    for dl in range(5):
                        B.mm(V(OS.t[:, 256 + hb * 128:256 + (hb + 1) * 128], OS[:].keys), ones[:],
                             V(Es[hb].t[:, dl * 128:(dl + 1) * 128], Es[hb][:].keys),
                             start=(dl == 0), stop=(dl == 4))
                r = B.nxt(rp)
                B.recip(r[:], V(OS.t[:, 256:512], OS[:].keys))
                w = B.nxt(wp)
                for hb in range(2):
                    pr = slice(64 * hb, 64 * hb + 64)
                    B.tt("pool", V(w.t[pr, :], w[:].keys), V(r.t[pr, hb * 128:(hb + 1) * 128], r[:].keys),
                         V(sgT.t[pr, j, qc], sgT[:, j].keys), ALU.mult)
                for hb in range(2):
                    pr = slice(64 * hb, 64 * hb + 64)
                    B.tt("dve", V(ogT.t[pr, j, qc], ogT[:, j].keys),
                         V(OS.t[pr, hb * 128:(hb + 1) * 128], OS[:].keys), V(w.t[pr, :], w[:].keys), ALU.mult)
        for s in range(NS):
            for n in range(2):
                p = B.nxt(pb)
                for kc in range(8):
                    B.mm(p[:], V(ogT.t[:, kc, s * 128:(s + 1) * 128], ogT[:, kc].keys),
                         V(wo.t[:, kc, n * 512:(n + 1) * 512], wo[:].keys), start=(kc == 0), stop=(kc == 7))
                xv = V(xt.t[:, s, n * 512:(n + 1) * 512], xt[:, s].keys)
                B.tt("dve", xv, xv, p[:], ALU.add)
        B.dma("sp", rows_view(xout, (i - NH) * T, NS), xt[:], "st_o%d" % (i % 2))
    B.phase_end()


TWO_PI = 6.283185307179586
DEBUG_STOP = None
DEBUG_NT = None


def load_cols(B, name, dram_row, key):
    t = B.tile(name, [128, 4], F32)
    B.dma("sp", t[:], dram_row.rearrange("o (c p) -> p (o c)", p=128), key,
          allow_slow_non_contiguous=True)
    return t


def phase_l0(B, C, io, xin, xout):
    T = 256
    NS = 2
    T5 = 128
    npre = PRE // T
    nt = (PRE + EXT) // T
    B.phase_begin()
    ident, ones = C["ident"], C["ones"]
    g = bcast_row(B, "l0g", io["ev_norm_g"], D, B.fresh())
    win = load_w_bf16(B, "l0win", io["ev_w_in"][0], D, 2560, B.fresh())
    wo = load_w_bf16(B, "l0wo", io["ev_w_out"][0], D, D, B.fresh())
    gluw = load_w_bf16(B, "l0gluw", io["ev_s5_glu_w"][0], 512, 512, B.fresh())
    vecs = B.tile("l0vecs", [128, 20], F32)
    lrli = B.tile("l0lrli", [128, 32], F32)

    class _Sub:
        def __init__(self, t, c0, n):
            self.t = _SubT(t, c0)
            self._t = t

        def __getitem__(self, idx):
            return self._t[:]

    class _SubT:
        def __init__(self, t, c0):
            self.tt_, self.c0 = t, c0

        def __getitem__(self, idx):
            p, c = idx
            return self.tt_.t[p, self.c0 + c.start:self.c0 + c.stop]

    Dt, glub, convb, lng, lnb = [_Sub(vecs, 4 * i, 4) for i in range(5)]
    onesM = B.tile("l0onesM", [128, 128], BF16)
    B.memset("pool", onesM[:], 1.0 / 512.0)
    CS = B.tile("l0CS", [128, 16, 2, T5], F32)
    BT = [B.tile("l0BT%d" % ri, [128, 16, 128], BF16) for ri in range(2)]
    CT = [B.tile("l0CT%d" % ri, [128, 16, 128], BF16) for ri in range(3)]
    DG = B.tile("l0DG", [128, 4, 31, 128], BF16)
    mag = B.tile("l0mag", [128, 16], F32)
    cT = B.tile("l0cT", [128, 16], F32)
    sT = B.tile("l0sT", [128, 16], F32)
    init = [B.tile("l0init%d" % ri, [128, 16], F32, kaxes=1) for ri in range(2)]
    B.memset("pool", init[0][:], 0.0)
    B.memset("pool", init[1][:], 0.0)

    outer = B.pstack
    B.pstack = contextlib.ExitStack()
    pps = B.tile("l0pps", [128, 4, 128], BF16, space="psum")
    cnt = [0]

    def small(name):
        return B.tile(name, [128, 16], F32)

    ident32 = B.tile("l0id32", [128, 128], F32)
    B.dma("sp", ident32[:], io["c_ident"][:, :], B.fresh())
    Vst = B.tile("l0Vst", [32, 128], F32)
    vkey = B.fresh()
    for vi, nm in enumerate(("ev_s5_d", "ev_s5_glu_b", "ev_conv_b", "ev_conv_ln_g", "ev_conv_ln_b")):
        B.dma("sp", V(Vst.t[4 * vi:4 * vi + 4, :], Vst[:].keys),
              io[nm].rearrange("o (c p) -> (o c) p", p=128), vkey)
    LL = B.tile("l0LL", [32, 128], F32)
    lkey2 = B.fresh()
    B.dma("sp", V(LL.t[0:16, :], LL[:].keys),
          io["ev_s5_lambda_re"][0].rearrange("(k g) p -> k (g p)", g=2), lkey2)
    B.dma("sp", V(LL.t[16:32, :], LL[:].keys),
          io["ev_s5_lambda_im"][0].rearrange("(k g) p -> k (g p)", g=2), lkey2)
    pv32 = B.tile("l0pv32", [128, 64], F32, space="psum")
    B.S.op("pe", lambda: B.nc.tensor.transpose(out=pv32.t[:, 0:32], in_=LL.t[0:32, :],
                                               identity=ident32.t[0:32, 0:32]),
           LL[:].keys + ident32[:].keys, pv32[:].keys)
    B.S.op("pe", lambda: B.nc.tensor.transpose(out=pv32.t[:, 32:52], in_=Vst.t[0:20, :],
                                               identity=ident32.t[0:20, 0:20]),
           Vst[:].keys + ident32[:].keys, pv32[:].keys)
    B.copy("dve", lrli[:], V(pv32.t[:, 0:32], pv32[:].keys))
    B.copy("dve", vecs[:], V(pv32.t[:, 32:52], pv32[:].keys))
    lr = _Sub(lrli, 0, 16)
    li = _Sub(lrli, 16, 16)
    lr = V(lrli.t[:, 0:16], lrli[:].keys)
    li = V(lrli.t[:, 16:32], lrli[:].keys)
    LB = bcast_row(B, "l0LB", io["ev_s5_log_dt"], 32, B.fresh())
    ldt = small("ldt")
    for gg in range(2):
        pr = slice(64 * gg, 64 * gg + 64)
        B.copy("dve", V(ldt.t[pr, :], ldt[:].keys),
               V(LB.t[pr, :].rearrange("p (k g) -> p k g", g=2)[:, :, gg], LB[:].keys))
    if DEBUG_STOP == "loads":
        B.S.mute = True
    dt = small("dt")
    B.act(dt[:], ldt[:], AF.Exp)
    lrdt = small("lrdt")
    B.tt("dve", lrdt[:], lr, dt[:], ALU.mult)
    B.act(mag[:], lrdt[:], AF.Exp)
    th = small("th")
    B.tt("dve", th[:], li, dt[:], ALU.mult)
    tq = small("tq")
    B.ts("dve", tq[:], th[:], 1.0 / TWO_PI, None, ALU.mult)
    tqi = B.tile("tqi", [128, 16], I32)
    B.copy("dve", tqi[:], tq[:])
    tqf = small("tqf")
    B.copy("dve", tqf[:], tqi[:])
    thr = small("thr")
    B.stt("dve", thr[:], tqf[:], -TWO_PI, th[:], ALU.mult, ALU.add)
    ath = small("ath")
    B.ts("dve", ath[:], thr[:], -1.0, None, ALU.mult)
    B.tt("dve", ath[:], ath[:], thr[:], ALU.max)
    cm, sm = small("c1"), small("s1")
    B.act(cm[:], ath[:], AF.Sin, bias=C["halfpi"][:, 0:1], scale=-1.0)
    B.act(sm[:], thr[:], AF.Sin)
    if DEBUG_STOP == "trig":
        B.S.mute = True
    abre, abim = small("abre"), small("abim")
    B.tt("dve", abre[:], mag[:], cm[:], ALU.mult)
    B.tt("dve", abim[:], mag[:], sm[:], ALU.mult)
    den, t0, t1 = small("den"), small("t0"), small("t1")
    B.tt("dve", den[:], lr, lr, ALU.mult)
    B.tt("dve", t0[:], li, li, ALU.mult)
    B.tt("dve", den[:], den[:], t0[:], ALU.add)
    B.recip(den[:], den[:])
    nr = small("nr")
    B.ts("dve", nr[:], abre[:], -1.0, None, ALU.add)
    cre, cim = small("cre"), small("cim")
    B.tt("dve", t0[:], nr[:], lr, ALU.mult)
    B.tt("dve", t1[:], abim[:], li, ALU.mult)
    B.tt("dve", t0[:], t0[:], t1[:], ALU.add)
    B.tt("dve", cre[:], t0[:], den[:], ALU.mult)
    B.tt("dve", t0[:], abim[:], lr, ALU.mult)
    B.tt("dve", t1[:], nr[:], li, ALU.mult)
    B.tt("dve", t0[:], t0[:], t1[:], ALU.subtract)
    B.tt("dve", cim[:], t0[:], den[:], ALU.mult)
    if DEBUG_STOP == "abar":
        B.S.mute = True
    B.memset("pool", V(CS.t[:, :, 0, 0:1], CS[:].keys), 1.0)
    B.memset("pool", V(CS.t[:, :, 1, 0:1], CS[:].keys), 0.0)
    tb = [B.tile("l0tb%d" % i, [128, 16, 64], F32) for i in range(4)]

    def bc(t, n):
        return V(t.t[:, :].rearrange("p (k o) -> p k o", o=1).broadcast_to([128, 16, n]), t[:].keys)

    for m in range(7):
        n = 1 << m
        cosn = V(CS.t[:, :, 0, 0:n], CS[:].keys)
        sinn = V(CS.t[:, :, 1, 0:n], CS[:].keys)
        tv = [V(t.t[:, :, 0:n], t[:].keys) for t in tb]
        B.tt("dve", tv[0], cosn, bc(cm, n), ALU.mult)
        B.tt("dve", tv[1], sinn, bc(sm, n), ALU.mult)
        B.tt("dve", tv[2], cosn, bc(sm, n), ALU.mult)
        B.tt("dve", tv[3], sinn, bc(cm, n), ALU.mult)
        B.tt("dve", V(CS.t[:, :, 0, n:2 * n], CS[:].keys), tv[0], tv[1], ALU.subtract)
        B.tt("dve", V(CS.t[:, :, 1, n:2 * n], CS[:].keys), tv[2], tv[3], ALU.add)
        c2, s2, cs2 = small("c2_%d" % m), small("s2_%d" % m), small("cs_%d" % m)
        B.tt("dve", c2[:], cm[:], cm[:], ALU.mult)
        B.tt("dve", s2[:], sm[:], sm[:], ALU.mult)
        B.tt("dve", cs2[:], cm[:], sm[:], ALU.mult)
        if m < 6:
            cmn, smn = small("cm_%d" % m), small("sm_%d" % m)
        else:
            cmn, smn = cT, sT
        B.tt("dve", cmn[:], c2[:], s2[:], ALU.subtract)
        B.ts("dve", smn[:], cs2[:], 2.0, None, ALU.mult)
        cm, sm = cmn, smn
    if DEBUG_STOP == "tables":
        B.S.mute = True
    bre = B.tile("l0bre", [128, 16, 16], F32)
    bim = B.tile("l0bim", [128, 16, 16], F32)
    for bt_, nm in ((bre, "ev_s5_b_re"), (bim, "ev_s5_b_im")):
        bsrc = io[nm][0].rearrange("(k g) p c -> (g p) k c", g=2)
        bk = B.fresh()
        for k0 in range(0, 16, 4):
            B.dma("sp", V(bt_.t[:, k0:k0 + 4, :], bt_[:].keys), bsrc[:, k0:k0 + 4, :], bk)
    bbre = B.tile("l0bbre", [128, 16, 16], F32)
    bbim = B.tile("l0bbim", [128, 16, 16], F32)
    tb16 = [V(t.t[:, :, 0:16], t[:].keys) for t in tb]
    B.tt("dve", tb16[0], bre[:], bc(cre, 16), ALU.mult)
    B.tt("dve", tb16[1], bim[:], bc(cim, 16), ALU.mult)
    B.tt("dve", bbre[:], tb16[0], tb16[1], ALU.subtract)
    B.tt("dve", tb16[2], bim[:], bc(cre, 16), ALU.mult)
    B.tt("dve", tb16[3], bre[:], bc(cim, 16), ALU.mult)
    B.tt("dve", bbim[:], tb16[2], tb16[3], ALU.add)
    BD = [B.tile("l0BD%d" % ri, [128, 16, 128], BF16) for ri in range(2)]
    for ri, (bd, bb) in enumerate(zip(BD, (bbre, bbim))):
        B.memset("pool", bd[:], 0.0)
        for k in range(16):
            r = k % 4
            for gg in range(2):
                pr = slice(64 * gg, 64 * gg + 64)
                c0 = 32 * r + 16 * gg
                B.copy("pool" if gg else "dve", V(bd.t[pr, k, c0:c0 + 16], bd[:].keys),
                       V(bb.t[pr, k, :], bb[:].keys))
        for kg in range(4):
            for j in range(4):
                B.transpose(V(pps.t[:, j, :], pps[:].keys), V(bd.t[:, kg * 4 + j, :], bd[:].keys), ident[:])
            B.copy("act", V(BT[ri].t[:, kg * 4:kg * 4 + 4, :], BT[ri][:].keys), pps[:])
    if DEBUG_STOP == "bbar":
        B.S.mute = True
    for ct in CT:
        B.memset("pool", ct[:], 0.0)
    Zs = [B.tile("l0Z%d" % ri, [64, 16, 128], F32) for ri in range(2)]
    for ri, nm in enumerate(("ev_s5_c_re", "ev_s5_c_im")):
        B.memset("pool", Zs[ri][:], 0.0)
        src = io[nm][0].rearrange("(k g) co p -> g co k p", g=2)
        ckey = B.fresh()
        B.dma("sp", V(Zs[ri].t[0:16, :, 0:64], Zs[ri][:].keys), src[0], ckey)
        B.dma("sp", V(Zs[ri].t[32:48, :, 64:128], Zs[ri][:].keys), src[1], ckey)
        for k in range(16):
            r = k % 4
            Zk = Zs[ri]
            B.S.op("pe", (lambda Zk=Zk, k=k: B.nc.tensor.transpose(out=pv32.t[:, 0:64], in_=Zk.t[0:64, k, :],
                                                                  identity=ident32.t[0:64, 0:64])),
                   Zk[:].keys + ident32[:].keys, pv32[:].keys)
            for gg in range(2):
                srcv = V(pv32.t[:, 32 * gg:32 * gg + 16], pv32[:].keys)
                c0 = 32 * r + 16 * gg
                if ri == 0:
                    B.copy("act", V(CT[0].t[:, k, c0:c0 + 16], CT[0][:].keys), srcv)
                    B.ts("dve", V(CT[1].t[:, k, c0:c0 + 16], CT[1][:].keys), srcv, -1.0, None, ALU.mult)
                else:
                    B.ts("dve", V(CT[2].t[:, k, c0:c0 + 16], CT[2][:].keys), srcv, -1.0, None, ALU.mult)
    if DEBUG_STOP == "cmat":
        B.S.mute = True
    cw = B.tile("l0cw", [31, 512], F32)
    B.dma("sp", cw[:], io["ev_conv_w"][0], B.fresh())
    wk = B.tile("l0wk", [128, 4, 31], F32)
    for cc in range(4):
        B.S.op("pe", (lambda cc=cc: B.nc.tensor.transpose(out=pv32.t[:, 0:31], in_=cw.t[0:31, cc * 128:(cc + 1) * 128],
                                                          identity=ident32.t[0:31, 0:31])),
               cw[:].keys + ident32[:].keys, pv32[:].keys)
        B.copy("act", V(wk.t[:, cc, :], wk[:].keys), V(pv32.t[:, 0:31], pv32[:].keys))
    for cc in range(4):
        for k in range(31):
            B.ts("dve" if (k % 2) else "pool", V(DG.t[:, cc, k, :], DG[:].keys), ident[:],
                 V(wk.t[:, cc, k:k + 1], wk[:].keys), None, ALU.mult)
    if DEBUG_STOP in ("loads", "trig", "abar", "tables", "bbar", "cmat"):
        B.S.mute = False
        B.phase_end()
        B.pstack = outer
        B.phase_end()
        return
    B.S.barrier()
    B.pstack.close()
    B.pstack = outer

    if DEBUG_STOP == "prep":
        B.phase_end()
        return
    xts = B.pool("l0xt", 2, [128, NS, D], F32, kaxes=1)
    ss = B.tile("l0ss", [128, NS], F32, kaxes=1)
    rstd = B.tile("l0rstd", [128, NS], F32, kaxes=1)
    hn = B.pool("l0hn", 2, [128, D], BF16)
    hnT = B.tile("l0hnT", [128, 8, NS, 128], BF16, kaxes=2)
    u32 = B.tile("l0u32", [128, 4, T], F32, kaxes=1)
    ubf = B.tile("l0ubf", [128, 4, T], BF16, kaxes=1)
    sga = B.tile("l0sga", [128, 4, T], BF16, kaxes=1)
    sgb = B.tile("l0sgb", [128, 4, T], BF16, kaxes=1)
    vbf = B.tile("l0vbf", [128, 4, 30 + T], BF16, kaxes=1)
    B.memset("pool", vbf[:], 0.0)
    sgl = B.pool("l0sgl", 2, [128, T], F32)
    A4 = B.pool("l0A4", 2, [128, 2, 2, T5], F32)
    nubf = B.tile("l0nubf", [128, 4, T], BF16, kaxes=1)
    rl_all = B.tile("l0rl", [128, 16, 2], F32)
    cr16 = B.tile("l0cr16", [128, 4, 16], F32)
    G = B.pool("l0G", 2, [128, 2, T5], F32)
    R = B.pool("l0R", 2, [128, 2, T5], F32)
    PA = B.pool("l0PA", 2, [128, 2, T5], BF16)
    PB = B.pool("l0PB", 2, [128, 2, T5], BF16)
    y32 = B.pool("l0y32", 1, [128, T], F32)
    z32 = B.tile("l0z32", [128, 4, T], F32, kaxes=1)
    zbf = B.tile("l0zbf", [128, 4, T], BF16, kaxes=1)
    sig = sgl
    c32 = B.tile("l0c32", [128, 4, T], F32, kaxes=1)
    cbf = B.tile("l0cbf", [128, 4, T], BF16, kaxes=1)
    csq = B.tile("l0csq", [128, 4, T], BF16, kaxes=1)
    mus = B.tile("l0mus", [128, T], F32)
    var = B.tile("l0var", [128, T], F32)
    rsd = B.tile("l0rsd", [128, T], F32)
    cn = B.pool("l0cn", 2, [128, T], F32)
    yab = B.tile("l0yab", [128, 8, T], BF16, kaxes=1)
    pb = B.pool("l0pb", 2, [128, 512], F32, space="psum")
    pS = B.pool("l0pS", 2, [128, 2, 2, T5], F32, space="psum")
    yps = B.pool("l0yps", 1, [128, T], F32, space="psum")
    pst = B.pool("l0pst", 2, [128, T], F32, space="psum")
    ptr = B.tile("l0ptr", [128, 4, 128], BF16, space="psum")

    def load(i):
        B.dma("sp", xts[i % 2][:], rows_view(xin, i * T, NS), "ld_x%d" % (i % 2))

    def proj(c):
        p = B.nxt(pb)
        pv = V(p.t[:, 0:T], p[:].keys)
        for kc in range(8):
            B.mm(pv, V(win.t[:, kc, c * 128:(c + 1) * 128], win[:].keys),
                 V(hnT.t[:, kc, :, :], hnT[:, kc].keys), start=(kc == 0), stop=(kc == 7))
        return pv

    load(0)
    for i in range(nt):
        if DEBUG_NT is not None and i >= DEBUG_NT:
            break
        if i + 1 < nt:
            load(i + 1)
        xt = xts[i % 2]
        pre = i < npre
        lastpre = i == npre - 1
        for s in range(NS):
            h = B.nxt(hn)
            rmsnorm_tile(B, xt[:, s, :], ss[:, s:s + 1], rstd[:, s:s + 1], g, h[:], h[:])
            for kg in range(2):
                for j in range(4):
                    kc = kg * 4 + j
                    B.transpose(V(ptr.t[:, j, :], ptr[:].keys), V(h.t[:, kc * 128:(kc + 1) * 128], h[:].keys),
                                ident[:])
                B.copy("act" if kg == 0 else "dve", hnT[:, kg * 4:(kg + 1) * 4, s, :], ptr[:])
        for cc in range(4):
            pv = proj(cc)
            if pre:
                B.copy("act", ubf[:, cc, :], pv)
            else:
                B.copy("act", u32[:, cc, :], pv)
                B.copy("pool", ubf[:, cc, :], u32[:, cc, :])
        if not pre:
            for cc in range(4):
                B.act(sga[:, cc, :], proj(4 + cc), AF.Silu)
        if (not pre) or lastpre:
            for cc in range(4):
                sg = B.nxt(sgl)
                B.act(sg[:], proj(12 + cc), AF.Sigmoid)
                B.tt("dve", V(vbf.t[:, cc, 30:30 + T], vbf[:, cc].keys), proj(8 + cc), sg[:], ALU.mult)
        if not pre:
            for cc in range(4):
                B.act(sgb[:, cc, :], proj(16 + cc), AF.Silu)
        if DEBUG_STOP == "e_proj" and not pre:
            B.S.mute = True
        for cc in range(4):
            B.ts("pool", nubf[:, cc, :], ubf[:, cc, :], -1.0, None, ALU.mult)
        its = [(h5, cc, r) for h5 in range(2) for cc in range(4) for r in range(4)]
        st = [dict() for _ in its]

        def s1(n):
            h5, cc, r = its[n]
            k = cc * 4 + r
            c5 = slice(h5 * T5, (h5 + 1) * T5)
            P = B.nxt(pS)
            uv = V(ubf.t[:, cc, c5], ubf[:, cc].keys)
            nuv = V(nubf.t[:, cc, c5], nubf[:, cc].keys)
            for (x_, y_, ri, rv) in ((0, 0, 0, uv), (0, 1, 1, nuv), (1, 0, 1, uv), (1, 1, 0, uv)):
                B.mm(V(P.t[:, x_, y_, :], P[:].keys), V(BT[ri].t[:, k, :], BT[ri][:].keys), rv)
            csb = V(CS.t[:, k:k + 1, :, :].broadcast_to([128, 2, 2, T5]), CS[:].keys)
            a_ = B.nxt(A4)
            B.tt("dve", a_[:], P[:], csb, ALU.mult)
            st[n]["a"] = a_

        def s2(n):
            a_ = st[n]["a"]
            gt = B.nxt(G)
            B.tt("pool", gt[:], V(a_.t[:, :, 0, :], a_[:].keys), V(a_.t[:, :, 1, :], a_[:].keys), ALU.subtract)
            st[n]["g"] = gt

        def s3(n):
            h5, cc, r = its[n]
            k = cc * 4 + r
            gt = st[n]["g"]
            rt = B.nxt(R)
            magk = V(mag.t[:, k:k + 1].broadcast_to([128, T5]), mag[:].keys)
            for ri in range(2):
                B.scan(V(rt.t[:, ri, :], rt[:].keys), magk, V(gt.t[:, ri, :], gt[:].keys),
                       V(init[ri].t[:, k:k + 1], init[ri][:, k].keys))
            st[n]["r"] = rt
            if not pre:
                pa = B.nxt(PA)
                B.tt("dve", pa[:], rt[:], V(CS.t[:, k, :, :], CS[:].keys), ALU.mult)
                st[n]["pa"] = pa

        def s3b(n):
            h5, cc, r = its[n]
            k = cc * 4 + r
            rt = st[n]["r"]
            B.copy("pool", V(rl_all.t[:, k, :], rl_all[:].keys), V(rt.t[:, :, T5 - 1], rt[:].keys))
            if not pre:
                pb_ = B.nxt(PB)
                B.tt("pool", V(pb_.t[:, 0, :], pb_[:].keys), V(rt.t[:, 0, :], rt[:].keys),
                     V(CS.t[:, k, 1, :], CS[:].keys), ALU.mult)
                B.tt("pool", V(pb_.t[:, 1, :], pb_[:].keys), V(rt.t[:, 1, :], rt[:].keys),
                     V(CS.t[:, k, 0, :], CS[:].keys), ALU.mult)
                st[n]["pb"] = pb_
            if cc == 3 and r == 3:
                rlr = V(rl_all.t[:, :, 0], rl_all[:].keys)
                rli = V(rl_all.t[:, :, 1], rl_all[:].keys)
                crv = [V(cr16.t[:, j_, :], cr16[:].keys) for j_ in range(4)]
                B.tt("pool", crv[0], rli, sT[:], ALU.mult)
                B.tt("pool", crv[1], rli, cT[:], ALU.mult)
                B.tt("pool", crv[2], rlr, cT[:], ALU.mult)
                B.tt("pool", crv[3], rlr, sT[:], ALU.mult)
                B.tt("pool", init[0][:], crv[2], crv[0], ALU.subtract)
                B.tt("pool", init[1][:], crv[3], crv[1], ALU.add)

        def s4(n):
            if pre:
                return
            h5, cc, r = its[n]
            k = cc * 4 + r
            c5 = slice(h5 * T5, (h5 + 1) * T5)
            if r == 0:
                st[n]["yp"] = B.nxt(yps)
            else:
                st[n]["yp"] = st[n - 1]["yp"]
            yp = st[n]["yp"]
            pa, pb_ = st[n]["pa"], st[n]["pb"]
            yv = V(yp.t[:, 0:T5], yp[:].keys)
            B.mm(yv, V(CT[0].t[:, k, :], CT[0][:].keys), V(pa.t[:, 0, :], pa[:].keys), start=(r == 0), stop=False)
            B.mm(yv, V(CT[1].t[:, k, :], CT[1][:].keys), V(pa.t[:, 1, :], pa[:].keys), start=False, stop=False)
            B.mm(yv, V(CT[2].t[:, k, :], CT[2][:].keys), V(pb_.t[:, 0, :], pb_[:].keys), start=False, stop=False)
            B.mm(yv, V(CT[2].t[:, k, :], CT[2][:].keys), V(pb_.t[:, 1, :], pb_[:].keys), start=False,
                 stop=(r == 3))
            if r == 3:
                y = B.nxt(y32)
                yh = V(y.t[:, 0:T5], y[:].keys)
                B.stt("dve", yh, V(u32.t[:, cc, c5], u32[:, cc].keys), V(Dt.t[:, cc:cc + 1], Dt[:].keys),
                      yv, ALU.mult, ALU.add)
                B.act(V(z32.t[:, cc, c5], z32[:, cc].keys), yh, AF.Gelu_apprx_tanh)
                B.copy("pool", V(zbf.t[:, cc, c5], zbf[:, cc].keys), V(z32.t[:, cc, c5], z32[:, cc].keys))

        NI = len(its)
        for t in range(NI + 4):
            if t < NI:
                s1(t)
            if 0 <= t - 3 < NI:
                s3b(t - 3)
            if 0 <= t - 4 < NI:
                s4(t - 4)
            if 0 <= t - 1 < NI:
                s2(t - 1)
            if 0 <= t - 2 < NI:
                s3(t - 2)
        if pre:
            if lastpre:
                B.copy("pool", V(vbf.t[:, :, 0:30], vbf[:].keys), V(vbf.t[:, :, T:T + 30], vbf[:].keys))
            continue
        if DEBUG_STOP == "e_s5" and not pre:
            B.S.mute = True
        for co in range(4):
            p = B.nxt(pb)
            pv = V(p.t[:, 0:T], p[:].keys)
            for ci in range(4):
                B.mm(pv, V(gluw.t[:, ci, co * 128:(co + 1) * 128], gluw[:].keys), zbf[:, ci, :],
                     start=(ci == 0), stop=(ci == 3))
            sg = B.nxt(sig)
            B.act(sg[:], pv, AF.Sigmoid, bias=V(glub.t[:, co:co + 1], glub[:].keys))
            B.tt("pool", sg[:], sg[:], sga[:, co, :], ALU.mult)
            B.tt("dve", yab[:, co, :], z32[:, co, :], sg[:], ALU.mult)
        if DEBUG_STOP == "e_glu" and not pre:
            B.S.mute = True
        for cc in range(4):
            p = B.nxt(pb)
            pv = V(p.t[:, 0:T], p[:].keys)
            for k in range(31):
                B.mm(pv, V(DG.t[:, cc, k, :], DG[:].keys), V(vbf.t[:, cc, k:k + T], vbf[:, cc].keys),
                     start=(k == 0), stop=(k == 30))
            bcc = V(convb.t[:, cc:cc + 1], convb[:].keys)
            B.act(c32[:, cc, :], pv, AF.Identity, bias=bcc)
            B.act(cbf[:, cc, :], pv, AF.Identity, bias=bcc)
            B.act(csq[:, cc, :], pv, AF.Square, bias=bcc)
        B.copy("pool", V(vbf.t[:, :, 0:30], vbf[:].keys), V(vbf.t[:, :, T:T + 30], vbf[:].keys))
        if DEBUG_STOP == "e_conv" and not pre:
            B.S.mute = True
        pmu, pm2 = B.nxt(pst), B.nxt(pst)
        for cc in range(4):
            B.mm(pmu[:], onesM[:], cbf[:, cc, :], start=(cc == 0), stop=(cc == 3))
        for cc in range(4):
            B.mm(pm2[:], onesM[:], csq[:, cc, :], start=(cc == 0), stop=(cc == 3))
        B.copy("act", mus[:], pmu[:])
        B.tt("dve", var[:], mus[:], mus[:], ALU.mult)
        B.tt("dve", var[:], pm2[:], var[:], ALU.subtract)
        B.act(rsd[:], var[:], AF.Sqrt, bias=EPS)
        B.recip(rsd[:], rsd[:])
        for cc in range(4):
            c_ = B.nxt(cn)
            B.tt("dve", c_[:], c32[:, cc, :], mus[:], ALU.subtract)
            B.tt("pool", c_[:], c_[:], rsd[:], ALU.mult)
            B.act(c_[:], c_[:], AF.Silu, bias=V(lnb.t[:, cc:cc + 1], lnb[:].keys),
                  scale=V(lng.t[:, cc:cc + 1], lng[:].keys))
            B.tt("dve", yab[:, 4 + cc, :], c_[:], sgb[:, cc, :], ALU.mult)
        if DEBUG_STOP == "e_ln" and not pre:
            B.S.mute = True
        for s in range(NS):
            for n in range(2):
                p = B.nxt(pb)
                for kc in range(8):
                    B.mm(p[:], V(yab.t[:, kc, s * 128:(s + 1) * 128], yab[:, kc].keys),
                         V(wo.t[:, kc, n * 512:(n + 1) * 512], wo[:].keys), start=(kc == 0), stop=(kc == 7))
                xv = V(xt.t[:, s, n * 512:(n + 1) * 512], xt[:, s].keys)
                B.tt("dve", xv, xv, p[:], ALU.add)
        if DEBUG_STOP != "nostore":
            B.dma("pool" if DEBUG_STOP == "poolstore" else "sp", rows_view(xout, (i - npre) * T, NS), xt[:],
                  "st_o%d" % (i % 2))
    B.S.mute = False
    B.phase_end()


W_NAMES = ["mem_norm_g", "ev_norm_g", "ev_w_in", "ev_s5_lambda_re", "ev_s5_lambda_im", "ev_s5_log_dt",
           "ev_s5_b_re", "ev_s5_b_im", "ev_s5_c_re", "ev_s5_c_im", "ev_s5_d", "ev_s5_glu_w",
           "ev_s5_glu_b", "ev_conv_w", "ev_conv_b", "ev_conv_ln_g", "ev_conv_ln_b", "ev_w_out",
           "od_norm_g", "od_w_in", "od_rel_bias", "od_w_out", "xa_norm_g", "xa_w_qg", "xa_w_kv",
           "xa_w_o", "final_norm_g"]


def setup_consts(B, io):
    C = {}
    ident = B.tile("ident", [128, 128], BF16, persistent=True)
    B.dma("pool", ident[:], io["c_ident"][:, :], "ld_ident")
    ones = B.tile("ones", [128, 128], BF16, persistent=True)
    B.memset("pool", ones[:], 1.0)
    C["ident"] = ident
    C["ones"] = ones
    halfpi = B.tile("halfpi", [128, 1], F32, persistent=True)
    B.memset("pool", halfpi[:], 1.5707963267948966)
    C["halfpi"] = halfpi
    return C


def build_program(mode="full"):
    nc = bass.Bass("TRN2", target_bir_lowering=False)
    io = {}

    def din(name, shape):
        io[name] = nc.dram_tensor(name, list(shape), F32, kind="ExternalInput").ap()

    def dout(name, shape):
        io[name] = nc.dram_tensor(name, list(shape), F32, kind="ExternalOutput").ap()

    def dint(name, shape):
        io[name] = nc.dram_tensor(name, list(shape), F32, kind="Internal").ap()

    din("c_ident", [128, 128])
    din("mem", [MEM, D])
    din("mem_norm_g", [1, D])
    din("xa_norm_g", [2, D])
    din("xa_w_qg", [2, D, 2048])
    din("xa_w_kv", [2, D, 2048])
    din("xa_w_o", [2, D, D])
    din("final_norm_g", [1, D])
    B = Builder(nc)
    outs = []
    if mode == "xa_test":
        din("xin", [OWN, D])
        dout("out", [OWN, D])
        C = setup_consts(B, io)
        phase_xa(B, C, io, 1, io["xin"], io["out"], OWN, final=True)
        outs = ["st_o0", "st_o1"]
    if mode == "l0_test":
        din("xin", [PRE + EXT, D])
        dout("out", [EXT, D])
        for nm, shp in (("ev_norm_g", [1, D]), ("ev_w_in", [1, D, 2560]), ("ev_s5_lambda_re", [1, 32, 64]),
                        ("ev_s5_lambda_im", [1, 32, 64]), ("ev_s5_log_dt", [1, 32]),
                        ("ev_s5_b_re", [1, 32, 64, 16]), ("ev_s5_b_im", [1, 32, 64, 16]),
                        ("ev_s5_c_re", [1, 32, 16, 64]), ("ev_s5_c_im", [1, 32, 16, 64]),
                        ("ev_s5_d", [1, 512]), ("ev_s5_glu_w", [1, 512, 512]), ("ev_s5_glu_b", [1, 512]),
                        ("ev_conv_w", [1, 31, 512]), ("ev_conv_b", [1, 512]), ("ev_conv_ln_g", [1, 512]),
                        ("ev_conv_ln_b", [1, 512]), ("ev_w_out", [1, D, D])):
            din(nm, shp)
        C = setup_consts(B, io)
        phase_l0(B, C, io, io["xin"], io["out"])
        outs = [k for k in ("st_o0", "st_o1") if k in B.S.dsem]
    if mode == "l1_test":
        din("xin", [EXT, D])
        dout("out", [OWN, D])
        din("od_norm_g", [1, D])
        din("od_w_in", [1, D, 4096])
        din("od_w_out", [1, D, D])
        din("c_bias", [80, 128, 128])
        din("c_hmask", [128, 128])
        C = setup_consts(B, io)
        phase_l1(B, C, io, io["xin"], io["out"])
        outs = ["st_o0", "st_o1"]
    if mode == "full":
        din("x_ext", [PRE + EXT, D])
        dout("out", [OWN, D])
        dint("xa", [EXT, D])
        dint("xb", [EXT, D])
        dint("xc", [OWN, D])
        for nm, shp in (("ev_norm_g", [1, D]), ("ev_w_in", [1, D, 2560]), ("ev_s5_lambda_re", [1, 32, 64]),
                        ("ev_s5_lambda_im", [1, 32, 64]), ("ev_s5_log_dt", [1, 32]),
                        ("ev_s5_b_re", [1, 32, 64, 16]), ("ev_s5_b_im", [1, 32, 64, 16]),
                        ("ev_s5_c_re", [1, 32, 16, 64]), ("ev_s5_c_im", [1, 32, 16, 64]),
                        ("ev_s5_d", [1, 512]), ("ev_s5_glu_w", [1, 512, 512]), ("ev_s5_glu_b", [1, 512]),
                        ("ev_conv_w", [1, 31, 512]), ("ev_conv_b", [1, 512]), ("ev_conv_ln_g", [1, 512]),
                        ("ev_conv_ln_b", [1, 512]), ("ev_w_out", [1, D, D]),
                        ("od_norm_g", [1, D]), ("od_w_in", [1, D, 4096]), ("od_w_out", [1, D, D]),
                        ("c_bias", [80, 128, 128]), ("c_hmask", [128, 128])):
            din(nm, shp)
        C = setup_consts(B, io)
        phase_l0(B, C, io, io["x_ext"], io["xa"])
        phase_xa(B, C, io, 0, io["xa"], io["xb"], EXT, final=False)
        phase_l1(B, C, io, io["xb"], io["xc"])
        phase_xa(B, C, io, 1, io["xc"], io["out"], OWN, final=True)
        outs = ["st_o0", "st_o1"]
    B.S.flush(final_wait_keys=outs)
    B.gstack.close()
    return nc


def host_bias_tiles(rel_bias):
    k = np.arange(128)[:, None]
    q = np.arange(128)[None, :]
    out = np.empty((16, 5, 128, 128), np.float32)
    for dl in range(5):
        rel = np.clip(q - k + 128 * dl, -128, 128) + 128
        kb = (k >= 64).astype(np.int64)
        qb = (q >= 64).astype(np.int64)
        dc = -2 * dl + kb - qb
        masked = (dc > 0) | (dc < -8)
        for h in range(16):
            out[h, dl] = np.where(masked, np.float32(NEG), rel_bias[h][rel])
    return out.reshape(80, 128, 128)


_PROGRAM = None


def kernel(**inputs):
    global _PROGRAM
    if _PROGRAM is None:
        _PROGRAM = build_program("full")
    nc = _PROGRAM
    f32 = np.float32
    x = np.asarray(inputs["x"], f32)
    mem = np.asarray(inputs["mem"], f32)
    shared = {"c_ident": np.eye(128, dtype=f32),
              "c_bias": host_bias_tiles(np.asarray(inputs["od_rel_bias"], f32)[0]),
              "mem_norm_g": np.asarray(inputs["mem_norm_g"], f32)[None, :],
              "final_norm_g": np.asarray(inputs["final_norm_g"], f32)[None, :]}
    for k in W_NAMES:
        if k not in shared and k != "od_rel_bias":
            shared[k] = np.ascontiguousarray(np.asarray(inputs[k], f32))
    in_maps = []
    for c in range(NCORES):
        b, hh = divmod(c, 2)
        m = dict(shared)
        if hh == 0:
            m["x_ext"] = np.concatenate([np.zeros((OWN, D), f32), x[b, :OWN]], 0)
            m["c_hmask"] = np.full((128, 128), NEG, f32)
        else:
            m["x_ext"] = np.ascontiguousarray(x[b])
            m["c_hmask"] = np.zeros((128, 128), f32)
        m["mem"] = np.ascontiguousarray(mem[b])
        in_maps.append(m)
    res = run_bass_kernel_spmd(nc, in_maps, core_ids=list(range(NCORES)))
    out = np.empty((NB, SEQ, D), f32)
    for c in range(NCORES):
        b, hh = divmod(c, 2)
        out[b, hh * OWN:(hh + 1) * OWN] = res.results[c]["out"]
    return out
```

```python
import bisect
import contextlib
import itertools
import numpy as np
import concourse.bass as bass
import concourse.mybir as mybir
from concourse.bass_utils import run_bass_kernel_spmd

F32 = mybir.dt.float32
BF16 = mybir.dt.bfloat16
I32 = mybir.dt.int32
AF = mybir.ActivationFunctionType
ALU = mybir.AluOpType

D = 1024
SEQ = 8192
NB = 4
NCORES = 8
OWN = 4096
HALO = 512
EXT = OWN + HALO
PRE = SEQ // 2 - HALO
MEM = 256
EPS = 1e-6
NEG = -30000.0
RAW_ONLY_SAME_ENGINE = False


class _Op:
    __slots__ = ("eng", "fn", "reads", "writes", "dma", "deps", "sig", "tick", "idx", "dcount",
                 "after")


class Sched:
    NDSEM = 40

    def __init__(self, nc, stack, same_engine_sync=True):
        self.nc = nc
        self.engs = {"pe": nc.tensor, "act": nc.scalar, "dve": nc.vector,
                     "pool": nc.gpsimd, "sp": nc.sync}
        self.same = same_engine_sync
        self.ops = []
        self.flushed = 0
        self.last_write = {}
        self.readers = {}
        self.tick = {e: 0 for e in self.engs}
        self.dcount = {}
        self.waited = {e: {} for e in self.engs}
        self.esem = {e: stack.enter_context(nc.semaphore("sem_" + e)) for e in self.engs}
        self.dsem_pool = [stack.enter_context(nc.semaphore("dsem_%d" % i))
                          for i in range(self.NDSEM)]
        self.dsem = {}
        self.last_eng_op = {}
        self.last_dma_op = {}
        self.n_wait = 0
        self.sigidx = {e: [] for e in self.engs}

    mute = False

    def op(self, eng, fn, reads=(), writes=(), dma=None, after=()):
        if self.mute:
            return None
        o = _Op()
        o.eng = eng
        o.fn = fn
        o.reads = tuple(reads)
        o.writes = tuple(writes)
        o.dma = dma
        o.sig = False
        o.after = tuple(after)
        o.idx = len(self.ops)
        self.ops.append(o)
        return o

    def barrier(self):
        self.flush()
        after = [o for o in self.last_eng_op.values()] + [o for o in self.last_dma_op.values()]
        sp = self.engs["sp"]
        b = self.op("sp", lambda: sp.nop(), after=after)
        for e in ("pe", "act", "dve", "pool"):
            en = self.engs[e]
            self.op(e, (lambda en=en: en.nop()), after=[b])
        self.flush()

    def flush(self, final_wait_keys=()):
        ops = self.ops
        new = ops[self.flushed:]
        for o in new:
            deps = set()
            raw = set()
            for k in o.reads:
                if k in self.last_write:
                    deps.add(self.last_write[k])
                    raw.add(self.last_write[k])
            for k in o.writes:
                if k in self.last_write:
                    deps.add(self.last_write[k])
                for r in self.readers.get(k, ()):
                    deps.add(r)
            for a in o.after:
                deps.add(a.idx)
                raw.add(a.idx)
            deps.discard(o.idx)
            if RAW_ONLY_SAME_ENGINE:
                deps = {d for d in deps if ops[d].eng != o.eng or ops[d].dma is not None or d in raw}
            for k in o.reads:
                self.readers.setdefault(k, []).append(o.idx)
            for k in o.writes:
                self.last_write[k] = o.idx
                self.readers[k] = []
            best = {}
            dd = []
            for d in deps:
                od = ops[d]
                if od.dma is not None:
                    dd.append(d)
                    continue
                if od.eng == o.eng and (od.eng in ("pe", "sp") or not self.same):
                    continue
                if od.eng not in best or best[od.eng] < d:
                    best[od.eng] = d
            o.deps = dd
            for d in best.values():
                od = ops[d]
                if not od.sig:
                    if d >= self.flushed:
                        od.sig = True
                    else:
                        lst = self.sigidx[od.eng]
                        j = bisect.bisect_left(lst, d)
                        d = lst[j]
                o.deps.append(d)
            if o.dma is None:
                self.last_eng_op[o.eng] = o
            else:
                self.last_dma_op[o.dma] = o
        for e, o in self.last_eng_op.items():
            if o.idx >= self.flushed:
                o.sig = True
        for o in new:
            if o.dma is not None:
                if o.dma not in self.dsem:
                    self.dsem[o.dma] = self.dsem_pool[len(self.dsem)]
                self.dcount[o.dma] = self.dcount.get(o.dma, 0) + 1
                o.dcount = self.dcount[o.dma]
            elif o.sig:
                self.tick[o.eng] += 1
                o.tick = self.tick[o.eng]
                self.sigidx[o.eng].append(o.idx)
        for o in new:
            e = self.engs[o.eng]
            w = self.waited[o.eng]
            need = {}
            for d in o.deps:
                od = ops[d]
                if od.dma is not None:
                    s = self.dsem[od.dma]
                    v = 16 * od.dcount
                else:
                    s = self.esem[od.eng]
                    v = od.tick
                if need.get(s, 0) < v:
                    need[s] = v
            for s, v in need.items():
                if w.get(s, 0) < v:
                    e.wait_ge(s, v)
                    w[s] = v
                    self.n_wait += 1
            inst = o.fn()
            if o.dma is not None:
                inst.then_inc(self.dsem[o.dma], 16)
            elif o.sig:
                inst.then_inc(self.esem[o.eng], 1)
            o.fn = None
        self.flushed = len(ops)
        sp = self.engs["sp"]
        for k in final_wait_keys:
            sp.wait_ge(self.dsem[k], 16 * self.dcount[k])


class V:
    __slots__ = ("ap", "keys")

    def __init__(self, ap, keys):
        self.ap = ap
        self.keys = tuple(keys)


class Tile:
    def __init__(self, name, handle, shape, kaxes):
        self.name = name
        self.t = handle
        self.shape = list(shape)
        self.kaxes = kaxes

    def __getitem__(self, idx):
        if not isinstance(idx, tuple):
            idx = (idx,)
        ranges = []
        for a in range(self.kaxes):
            i = idx[1 + a] if len(idx) > 1 + a else slice(None)
            n = self.shape[1 + a]
            if isinstance(i, int):
                ranges.append((i,))
            else:
                ranges.append(tuple(range(*i.indices(n))))
        keys = [(self.name,) + c for c in itertools.product(*ranges)]
        return V(self.t[idx], keys)


def _k(x):
    return x.keys if isinstance(x, V) else ()


def _a(x):
    return x.ap if isinstance(x, V) else x


class Builder:
    def __init__(self, nc):
        self.nc = nc
        self.gstack = contextlib.ExitStack()
        self.S = Sched(nc, self.gstack)
        self.pstack = None
        self.uid = 0
        self.nkey = 0
        self.rr = {}

    def phase_begin(self):
        self.pstack = contextlib.ExitStack()
        self.nkey = 0

    def fresh(self):
        self.nkey += 1
        return "u%d" % self.nkey

    def phase_end(self):
        self.S.barrier()
        self.pstack.close()
        self.pstack = None

    def tile(self, name, shape, dtype=F32, kaxes=0, space="sbuf", persistent=False):
        self.uid += 1
        nm = "%s_%d" % (name, self.uid)
        st = self.gstack if persistent else self.pstack
        if space == "sbuf":
            h = st.enter_context(self.nc.sbuf_tensor(nm, list(shape), dtype))
        else:
            h = st.enter_context(self.nc.psum_tensor(nm, list(shape), dtype))
        return Tile(nm, h, shape, kaxes)

    def pool(self, name, n, shape, dtype=F32, kaxes=0, space="sbuf"):
        tiles = [self.tile("%s%d" % (name, i), shape, dtype, kaxes, space) for i in range(n)]
        self.rr[id(tiles)] = 0
        return tiles

    def nxt(self, tiles):
        i = self.rr[id(tiles)]
        self.rr[id(tiles)] = (i + 1) % len(tiles)
        return tiles[i]

    def mm(self, out, lhsT, rhs, start=True, stop=True, **kw):
        nc = self.nc
        rd = lhsT.keys + rhs.keys + (() if start else out.keys)
        self.S.op("pe", lambda: nc.tensor.matmul(out.ap, lhsT=lhsT.ap, rhs=rhs.ap, start=start,
                                                 stop=stop, **kw), rd, out.keys)

    def transpose(self, out, in_, ident):
        nc = self.nc
        self.S.op("pe", lambda: nc.tensor.transpose(out=out.ap, in_=in_.ap, identity=ident.ap),
                  in_.keys + ident.keys, out.keys)

    def act(self, out, in_, func, bias=None, scale=None, accum=None):
        nc = self.nc
        kw = {}
        if bias is not None:
            kw["bias"] = _a(bias)
        if scale is not None:
            kw["scale"] = _a(scale)
        if accum is not None:
            kw["accum_out"] = accum.ap
        self.S.op("act", lambda: nc.scalar.activation(out=out.ap, in_=in_.ap, func=func, **kw),
                  in_.keys + _k(bias) + _k(scale), out.keys + _k(accum))

    def _ve(self, eng):
        return self.nc.vector if eng == "dve" else self.nc.gpsimd

    def tt(self, eng, out, in0, in1, op):
        e = self._ve(eng)
        self.S.op(eng, lambda: e.tensor_tensor(out=out.ap, in0=in0.ap, in1=in1.ap, op=op),
                  in0.keys + in1.keys, out.keys)

    def ts(self, eng, out, in0, s1, s2=None, op0=ALU.mult, op1=None):
        e = self._ve(eng)
        kw = {}
        if op1 is not None:
            kw["op1"] = op1
        self.S.op(eng, lambda: e.tensor_scalar(out=out.ap, in0=in0.ap, scalar1=_a(s1),
                                               scalar2=_a(s2), op0=op0, **kw),
                  in0.keys + _k(s1) + _k(s2), out.keys)

    def stt(self, eng, out, in0, scalar, in1, op0, op1):
        e = self._ve(eng)
        self.S.op(eng, lambda: e.scalar_tensor_tensor(out=out.ap, in0=in0.ap, scalar=_a(scalar),
                                                      in1=in1.ap, op0=op0, op1=op1),
                  in0.keys + _k(scalar) + in1.keys, out.keys)

    def copy(self, eng, out, in_):
        if eng == "act":
            nc = self.nc
            self.S.op("act", lambda: nc.scalar.copy(out=out.ap, in_=in_.ap), in_.keys, out.keys)
        else:
            e = self._ve(eng)
            self.S.op(eng, lambda: e.tensor_copy(out=out.ap, in_=in_.ap), in_.keys, out.keys)

    def memset(self, eng, out, val):
        e = self._ve(eng)
        self.S.op(eng, lambda: e.memset(out.ap, val), (), out.keys)

    def recip(self, out, in_):
        nc = self.nc
        self.S.op("dve", lambda: nc.vector.reciprocal(out=out.ap, in_=in_.ap), in_.keys, out.keys)

    def scan(self, out, d0, d1, init, op0=ALU.mult, op1=ALU.add):
        nc = self.nc
        self.S.op("dve", lambda: nc.vector.tensor_tensor_scan(out=out.ap, data0=d0.ap, data1=d1.ap,
                                                              initial=_a(init), op0=op0, op1=op1),
                  d0.keys + d1.keys + _k(init), out.keys)

    def dma(self, eng, out, in_, key, **kw):
        e = self.S.engs[eng]
        self.S.op(eng, lambda: e.dma_start(out=_a(out), in_=_a(in_), **kw), _k(in_), _k(out),
                  dma=key)


def rows_view(dram, r0, nsub):
    return dram[r0:r0 + 128 * nsub, :].rearrange("(s p) d -> p s d", p=128)


def load_w_bf16(B, name, dram, kdim, ndim, key, c0=0, eng="pool"):
    kc = kdim // 128
    t = B.tile(name, [128, kc, ndim], BF16, kaxes=0)
    src = dram[:, c0:c0 + ndim].rearrange("(k p) n -> p k n", p=128)
    half = max(1, kc // 2)
    for i in range(0, kc, half):
        B.dma(eng, V(t.t[:, i:i + half, :], t[:].keys), src[:, i:i + half, :], key)
    return t


def bcast_row(B, name, dram_row, n, key, persistent=False):
    t = B.tile(name, [128, n], F32, persistent=persistent)
    B.dma("sp", t[:], dram_row.broadcast_to([128, n]), key)
    return t


def rmsnorm_tile(B, xt_v, ss_v, rstd_v, g_t, out_v, sq_scratch):
    B.act(sq_scratch, xt_v, AF.Square, accum=ss_v)
    B.act(rstd_v, ss_v, AF.Sqrt, bias=EPS, scale=1.0 / D)
    B.recip(rstd_v, rstd_v)
    B.stt("dve", out_v, xt_v, rstd_v, g_t[:], ALU.mult, ALU.mult)


def mem_kv(B, C, io, l, kt, vv):
    outer = B.pstack
    B.pstack = contextlib.ExitStack()
    ident = C["ident"]
    g = bcast_row(B, "memg", io["mem_norm_g"], D, B.fresh())
    mt = B.tile("memx", [128, 2, D], F32, kaxes=1)
    B.dma("sp", mt[:], rows_view(io["mem"], 0, 2), "ld_x0")
    sq = B.tile("memsq", [128, D], F32)
    ss = B.tile("memss", [128, 2], F32, kaxes=1)
    rstd = B.tile("memrs", [128, 2], F32, kaxes=1)
    mn = B.tile("memn", [128, 2, D], BF16, kaxes=1)
    memT = B.tile("memT", [128, 8, 2, 128], BF16, kaxes=2)
    pbig = B.pool("mpb", 4, [128, 512], F32, space="psum")
    ptr = B.pool("mptr", 2, [128, 4, 128], BF16, space="psum")
    for s in range(2):
        rmsnorm_tile(B, mt[:, s, :], ss[:, s:s + 1], rstd[:, s:s + 1], g, mn[:, s, :], sq[:])
        for kg in range(2):
            p = B.nxt(ptr)
            for j in range(4):
                kc = kg * 4 + j
                B.transpose(V(p.t[:, j, :], p[:].keys), mn[:, s, kc * 128:(kc + 1) * 128], ident[:])
            B.copy("act" if kg == 0 else "dve", memT[:, kg * 4:(kg + 1) * 4, s, :], p[:])
    wkv = load_w_bf16(B, "wkv%d" % l, io["xa_w_kv"][l], D, 2048, B.fresh())
    for j in range(8):
        p = B.nxt(pbig)
        for kc in range(8):
            B.mm(V(p.t[:, 0:256], p[:].keys), V(wkv.t[:, kc, j * 128:(j + 1) * 128], wkv[:].keys),
                 V(memT.t[:, kc, :, :], memT[:, kc].keys), start=(kc == 0), stop=(kc == 7))
        B.copy("act" if j % 2 == 0 else "dve", kt[:, j, :], V(p.t[:, 0:256], p[:].keys))
    for m in range(2):
        for n in range(2):
            p = B.nxt(pbig)
            for kc in range(8):
                B.mm(p[:], memT[:, kc, m, :],
                     V(wkv.t[:, kc, 1024 + n * 512:1024 + (n + 1) * 512], wkv[:].keys),
                     start=(kc == 0), stop=(kc == 7))
            B.copy("act" if n == 0 else "dve", V(vv.t[:, m, n * 512:(n + 1) * 512], vv[:, m].keys),
                   p[:])
    B.S.barrier()
    B.pstack.close()
    B.pstack = outer


def phase_xa(B, C, io, l, xin, xout, ntok, final):
    T = 512
    NS = T // 128
    nt = ntok // T
    B.phase_begin()
    ident, ones = C["ident"], C["ones"]
    KTl = B.tile("KT%d" % l, [128, 8, 256], BF16, kaxes=1)
    VVl = B.tile("VV%d" % l, [128, 2, 1024], BF16, kaxes=1)
    mem_kv(B, C, io, l, KTl, VVl)
    g = bcast_row(B, "xag", io["xa_norm_g"][l:l + 1, :], D, B.fresh())
    gf = bcast_row(B, "fng", io["final_norm_g"], D, B.fresh()) if final else None
    wqg = load_w_bf16(B, "wqg", io["xa_w_qg"][l], D, 2048, B.fresh())
    wo = load_w_bf16(B, "wo", io["xa_w_o"][l], D, D, B.fresh())
    xts = B.pool("xt", 2, [128, NS, D], F32, kaxes=1)
    sq = B.tile("sq", [128, D], F32)
    ss = B.tile("ss", [128, 2 * NS], F32, kaxes=1)
    rstd = B.tile("rstd", [128, 2 * NS], F32, kaxes=1)
    hn = B.pool("hn", 2, [128, D], BF16)
    hnT = B.tile("hnT", [128, 8, NS, 128], BF16, kaxes=2)
    qT = B.tile("qT", [128, 8, T], BF16, kaxes=1)
    sgT = B.tile("sgT", [128, 8, T], BF16, kaxes=1)
    Et = B.pool("E", 4, [128, T], BF16)
    rs = B.pool("rs", 2, [128, T], F32)
    wt = B.pool("wt", 2, [128, T], F32)
    ogT = B.tile("ogT", [128, 8, T], BF16, kaxes=1)
    pbig = B.pool("pb", 6, [128, 512], F32, space="psum")
    ptr = B.pool("ptr", 2, [128, 4, 128], BF16, space="psum")

    def load(i):
        xt = xts[i % 2]
        B.dma("sp", xt[:], rows_view(xin, i * T, NS), "ld_x%d" % (i % 2))

    load(0)
    for i in range(nt):
        if i + 1 < nt:
            load(i + 1)
        xt = xts[i % 2]
        for s in range(NS):
            h = B.nxt(hn)
            rmsnorm_tile(B, xt[:, s, :], ss[:, s:s + 1], rstd[:, s:s + 1], g, h[:], sq[:])
            for kg in range(2):
                p = B.nxt(ptr)
                for j in range(4):
                    kc = kg * 4 + j
                    B.transpose(V(p.t[:, j, :], p[:].keys), V(h.t[:, kc * 128:(kc + 1) * 128], h[:].keys),
                                ident[:])
                B.copy("act" if kg == 0 else "dve", hnT[:, kg * 4:(kg + 1) * 4, s, :], p[:])
        for j in range(16):
            p = B.nxt(pbig)
            for kc in range(8):
                B.mm(p[:], V(wqg.t[:, kc, j * 128:(j + 1) * 128], wqg[:].keys),
                     V(hnT.t[:, kc, :, :], hnT[:, kc].keys), start=(kc == 0), stop=(kc == 7))
            if j < 8:
                B.ts("dve", qT[:, j, :], p[:], 1.0 / 16.0, None, ALU.mult)
            else:
                B.act(sgT[:, j - 8, :], p[:], AF.Silu)
        for hh in range(4):
            es = []
            for m in range(2):
                p = B.nxt(pbig)
                for dc in range(2):
                    B.mm(p[:], V(KTl.t[:, 2 * hh + dc, m * 128:(m + 1) * 128], KTl[:, 2 * hh + dc].keys),
                         qT[:, 2 * hh + dc, :], start=(dc == 0), stop=(dc == 1))
                e = B.nxt(Et)
                B.act(e[:], p[:], AF.Exp)
                es.append(e)
            psum = B.nxt(pbig)
            for m in range(2):
                B.mm(psum[:], ones[:], es[m][:], start=(m == 0), stop=(m == 1))
            r = B.nxt(rs)
            B.recip(r[:], psum[:])
            for dc in range(2):
                p = B.nxt(pbig)
                for m in range(2):
                    c0 = hh * 256 + dc * 128
                    B.mm(p[:], V(VVl.t[:, m, c0:c0 + 128], VVl[:, m].keys), es[m][:],
                         start=(m == 0), stop=(m == 1))
                w = B.nxt(wt)
                B.tt("pool", w[:], r[:], sgT[:, 2 * hh + dc, :], ALU.mult)
                B.tt("dve", ogT[:, 2 * hh + dc, :], p[:], w[:], ALU.mult)
        for s in range(NS):
            for n in range(2):
                p = B.nxt(pbig)
                for kc in range(8):
                    B.mm(p[:], V(ogT.t[:, kc, s * 128:(s + 1) * 128], ogT[:, kc].keys),
                         V(wo.t[:, kc, n * 512:(n + 1) * 512], wo[:].keys), start=(kc == 0), stop=(kc == 7))
                xv = V(xt.t[:, s, n * 512:(n + 1) * 512], xt[:, s].keys)
                B.tt("dve", xv, xv, p[:], ALU.add)
            if final:
                rmsnorm_tile(B, xt[:, s, :], ss[:, NS + s:NS + s + 1], rstd[:, NS + s:NS + s + 1], gf,
                             xt[:, s, :], sq[:])
        B.dma("sp", rows_view(xout, i * T, NS), xt[:], "st_o%d" % (i % 2))
    B.phase_end()


def phase_l1(B, C, io, xin, xout):
    T = 256
    NS = 2
    nt = EXT // T
    NH = HALO // T
    B.phase_begin()
    ident, ones = C["ident"], C["ones"]
    g = bcast_row(B, "l1g", io["od_norm_g"], D, B.fresh())
    win = load_w_bf16(B, "l1win", io["od_w_in"][0], D, 4096, B.fresh())
    wo = load_w_bf16(B, "l1wo", io["od_w_out"][0], D, D, B.fresh())
    bias = B.tile("l1bias", [128, 80, 128], BF16)
    bsrc = io["c_bias"].rearrange("t k q -> k t q")
    bkey = B.fresh()
    for t0 in range(0, 80, 8):
        B.dma("pool", V(bias.t[:, t0:t0 + 8, :], bias[:].keys), bsrc[:, t0:t0 + 8, :], bkey)
    hmask = B.tile("l1hm", [128, 128], BF16)
    B.dma("pool", hmask[:], io["c_hmask"][:, :], B.fresh())
    xts = B.pool("l1xt", 2, [128, NS, D], F32, kaxes=1)
    sq = B.tile("l1sq", [128, D], F32)
    ss = B.tile("l1ss", [128, NS], F32, kaxes=1)
    rstd = B.tile("l1rstd", [128, NS], F32, kaxes=1)
    hn = B.pool("l1hn", 2, [128, D], BF16)
    hnT = B.tile("l1hnT", [128, 8, NS, 128], BF16, kaxes=2)
    KTr = B.tile("l1KT", [128, 8, 8, 128], BF16, kaxes=2)
    Vr = B.tile("l1V", [128, 8, D], BF16, kaxes=1)
    qTA = B.tile("l1qA", [128, 8, T], BF16, kaxes=1)
    qTB = B.tile("l1qB", [128, 8, T], BF16, kaxes=1)
    sgT = B.tile("l1sg", [128, 8, T], BF16, kaxes=1)
    ogT = B.tile("l1og", [128, 8, T], BF16, kaxes=1)
    Ep = B.pool("l1E", 4, [128, 640], BF16)
    rp = B.pool("l1r", 2, [128, 256], F32)
    wp = B.pool("l1w", 2, [128, 128], F32)
    Sall = B.tile("l1S", [128, 2, 1024], F32, kaxes=1, space="psum")
    scnt = [0]
    pb = B.pool("l1pb", 2, [128, 512], F32, space="psum")
    OSp = B.pool("l1OS", 1, [128, 512], F32, space="psum")
    ptr2 = B.tile("l1ptr", [128, 1, 4, 128], BF16, kaxes=1, space="psum")
    pcnt = [0]
    B.memset("pool", qTA[:], 0.0)
    B.memset("pool", qTB[:], 0.0)

    def load(i):
        B.dma("sp", xts[i % 2][:], rows_view(xin, i * T, NS), "ld_x%d" % (i % 2))

    load(0)
    cnt = 0
    for i in range(nt):
        if i + 1 < nt:
            load(i + 1)
        xt = xts[i % 2]
        own = i >= NH
        for s in range(NS):
            h = B.nxt(hn)
            rmsnorm_tile(B, xt[:, s, :], ss[:, s:s + 1], rstd[:, s:s + 1], g, h[:], sq[:])
            for kg in range(2):
                pi = 0
                pcnt[0] += 1
                for j in range(4):
                    kc = kg * 4 + j
                    B.transpose(V(ptr2.t[:, pi, j, :], ptr2[:, pi].keys),
                                V(h.t[:, kc * 128:(kc + 1) * 128], h[:].keys), ident[:])
                B.copy("act" if kg == 0 else "dve", hnT[:, kg * 4:(kg + 1) * 4, s, :], ptr2[:, pi])
        sl0 = (2 * i) % 8
        for j in range(8):
            p = B.nxt(pb)
            pv = V(p.t[:, 0:T], p[:].keys)
            for kc in range(8):
                B.mm(pv, V(win.t[:, kc, 1024 + j * 128:1024 + (j + 1) * 128], win[:].keys),
                     V(hnT.t[:, kc, :, :], hnT[:, kc].keys), start=(kc == 0), stop=(kc == 7))
            cnt += 1
            B.copy("act" if cnt % 2 else "dve",
                   V(KTr.t[:, j, sl0:sl0 + 2, :], KTr[:, j, sl0:sl0 + 2].keys), pv)
        for s in range(NS):
            for n in range(2):
                p = B.nxt(pb)
                for kc in range(8):
                    B.mm(p[:], hnT[:, kc, s, :],
                         V(win.t[:, kc, 2048 + n * 512:2048 + (n + 1) * 512], win[:].keys),
                         start=(kc == 0), stop=(kc == 7))
                cnt += 1
                B.copy("act" if cnt % 2 else "dve",
                       V(Vr.t[:, sl0 + s, n * 512:(n + 1) * 512], Vr[:, sl0 + s].keys), p[:])
        if not own:
            continue
        for j in range(8):
            p = B.nxt(pb)
            pv = V(p.t[:, 0:T], p[:].keys)
            for kc in range(8):
                B.mm(pv, V(win.t[:, kc, j * 128:(j + 1) * 128], win[:].keys),
                     V(hnT.t[:, kc, :, :], hnT[:, kc].keys), start=(kc == 0), stop=(kc == 7))
            B.ts("dve", V(qTA.t[0:64, j, :], qTA[:, j].keys), V(p.t[0:64, 0:T], p[:].keys), 0.125, None,
                 ALU.mult)
            B.ts("dve", V(qTB.t[64:128, j, :], qTB[:, j].keys), V(p.t[64:128, 0:T], p[:].keys), 0.125, None,
                 ALU.mult)
        for j in range(8):
            p = B.nxt(pb)
            pv = V(p.t[:, 0:T], p[:].keys)
            for kc in range(8):
                B.mm(pv, V(win.t[:, kc, 3072 + j * 128:3072 + (j + 1) * 128], win[:].keys),
                     V(hnT.t[:, kc, :, :], hnT[:, kc].keys), start=(kc == 0), stop=(kc == 7))
            B.act(sgT[:, j, :], pv, AF.Silu)
        for s in range(NS):
            u = 2 * i + s
            qc = slice(s * 128, (s + 1) * 128)
            for j in range(8):
                Es = []
                for hb in range(2):
                    hd = 2 * j + hb
                    qq = qTA if hb == 0 else qTB
                    sb_ = scnt[0] % 2
                    scnt[0] += 1
                    Skeys = Sall[:, sb_].keys
                    for dl in range(5):
                        uk = u - dl
                        slot = uk % 8
                        blk = V(Sall.t[:, sb_, dl * 128:(dl + 1) * 128], Skeys)
                        halo = uk < HALO // 128
                        B.mm(blk, KTr[:, j, slot, :], V(qq.t[:, j, qc], qq[:, j].keys), start=True, stop=False)
                        B.mm(blk, ident[:], V(bias.t[:, hd * 5 + dl, :], bias[:].keys), start=False,
                             stop=not halo)
                        if halo:
                            B.mm(blk, ident[:], hmask[:], start=False, stop=True)
                    E = B.nxt(Ep)
                    B.act(E[:], V(Sall.t[:, sb_, 0:640], Skeys), AF.Exp)
                    Es.append(E)
                OS = B.nxt(OSp)
                for hb in range(2):
                    for dl in range(5):
                        slot = (u - dl) % 8
                        B.mm(V(OS.t[:, hb * 128:(hb + 1) * 128], OS[:].keys),
                             V(Vr.t[:, slot, j * 128:(j + 1) * 128], Vr[:, slot].keys),
                             V(Es[hb].t[:, dl * 128:(dl + 1) * 128], Es[hb][:].keys),
                             start=(dl == 0), stop=(dl == 4))
                for hb in range(2):
                    for dl in range(5):
                        B.mm(V(OS.t[:, 256 + hb * 128:256 + (hb + 1) * 128], OS[:].keys), ones[:],
                             V(Es[hb].t[:, dl * 128:(dl + 1) * 128], Es[hb][:].keys),
                             start=(dl == 0), stop=(dl == 4))
                r = B.nxt(rp)
                B.recip(r[:], V(OS.t[:, 256:512], OS[:].keys))
                w = B.nxt(wp)
                for hb in range(2):
                    pr = slice(64 * hb, 64 * hb + 64)
                    B.tt("pool", V(w.t[pr, :], w[:].keys), V(r.t[pr, hb * 128:(hb + 1) * 128], r[:].keys),
                         V(sgT.t[pr, j, qc], sgT[:, j].keys), ALU.mult)
                for hb in range(2):
                    pr = slice(64 * hb, 64 * hb + 64)
                    B.tt("dve", V(ogT.t[pr, j, qc], ogT[:, j].keys),
                         V(OS.t[pr, hb * 128:(hb + 1) * 128], OS[:].keys), V(w.t[pr, :], w[:].keys), ALU.mult)
        for s in range(NS):
            for n in range(2):
                p = B.nxt(pb)
                for kc in range(8):
                    B.mm(p[:], V(ogT.t[:, kc, s * 128:(s + 1) * 128], ogT[:, kc].keys),
                         V(wo.t[:, kc, n * 512:(n + 1) * 512], wo[:].keys), start=(kc == 0), stop=(kc == 7))
                xv = V(xt.t[:, s, n * 512:(n + 1) * 512], xt[:, s].keys)
                B.tt("dve", xv, xv, p[:], ALU.add)
        B.dma("sp", rows_view(xout, (i - NH) * T, NS), xt[:], "st_o%d" % (i % 2))
    B.phase_end()


TWO_PI = 6.283185307179586
DEBUG_STOP = None
DEBUG_NT = None


def load_cols(B, name, dram_row, key):
    t = B.tile(name, [128, 4], F32)
    B.dma("sp", t[:], dram_row.rearrange("o (c p) -> p (o c)", p=128), key,
          allow_slow_non_contiguous=True)
    return t


def phase_l0(B, C, io, xin, xout):
    T = 256
    NS = 2
    T5 = 128
    npre = PRE // T
    nt = (PRE + EXT) // T
    B.phase_begin()
    ident, ones = C["ident"], C["ones"]
    g = bcast_row(B, "l0g", io["ev_norm_g"], D, B.fresh())
    win = load_w_bf16(B, "l0win", io["ev_w_in"][0], D, 2560, B.fresh())
    wo = load_w_bf16(B, "l0wo", io["ev_w_out"][0], D, D, B.fresh())
    gluw = load_w_bf16(B, "l0gluw", io["ev_s5_glu_w"][0], 512, 512, B.fresh())
    vecs = B.tile("l0vecs", [128, 20], F32)
    lrli = B.tile("l0lrli", [128, 32], F32)

    class _Sub:
        def __init__(self, t, c0, n):
            self.t = _SubT(t, c0)
            self._t = t

        def __getitem__(self, idx):
            return self._t[:]

    class _SubT:
        def __init__(self, t, c0):
            self.tt_, self.c0 = t, c0

        def __getitem__(self, idx):
            p, c = idx
            return self.tt_.t[p, self.c0 + c.start:self.c0 + c.stop]

    Dt, glub, convb, lng, lnb = [_Sub(vecs, 4 * i, 4) for i in range(5)]
    onesM = B.tile("l0onesM", [128, 128], BF16)
    B.memset("pool", onesM[:], 1.0 / 512.0)
    CS = B.tile("l0CS", [128, 16, 2, T5], F32)
    BT = [B.tile("l0BT%d" % ri, [128, 16, 128], BF16) for ri in range(2)]
    CT = [B.tile("l0CT%d" % ri, [128, 16, 128], BF16) for ri in range(3)]
    DG = B.tile("l0DG", [128, 4, 31, 128], BF16)
    mag = B.tile("l0mag", [128, 16], F32)
    cT = B.tile("l0cT", [128, 16], F32)
    sT = B.tile("l0sT", [128, 16], F32)
    init = [B.tile("l0init%d" % ri, [128, 16], F32, kaxes=1) for ri in range(2)]
    B.memset("pool", init[0][:], 0.0)
    B.memset("pool", init[1][:], 0.0)

    outer = B.pstack
    B.pstack = contextlib.ExitStack()
    pps = B.tile("l0pps", [128, 4, 128], BF16, space="psum")
    cnt = [0]

    def small(name):
        return B.tile(name, [128, 16], F32)

    ident32 = B.tile("l0id32", [128, 128], F32)
    B.dma("sp", ident32[:], io["c_ident"][:, :], B.fresh())
    Vst = B.tile("l0Vst", [32, 128], F32)
    vkey = B.fresh()
    for vi, nm in enumerate(("ev_s5_d", "ev_s5_glu_b", "ev_conv_b", "ev_conv_ln_g", "ev_conv_ln_b")):
        B.dma("sp", V(Vst.t[4 * vi:4 * vi + 4, :], Vst[:].keys),
              io[nm].rearrange("o (c p) -> (o c) p", p=128), vkey)
    LL = B.tile("l0LL", [32, 128], F32)
    lkey2 = B.fresh()
    B.dma("sp", V(LL.t[0:16, :], LL[:].keys),
          io["ev_s5_lambda_re"][0].rearrange("(k g) p -> k (g p)", g=2), lkey2)
    B.dma("sp", V(LL.t[16:32, :], LL[:].keys),
          io["ev_s5_lambda_im"][0].rearrange("(k g) p -> k (g p)", g=2), lkey2)
    pv32 = B.tile("l0pv32", [128, 64], F32, space="psum")
    B.S.op("pe", lambda: B.nc.tensor.transpose(out=pv32.t[:, 0:32], in_=LL.t[0:32, :],
                                               identity=ident32.t[0:32, 0:32]),
           LL[:].keys + ident32[:].keys, pv32[:].keys)
    B.S.op("pe", lambda: B.nc.tensor.transpose(out=pv32.t[:, 32:52], in_=Vst.t[0:20, :],
                                               identity=ident32.t[0:20, 0:20]),
           Vst[:].keys + ident32[:].keys, pv32[:].keys)
    B.copy("dve", lrli[:], V(pv32.t[:, 0:32], pv32[:].keys))
    B.copy("dve", vecs[:], V(pv32.t[:, 32:52], pv32[:].keys))
    lr = _Sub(lrli, 0, 16)
    li = _Sub(lrli, 16, 16)
    lr = V(lrli.t[:, 0:16], lrli[:].keys)
    li = V(lrli.t[:, 16:32], lrli[:].keys)
    LB = bcast_row(B, "l0LB", io["ev_s5_log_dt"], 32, B.fresh())
    ldt = small("ldt")
    for gg in range(2):
        pr = slice(64 * gg, 64 * gg + 64)
        B.copy("dve", V(ldt.t[pr, :], ldt[:].keys),
               V(LB.t[pr, :].rearrange("p (k g) -> p k g", g=2)[:, :, gg], LB[:].keys))
    if DEBUG_STOP == "loads":
        B.S.mute = True
    dt = small("dt")
    B.act(dt[:], ldt[:], AF.Exp)
    lrdt = small("lrdt")
    B.tt("dve", lrdt[:], lr, dt[:], ALU.mult)
    B.act(mag[:], lrdt[:], AF.Exp)
    th = small("th")
    B.tt("dve", th[:], li, dt[:], ALU.mult)
    tq = small("tq")
    B.ts("dve", tq[:], th[:], 1.0 / TWO_PI, None, ALU.mult)
    tqi = B.tile("tqi", [128, 16], I32)
    B.copy("dve", tqi[:], tq[:])
    tqf = small("tqf")
    B.copy("dve", tqf[:], tqi[:])
    thr = small("thr")
    B.stt("dve", thr[:], tqf[:], -TWO_PI, th[:], ALU.mult, ALU.add)
    ath = small("ath")
    B.ts("dve", ath[:], thr[:], -1.0, None, ALU.mult)
    B.tt("dve", ath[:], ath[:], thr[:], ALU.max)
    cm, sm = small("c1"), small("s1")
    B.act(cm[:], ath[:], AF.Sin, bias=C["halfpi"][:, 0:1], scale=-1.0)
    B.act(sm[:], thr[:], AF.Sin)
    if DEBUG_STOP == "trig":
        B.S.mute = True
    abre, abim = small("abre"), small("abim")
    B.tt("dve", abre[:], mag[:], cm[:], ALU.mult)
    B.tt("dve", abim[:], mag[:], sm[:], ALU.mult)
    den, t0, t1 = small("den"), small("t0"), small("t1")
    B.tt("dve", den[:], lr, lr, ALU.mult)
    B.tt("dve", t0[:], li, li, ALU.mult)
    B.tt("dve", den[:], den[:], t0[:], ALU.add)
    B.recip(den[:], den[:])
    nr = small("nr")
    B.ts("dve", nr[:], abre[:], -1.0, None, ALU.add)
    cre, cim = small("cre"), small("cim")
    B.tt("dve", t0[:], nr[:], lr, ALU.mult)
    B.tt("dve", t1[:], abim[:], li, ALU.mult)
    B.tt("dve", t0[:], t0[:], t1[:], ALU.add)
    B.tt("dve", cre[:], t0[:], den[:], ALU.mult)
    B.tt("dve", t0[:], abim[:], lr, ALU.mult)
    B.tt("dve", t1[:], nr[:], li, ALU.mult)
    B.tt("dve", t0[:], t0[:], t1[:], ALU.subtract)
    B.tt("dve", cim[:], t0[:], den[:], ALU.mult)
    if DEBUG_STOP == "abar":
        B.S.mute = True
    B.memset("pool", V(CS.t[:, :, 0, 0:1], CS[:].keys), 1.0)
    B.memset("pool", V(CS.t[:, :, 1, 0:1], CS[:].keys), 0.0)
    tb = [B.tile("l0tb%d" % i, [128, 16, 64], F32) for i in range(4)]

    def bc(t, n):
        return V(t.t[:, :].rearrange("p (k o) -> p k o", o=1).broadcast_to([128, 16, n]), t[:].keys)

    for m in range(7):
        n = 1 << m
        cosn = V(CS.t[:, :, 0, 0:n], CS[:].keys)
        sinn = V(CS.t[:, :, 1, 0:n], CS[:].keys)
        tv = [V(t.t[:, :, 0:n], t[:].keys) for t in tb]
        B.tt("dve", tv[0], cosn, bc(cm, n), ALU.mult)
        B.tt("dve", tv[1], sinn, bc(sm, n), ALU.mult)
        B.tt("dve", tv[2], cosn, bc(sm, n), ALU.mult)
        B.tt("dve", tv[3], sinn, bc(cm, n), ALU.mult)
        B.tt("dve", V(CS.t[:, :, 0, n:2 * n], CS[:].keys), tv[0], tv[1], ALU.subtract)
        B.tt("dve", V(CS.t[:, :, 1, n:2 * n], CS[:].keys), tv[2], tv[3], ALU.add)
        c2, s2, cs2 = small("c2_%d" % m), small("s2_%d" % m), small("cs_%d" % m)
        B.tt("dve", c2[:], cm[:], cm[:], ALU.mult)
        B.tt("dve", s2[:], sm[:], sm[:], ALU.mult)
        B.tt("dve", cs2[:], cm[:], sm[:], ALU.mult)
        if m < 6:
            cmn, smn = small("cm_%d" % m), small("sm_%d" % m)
        else:
            cmn, smn = cT, sT
        B.tt("dve", cmn[:], c2[:], s2[:], ALU.subtract)
        B.ts("dve", smn[:], cs2[:], 2.0, None, ALU.mult)
        cm, sm = cmn, smn
    if DEBUG_STOP == "tables":
        B.S.mute = True
    bre = B.tile("l0bre", [128, 16, 16], F32)
    bim = B.tile("l0bim", [128, 16, 16], F32)
    for bt_, nm in ((bre, "ev_s5_b_re"), (bim, "ev_s5_b_im")):
        bsrc = io[nm][0].rearrange("(k g) p c -> (g p) k c", g=2)
        bk = B.fresh()
        for k0 in range(0, 16, 4):
            B.dma("sp", V(bt_.t[:, k0:k0 + 4, :], bt_[:].keys), bsrc[:, k0:k0 + 4, :], bk)
    bbre = B.tile("l0bbre", [128, 16, 16], F32)
    bbim = B.tile("l0bbim", [128, 16, 16], F32)
    tb16 = [V(t.t[:, :, 0:16], t[:].keys) for t in tb]
    B.tt("dve", tb16[0], bre[:], bc(cre, 16), ALU.mult)
    B.tt("dve", tb16[1], bim[:], bc(cim, 16), ALU.mult)
    B.tt("dve", bbre[:], tb16[0], tb16[1], ALU.subtract)
    B.tt("dve", tb16[2], bim[:], bc(cre, 16), ALU.mult)
    B.tt("dve", tb16[3], bre[:], bc(cim, 16), ALU.mult)
    B.tt("dve", bbim[:], tb16[2], tb16[3], ALU.add)
    BD = [B.tile("l0BD%d" % ri, [128, 16, 128], BF16) for ri in range(2)]
    for ri, (bd, bb) in enumerate(zip(BD, (bbre, bbim))):
        B.memset("pool", bd[:], 0.0)
        for k in range(16):
            r = k % 4
            for gg in range(2):
                pr = slice(64 * gg, 64 * gg + 64)
                c0 = 32 * r + 16 * gg
                B.copy("pool" if gg else "dve", V(bd.t[pr, k, c0:c0 + 16], bd[:].keys),
                       V(bb.t[pr, k, :], bb[:].keys))
        for kg in range(4):
            for j in range(4):
                B.transpose(V(pps.t[:, j, :], pps[:].keys), V(bd.t[:, kg * 4 + j, :], bd[:].keys), ident[:])
            B.copy("act", V(BT[ri].t[:, kg * 4:kg * 4 + 4, :], BT[ri][:].keys), pps[:])
    if DEBUG_STOP == "bbar":
        B.S.mute = True
    for ct in CT:
        B.memset("pool", ct[:], 0.0)
    Zs = [B.tile("l0Z%d" % ri, [64, 16, 128], F32) for ri in range(2)]
    for ri, nm in enumerate(("ev_s5_c_re", "ev_s5_c_im")):
        B.memset("pool", Zs[ri][:], 0.0)
        src = io[nm][0].rearrange("(k g) co p -> g co k p", g=2)
        ckey = B.fresh()
        B.dma("sp", V(Zs[ri].t[0:16, :, 0:64], Zs[ri][:].keys), src[0], ckey)
        B.dma("sp", V(Zs[ri].t[32:48, :, 64:128], Zs[ri][:].keys), src[1], ckey)
        for k in range(16):
            r = k % 4
            Zk = Zs[ri]
            B.S.op("pe", (lambda Zk=Zk, k=k: B.nc.tensor.transpose(out=pv32.t[:, 0:64], in_=Zk.t[0:64, k, :],
                                                                  identity=ident32.t[0:64, 0:64])),
                   Zk[:].keys + ident32[:].keys, pv32[:].keys)
            for gg in range(2):
                srcv = V(pv32.t[:, 32 * gg:32 * gg + 16], pv32[:].keys)
                c0 = 32 * r + 16 * gg
                if ri == 0:
                    B.copy("act", V(CT[0].t[:, k, c0:c0 + 16], CT[0][:].keys), srcv)
                    B.ts("dve", V(CT[1].t[:, k, c0:c0 + 16], CT[1][:].keys), srcv, -1.0, None, ALU.mult)
                else:
                    B.ts("dve", V(CT[2].t[:, k, c0:c0 + 16], CT[2][:].keys), srcv, -1.0, None, ALU.mult)
    if DEBUG_STOP == "cmat":
        B.S.mute = True
    cw = B.tile("l0cw", [31, 512], F32)
    B.dma("sp", cw[:], io["ev_conv_w"][0], B.fresh())
    wk = B.tile("l0wk", [128, 4, 31], F32)
    for cc in range(4):
        B.S.op("pe", (lambda cc=cc: B.nc.tensor.transpose(out=pv32.t[:, 0:31], in_=cw.t[0:31, cc * 128:(cc + 1) * 128],
                                                          identity=ident32.t[0:31, 0:31])),
               cw[:].keys + ident32[:].keys, pv32[:].keys)
        B.copy("act", V(wk.t[:, cc, :], wk[:].keys), V(pv32.t[:, 0:31], pv32[:].keys))
    for cc in range(4):
        for k in range(31):
            B.ts("dve" if (k % 2) else "pool", V(DG.t[:, cc, k, :], DG[:].keys), ident[:],
                 V(wk.t[:, cc, k:k + 1], wk[:].keys), None, ALU.mult)
    if DEBUG_STOP in ("loads", "trig", "abar", "tables", "bbar", "cmat"):
        B.S.mute = False
        B.phase_end()
        B.pstack = outer
        B.phase_end()
        return
    B.S.barrier()
    B.pstack.close()
    B.pstack = outer

    if DEBUG_STOP == "prep":
        B.phase_end()
        return
    xts = B.pool("l0xt", 2, [128, NS, D], F32, kaxes=1)
    ss = B.tile("l0ss", [128, NS], F32, kaxes=1)
    rstd = B.tile("l0rstd", [128, NS], F32, kaxes=1)
    hn = B.pool("l0hn", 2, [128, D], BF16)
    hnT = B.tile("l0hnT", [128, 8, NS, 128], BF16, kaxes=2)
    u32 = B.tile("l0u32", [128, 4, T], F32, kaxes=1)
    ubf = B.tile("l0ubf", [128, 4, T], BF16, kaxes=1)
    sga = B.tile("l0sga", [128, 4, T], BF16, kaxes=1)
    sgb = B.tile("l0sgb", [128, 4, T], BF16, kaxes=1)
    vbf = B.tile("l0vbf", [128, 4, 30 + T], BF16, kaxes=1)
    B.memset("pool", vbf[:], 0.0)
    sgl = B.pool("l0sgl", 2, [128, T], F32)
    A4 = B.pool("l0A4", 2, [128, 2, 2, T5], F32)
    nubf = B.tile("l0nubf", [128, 4, T], BF16, kaxes=1)
    rl_all = B.tile("l0rl", [128, 16, 2], F32)
    cr16 = B.tile("l0cr16", [128, 4, 16], F32)
    G = B.pool("l0G", 2, [128, 2, T5], F32)
    R = B.pool("l0R", 2, [128, 2, T5], F32)
    PA = B.pool("l0PA", 2, [128, 2, T5], BF16)
    PB = B.pool("l0PB", 2, [128, 2, T5], BF16)
    y32 = B.pool("l0y32", 1, [128, T], F32)
    z32 = B.tile("l0z32", [128, 4, T], F32, kaxes=1)
    zbf = B.tile("l0zbf", [128, 4, T], BF16, kaxes=1)
    sig = sgl
    c32 = B.tile("l0c32", [128, 4, T], F32, kaxes=1)
    cbf = B.tile("l0cbf", [128, 4, T], BF16, kaxes=1)
    csq = B.tile("l0csq", [128, 4, T], BF16, kaxes=1)
    mus = B.tile("l0mus", [128, T], F32)
    var = B.tile("l0var", [128, T], F32)
    rsd = B.tile("l0rsd", [128, T], F32)
    cn = B.pool("l0cn", 2, [128, T], F32)
    yab = B.tile("l0yab", [128, 8, T], BF16, kaxes=1)
    pb = B.pool("l0pb", 2, [128, 512], F32, space="psum")
    pS = B.pool("l0pS", 2, [128, 2, 2, T5], F32, space="psum")
    yps = B.pool("l0yps", 1, [128, T], F32, space="psum")
    pst = B.pool("l0pst", 2, [128, T], F32, space="psum")
    ptr = B.tile("l0ptr", [128, 4, 128], BF16, space="psum")

    def load(i):
        B.dma("sp", xts[i % 2][:], rows_view(xin, i * T, NS), "ld_x%d" % (i % 2))

    def proj(c):
        p = B.nxt(pb)
        pv = V(p.t[:, 0:T], p[:].keys)
        for kc in range(8):
            B.mm(pv, V(win.t[:, kc, c * 128:(c + 1) * 128], win[:].keys),
                 V(hnT.t[:, kc, :, :], hnT[:, kc].keys), start=(kc == 0), stop=(kc == 7))
        return pv

    load(0)
    for i in range(nt):
        if DEBUG_NT is not None and i >= DEBUG_NT:
            break
        if i + 1 < nt:
            load(i + 1)
        xt = xts[i % 2]
        pre = i < npre
        lastpre = i == npre - 1
        for s in range(NS):
            h = B.nxt(hn)
            rmsnorm_tile(B, xt[:, s, :], ss[:, s:s + 1], rstd[:, s:s + 1], g, h[:], h[:])
            for kg in range(2):
                for j in range(4):
                    kc = kg * 4 + j
                    B.transpose(V(ptr.t[:, j, :], ptr[:].keys), V(h.t[:, kc * 128:(kc + 1) * 128], h[:].keys),
                                ident[:])
                B.copy("act" if kg == 0 else "dve", hnT[:, kg * 4:(kg + 1) * 4, s, :], ptr[:])
        for cc in range(4):
            pv = proj(cc)
            if pre:
                B.copy("act", ubf[:, cc, :], pv)
            else:
                B.copy("act", u32[:, cc, :], pv)
                B.copy("pool", ubf[:, cc, :], u32[:, cc, :])
        side = []
        if lastpre:
            for cc in range(4):
                sg = B.nxt(sgl)
                B.act(sg[:], proj(12 + cc), AF.Sigmoid)
                B.tt("dve", V(vbf.t[:, cc, 30:30 + T], vbf[:, cc].keys), proj(8 + cc), sg[:], ALU.mult)
        if not pre:
            def w_sga(cc):
                B.act(sga[:, cc, :], proj(4 + cc), AF.Silu)

            def w_v(cc):
                sg = B.nxt(sgl)
                B.act(sg[:], proj(12 + cc), AF.Sigmoid)
                B.tt("dve", V(vbf.t[:, cc, 30:30 + T], vbf[:, cc].keys), proj(8 + cc), sg[:], ALU.mult)

            def w_sgb(cc):
                B.act(sgb[:, cc, :], proj(16 + cc), AF.Silu)

            def w_conv(cc):
                p = B.nxt(pb)
                pv = V(p.t[:, 0:T], p[:].keys)
                for k in range(31):
                    B.mm(pv, V(DG.t[:, cc, k, :], DG[:].keys), V(vbf.t[:, cc, k:k + T], vbf[:, cc].keys),
                         start=(k == 0), stop=(k == 30))
                bcc = V(convb.t[:, cc:cc + 1], convb[:].keys)
                B.act(c32[:, cc, :], pv, AF.Identity, bias=bcc)
                B.act(cbf[:, cc, :], pv, AF.Identity, bias=bcc)
                B.act(csq[:, cc, :], pv, AF.Square, bias=bcc)

            def w_ln():
                B.copy("pool", V(vbf.t[:, :, 0:30], vbf[:].keys), V(vbf.t[:, :, T:T + 30], vbf[:].keys))
                pmu, pm2 = B.nxt(pst), B.nxt(pst)
                for cc in range(4):
                    B.mm(pmu[:], onesM[:], cbf[:, cc, :], start=(cc == 0), stop=(cc == 3))
                for cc in range(4):
                    B.mm(pm2[:], onesM[:], csq[:, cc, :], start=(cc == 0), stop=(cc == 3))
                B.copy("act", mus[:], pmu[:])
                B.tt("dve", var[:], mus[:], mus[:], ALU.mult)
                B.tt("dve", var[:], pm2[:], var[:], ALU.subtract)
                B.act(rsd[:], var[:], AF.Sqrt, bias=EPS)
                B.recip(rsd[:], rsd[:])

            def w_cn(cc):
                c_ = B.nxt(cn)
                B.tt("dve", c_[:], c32[:, cc, :], mus[:], ALU.subtract)
                B.tt("pool", c_[:], c_[:], rsd[:], ALU.mult)
                B.act(c_[:], c_[:], AF.Silu, bias=V(lnb.t[:, cc:cc + 1], lnb[:].keys),
                      scale=V(lng.t[:, cc:cc + 1], lng[:].keys))
                B.tt("dve", yab[:, 4 + cc, :], c_[:], sgb[:, cc, :], ALU.mult)

            for cc in range(4):
                side.append(lambda cc=cc: w_sga(cc))
            for cc in range(4):
                side.append(lambda cc=cc: w_v(cc))
            for cc in range(4):
                side.append(lambda cc=cc: w_sgb(cc))
            for cc in range(4):
                side.append(lambda cc=cc: w_conv(cc))
            side.append(w_ln)
            for cc in range(4):
                side.append(lambda cc=cc: w_cn(cc))
        for cc in range(4):
            B.ts("pool", nubf[:, cc, :], ubf[:, cc, :], -1.0, None, ALU.mult)
        its = [(h5, cc, r) for h5 in range(2) for cc in range(4) for r in range(4)]
        st = [dict() for _ in its]

        def s1(n):
            h5, cc, r = its[n]
            k = cc * 4 + r
            c5 = slice(h5 * T5, (h5 + 1) * T5)
            P = B.nxt(pS)
            uv = V(ubf.t[:, cc, c5], ubf[:, cc].keys)
            nuv = V(nubf.t[:, cc, c5], nubf[:, cc].keys)
            for (x_, y_, ri, rv) in ((0, 0, 0, uv), (0, 1, 1, nuv), (1, 0, 1, uv), (1, 1, 0, uv)):
                B.mm(V(P.t[:, x_, y_, :], P[:].keys), V(BT[ri].t[:, k, :], BT[ri][:].keys), rv)
            csb = V(CS.t[:, k:k + 1, :, :].broadcast_to([128, 2, 2, T5]), CS[:].keys)
            a_ = B.nxt(A4)
            B.tt("dve", a_[:], P[:], csb, ALU.mult)
            st[n]["a"] = a_

        def s2(n):
            a_ = st[n]["a"]
            gt = B.nxt(G)
            B.tt("pool", gt[:], V(a_.t[:, :, 0, :], a_[:].keys), V(a_.t[:, :, 1, :], a_[:].keys), ALU.subtract)
            st[n]["g"] = gt

        def s3(n):
            h5, cc, r = its[n]
            k = cc * 4 + r
            gt = st[n]["g"]
            rt = B.nxt(R)
            magk = V(mag.t[:, k:k + 1].broadcast_to([128, T5]), mag[:].keys)
            for ri in range(2):
                B.scan(V(rt.t[:, ri, :], rt[:].keys), magk, V(gt.t[:, ri, :], gt[:].keys),
                       V(init[ri].t[:, k:k + 1], init[ri][:, k].keys))
            st[n]["r"] = rt
            if not pre:
                pa = B.nxt(PA)
                B.tt("dve", pa[:], rt[:], V(CS.t[:, k, :, :], CS[:].keys), ALU.mult)
                st[n]["pa"] = pa

        def s3b(n):
            h5, cc, r = its[n]
            k = cc * 4 + r
            rt = st[n]["r"]
            B.copy("pool", V(rl_all.t[:, k, :], rl_all[:].keys), V(rt.t[:, :, T5 - 1], rt[:].keys))
            if not pre:
                pb_ = B.nxt(PB)
                B.tt("pool", V(pb_.t[:, 0, :], pb_[:].keys), V(rt.t[:, 0, :], rt[:].keys),
                     V(CS.t[:, k, 1, :], CS[:].keys), ALU.mult)
                B.tt("pool", V(pb_.t[:, 1, :], pb_[:].keys), V(rt.t[:, 1, :], rt[:].keys),
                     V(CS.t[:, k, 0, :], CS[:].keys), ALU.mult)
                st[n]["pb"] = pb_
            if cc == 3 and r == 3:
                rlr = V(rl_all.t[:, :, 0], rl_all[:].keys)
                rli = V(rl_all.t[:, :, 1], rl_all[:].keys)
                crv = [V(cr16.t[:, j_, :], cr16[:].keys) for j_ in range(4)]
                B.tt("pool", crv[0], rli, sT[:], ALU.mult)
                B.tt("pool", crv[1], rli, cT[:], ALU.mult)
                B.tt("pool", crv[2], rlr, cT[:], ALU.mult)
                B.tt("pool", crv[3], rlr, sT[:], ALU.mult)
                B.tt("pool", init[0][:], crv[2], crv[0], ALU.subtract)
                B.tt("pool", init[1][:], crv[3], crv[1], ALU.add)

        def s4(n):
            if pre:
                return
            h5, cc, r = its[n]
            k = cc * 4 + r
            c5 = slice(h5 * T5, (h5 + 1) * T5)
            if r == 0:
                st[n]["yp"] = B.nxt(yps)
            else:
                st[n]["yp"] = st[n - 1]["yp"]
            yp = st[n]["yp"]
            pa, pb_ = st[n]["pa"], st[n]["pb"]
            yv = V(yp.t[:, 0:T5], yp[:].keys)
            B.mm(yv, V(CT[0].t[:, k, :], CT[0][:].keys), V(pa.t[:, 0, :], pa[:].keys), start=(r == 0), stop=False)
            B.mm(yv, V(CT[1].t[:, k, :], CT[1][:].keys), V(pa.t[:, 1, :], pa[:].keys), start=False, stop=False)
            B.mm(yv, V(CT[2].t[:, k, :], CT[2][:].keys), V(pb_.t[:, 0, :], pb_[:].keys), start=False, stop=False)
            B.mm(yv, V(CT[2].t[:, k, :], CT[2][:].keys), V(pb_.t[:, 1, :], pb_[:].keys), start=False,
                 stop=(r == 3))
            if r == 3:
                y = B.nxt(y32)
                yh = V(y.t[:, 0:T5], y[:].keys)
                B.stt("dve", yh, V(u32.t[:, cc, c5], u32[:, cc].keys), V(Dt.t[:, cc:cc + 1], Dt[:].keys),
                      yv, ALU.mult, ALU.add)
                B.act(V(z32.t[:, cc, c5], z32[:, cc].keys), yh, AF.Gelu_apprx_tanh)
                B.copy("pool", V(zbf.t[:, cc, c5], zbf[:, cc].keys), V(z32.t[:, cc, c5], z32[:, cc].keys))

        NI = len(its)
        for t in range(NI + 4):
            if t < NI:
                s1(t)
            if 0 <= t - 3 < NI:
                s3b(t - 3)
            if 0 <= t - 4 < NI:
                s4(t - 4)
            if 0 <= t - 1 < NI:
                s2(t - 1)
            if 0 <= t - 2 < NI:
                s3(t - 2)
            if side and t >= 1:
                side.pop(0)()
        while side:
            side.pop(0)()
        if pre:
            if lastpre:
                B.copy("pool", V(vbf.t[:, :, 0:30], vbf[:].keys), V(vbf.t[:, :, T:T + 30], vbf[:].keys))
            continue
        if DEBUG_STOP == "e_s5" and not pre:
            B.S.mute = True
        for co in range(4):
            p = B.nxt(pb)
            pv = V(p.t[:, 0:T], p[:].keys)
            for ci in range(4):
                B.mm(pv, V(gluw.t[:, ci, co * 128:(co + 1) * 128], gluw[:].keys), zbf[:, ci, :],
                     start=(ci == 0), stop=(ci == 3))
            sg = B.nxt(sig)
            B.act(sg[:], pv, AF.Sigmoid, bias=V(glub.t[:, co:co + 1], glub[:].keys))
            B.tt("pool", sg[:], sg[:], sga[:, co, :], ALU.mult)
            B.tt("dve", yab[:, co, :], z32[:, co, :], sg[:], ALU.mult)
        if DEBUG_STOP == "e_glu" and not pre:
            B.S.mute = True
        if DEBUG_STOP == "e_ln" and not pre:
            B.S.mute = True
        for s in range(NS):
            for n in range(2):
                p = B.nxt(pb)
                for kc in range(8):
                    B.mm(p[:], V(yab.t[:, kc, s * 128:(s + 1) * 128], yab[:, kc].keys),
                         V(wo.t[:, kc, n * 512:(n + 1) * 512], wo[:].keys), start=(kc == 0), stop=(kc == 7))
                xv = V(xt.t[:, s, n * 512:(n + 1) * 512], xt[:, s].keys)
                B.tt("dve", xv, xv, p[:], ALU.add)
        if DEBUG_STOP != "nostore":
            B.dma("pool" if DEBUG_STOP == "poolstore" else "sp", rows_view(xout, (i - npre) * T, NS), xt[:],
                  "st_o%d" % (i % 2))
    B.S.mute = False
    B.phase_end()


W_NAMES = ["mem_norm_g", "ev_norm_g", "ev_w_in", "ev_s5_lambda_re", "ev_s5_lambda_im", "ev_s5_log_dt",
           "ev_s5_b_re", "ev_s5_b_im", "ev_s5_c_re", "ev_s5_c_im", "ev_s5_d", "ev_s5_glu_w",
           "ev_s5_glu_b", "ev_conv_w", "ev_conv_b", "ev_conv_ln_g", "ev_conv_ln_b", "ev_w_out",
           "od_norm_g", "od_w_in", "od_rel_bias", "od_w_out", "xa_norm_g", "xa_w_qg", "xa_w_kv",
           "xa_w_o", "final_norm_g"]


def setup_consts(B, io):
    C = {}
    ident = B.tile("ident", [128, 128], BF16, persistent=True)
    B.dma("pool", ident[:], io["c_ident"][:, :], "ld_ident")
    ones = B.tile("ones", [128, 128], BF16, persistent=True)
    B.memset("pool", ones[:], 1.0)
    C["ident"] = ident
    C["ones"] = ones
    halfpi = B.tile("halfpi", [128, 1], F32, persistent=True)
    B.memset("pool", halfpi[:], 1.5707963267948966)
    C["halfpi"] = halfpi
    return C


def build_program(mode="full"):
    nc = bass.Bass("TRN2", target_bir_lowering=False)
    io = {}

    def din(name, shape):
        io[name] = nc.dram_tensor(name, list(shape), F32, kind="ExternalInput").ap()

    def dout(name, shape):
        io[name] = nc.dram_tensor(name, list(shape), F32, kind="ExternalOutput").ap()

    def dint(name, shape):
        io[name] = nc.dram_tensor(name, list(shape), F32, kind="Internal").ap()

    din("c_ident", [128, 128])
    din("mem", [MEM, D])
    din("mem_norm_g", [1, D])
    din("xa_norm_g", [2, D])
    din("xa_w_qg", [2, D, 2048])
    din("xa_w_kv", [2, D, 2048])
    din("xa_w_o", [2, D, D])
    din("final_norm_g", [1, D])
    B = Builder(nc)
    outs = []
    if mode == "xa_test":
        din("xin", [OWN, D])
        dout("out", [OWN, D])
        C = setup_consts(B, io)
        phase_xa(B, C, io, 1, io["xin"], io["out"], OWN, final=True)
        outs = ["st_o0", "st_o1"]
    if mode == "l0_test":
        din("xin", [PRE + EXT, D])
        dout("out", [EXT, D])
        for nm, shp in (("ev_norm_g", [1, D]), ("ev_w_in", [1, D, 2560]), ("ev_s5_lambda_re", [1, 32, 64]),
                        ("ev_s5_lambda_im", [1, 32, 64]), ("ev_s5_log_dt", [1, 32]),
                        ("ev_s5_b_re", [1, 32, 64, 16]), ("ev_s5_b_im", [1, 32, 64, 16]),
                        ("ev_s5_c_re", [1, 32, 16, 64]), ("ev_s5_c_im", [1, 32, 16, 64]),
                        ("ev_s5_d", [1, 512]), ("ev_s5_glu_w", [1, 512, 512]), ("ev_s5_glu_b", [1, 512]),
                        ("ev_conv_w", [1, 31, 512]), ("ev_conv_b", [1, 512]), ("ev_conv_ln_g", [1, 512]),
                        ("ev_conv_ln_b", [1, 512]), ("ev_w_out", [1, D, D])):
            din(nm, shp)
        C = setup_consts(B, io)
        phase_l0(B, C, io, io["xin"], io["out"])
        outs = [k for k in ("st_o0", "st_o1") if k in B.S.dsem]
    if mode == "l1_test":
        din("xin", [EXT, D])
        dout("out", [OWN, D])
        din("od_norm_g", [1, D])
        din("od_w_in", [1, D, 4096])
        din("od_w_out", [1, D, D])
        din("c_bias", [80, 128, 128])
        din("c_hmask", [128, 128])
        C = setup_consts(B, io)
        phase_l1(B, C, io, io["xin"], io["out"])
        outs = ["st_o0", "st_o1"]
    if mode == "full":
        din("x_ext", [PRE + EXT, D])
        dout("out", [OWN, D])
        dint("xa", [EXT, D])
        dint("xb", [EXT, D])
        dint("xc", [OWN, D])
        for nm, shp in (("ev_norm_g", [1, D]), ("ev_w_in", [1, D, 2560]), ("ev_s5_lambda_re", [1, 32, 64]),
                        ("ev_s5_lambda_im", [1, 32, 64]), ("ev_s5_log_dt", [1, 32]),
                        ("ev_s5_b_re", [1, 32, 64, 16]), ("ev_s5_b_im", [1, 32, 64, 16]),
                        ("ev_s5_c_re", [1, 32, 16, 64]), ("ev_s5_c_im", [1, 32, 16, 64]),
                        ("ev_s5_d", [1, 512]), ("ev_s5_glu_w", [1, 512, 512]), ("ev_s5_glu_b", [1, 512]),
                        ("ev_conv_w", [1, 31, 512]), ("ev_conv_b", [1, 512]), ("ev_conv_ln_g", [1, 512]),
                        ("ev_conv_ln_b", [1, 512]), ("ev_w_out", [1, D, D]),
                        ("od_norm_g", [1, D]), ("od_w_in", [1, D, 4096]), ("od_w_out", [1, D, D]),
                        ("c_bias", [80, 128, 128]), ("c_hmask", [128, 128])):
            din(nm, shp)
        C = setup_consts(B, io)
        phase_l0(B, C, io, io["x_ext"], io["xa"])
        phase_xa(B, C, io, 0, io["xa"], io["xb"], EXT, final=False)
        phase_l1(B, C, io, io["xb"], io["xc"])
        phase_xa(B, C, io, 1, io["xc"], io["out"], OWN, final=True)
        outs = ["st_o0", "st_o1"]
    B.S.flush(final_wait_keys=outs)
    B.gstack.close()
    return nc


def host_bias_tiles(rel_bias):
    k = np.arange(128)[:, None]
    q = np.arange(128)[None, :]
    out = np.empty((16, 5, 128, 128), np.float32)
    for dl in range(5):
        rel = np.clip(q - k + 128 * dl, -128, 128) + 128
        kb = (k >= 64).astype(np.int64)
        qb = (q >= 64).astype(np.int64)
        dc = -2 * dl + kb - qb
        masked = (dc > 0) | (dc < -8)
        for h in range(16):
            out[h, dl] = np.where(masked, np.float32(NEG), rel_bias[h][rel])
    return out.reshape(80, 128, 128)


_PROGRAM = None


def kernel(**inputs):
    global _PROGRAM
    if _PROGRAM is None:
        _PROGRAM = build_program("full")
    nc = _PROGRAM
    f32 = np.float32
    x = np.asarray(inputs["x"], f32)
    mem = np.asarray(inputs["mem"], f32)
    shared = {"c_ident": np.eye(128, dtype=f32),
              "c_bias": host_bias_tiles(np.asarray(inputs["od_rel_bias"], f32)[0]),
              "mem_norm_g": np.asarray(inputs["mem_norm_g"], f32)[None, :],
              "final_norm_g": np.asarray(inputs["final_norm_g"], f32)[None, :]}
    for k in W_NAMES:
        if k not in shared and k != "od_rel_bias":
            shared[k] = np.ascontiguousarray(np.asarray(inputs[k], f32))
    in_maps = []
    for c in range(NCORES):
        b, hh = divmod(c, 2)
        m = dict(shared)
        if hh == 0:
            m["x_ext"] = np.concatenate([np.zeros((OWN, D), f32), x[b, :OWN]], 0)
            m["c_hmask"] = np.full((128, 128), NEG, f32)
        else:
            m["x_ext"] = np.ascontiguousarray(x[b])
            m["c_hmask"] = np.zeros((128, 128), f32)
        m["mem"] = np.ascontiguousarray(mem[b])
        in_maps.append(m)
    res = run_bass_kernel_spmd(nc, in_maps, core_ids=list(range(NCORES)))
    out = np.empty((NB, SEQ, D), f32)
    for c in range(NCORES):
        b, hh = divmod(c, 2)
        out[b, hh * OWN:(hh + 1) * OWN] = res.results[c]["out"]
    return out
```

```python
import bisect
import contextlib
import itertools
import numpy as np
import concourse.bass as bass
import concourse.mybir as mybir
from concourse.bass_utils import run_bass_kernel_spmd

F32 = mybir.dt.float32
BF16 = mybir.dt.bfloat16
I32 = mybir.dt.int32
AF = mybir.ActivationFunctionType
ALU = mybir.AluOpType

D = 1024
SEQ = 8192
NB = 4
NCORES = 8
OWN = 4096
HALO = 512
EXT = OWN + HALO
PRE = SEQ // 2 - HALO
MEM = 256
EPS = 1e-6
NEG = -30000.0
RAW_ONLY_SAME_ENGINE = False


class _Op:
    __slots__ = ("eng", "fn", "reads", "writes", "dma", "deps", "sig", "tick", "idx", "dcount",
                 "after")


class Sched:
    NDSEM = 40

    def __init__(self, nc, stack, same_engine_sync=True):
        self.nc = nc
        self.engs = {"pe": nc.tensor, "act": nc.scalar, "dve": nc.vector,
                     "pool": nc.gpsimd, "sp": nc.sync}
        self.same = same_engine_sync
        self.ops = []
        self.flushed = 0
        self.last_write = {}
        self.readers = {}
        self.tick = {e: 0 for e in self.engs}
        self.dcount = {}
        self.waited = {e: {} for e in self.engs}
        self.esem = {e: stack.enter_context(nc.semaphore("sem_" + e)) for e in self.engs}
        self.dsem_pool = [stack.enter_context(nc.semaphore("dsem_%d" % i))
                          for i in range(self.NDSEM)]
        self.dsem = {}
        self.last_eng_op = {}
        self.last_dma_op = {}
        self.n_wait = 0
        self.sigidx = {e: [] for e in self.engs}

    mute = False

    def op(self, eng, fn, reads=(), writes=(), dma=None, after=()):
        if self.mute:
            return None
        o = _Op()
        o.eng = eng
        o.fn = fn
        o.reads = tuple(reads)
        o.writes = tuple(writes)
        o.dma = dma
        o.sig = False
        o.after = tuple(after)
        o.idx = len(self.ops)
        self.ops.append(o)
        return o

    def barrier(self):
        self.flush()
        after = [o for o in self.last_eng_op.values()] + [o for o in self.last_dma_op.values()]
        sp = self.engs["sp"]
        b = self.op("sp", lambda: sp.nop(), after=after)
        for e in ("pe", "act", "dve", "pool"):
            en = self.engs[e]
            self.op(e, (lambda en=en: en.nop()), after=[b])
        self.flush()

    def flush(self, final_wait_keys=()):
        ops = self.ops
        new = ops[self.flushed:]
        for o in new:
            deps = set()
            raw = set()
            for k in o.reads:
                if k in self.last_write:
                    deps.add(self.last_write[k])
                    raw.add(self.last_write[k])
            for k in o.writes:
                if k in self.last_write:
                    deps.add(self.last_write[k])
                for r in self.readers.get(k, ()):
                    deps.add(r)
            for a in o.after:
                deps.add(a.idx)
                raw.add(a.idx)
            deps.discard(o.idx)
            if RAW_ONLY_SAME_ENGINE:
                deps = {d for d in deps if ops[d].eng != o.eng or ops[d].dma is not None or d in raw}
            for k in o.reads:
                self.readers.setdefault(k, []).append(o.idx)
            for k in o.writes:
                self.last_write[k] = o.idx
                self.readers[k] = []
            best = {}
            dd = []
            for d in deps:
                od = ops[d]
                if od.dma is not None:
                    dd.append(d)
                    continue
                if od.eng == o.eng and (od.eng in ("pe", "sp") or not self.same):
                    continue
                if od.eng not in best or best[od.eng] < d:
                    best[od.eng] = d
            o.deps = dd
            for d in best.values():
                od = ops[d]
                if not od.sig:
                    if d >= self.flushed:
                        od.sig = True
                    else:
                        lst = self.sigidx[od.eng]
                        j = bisect.bisect_left(lst, d)
                        d = lst[j]
                o.deps.append(d)
            if o.dma is None:
                self.last_eng_op[o.eng] = o
            else:
                self.last_dma_op[o.dma] = o
        for e, o in self.last_eng_op.items():
            if o.idx >= self.flushed:
                o.sig = True
        for o in new:
            if o.dma is not None:
                if o.dma not in self.dsem:
                    self.dsem[o.dma] = self.dsem_pool[len(self.dsem)]
                self.dcount[o.dma] = self.dcount.get(o.dma, 0) + 1
                o.dcount = self.dcount[o.dma]
            elif o.sig:
                self.tick[o.eng] += 1
                o.tick = self.tick[o.eng]
                self.sigidx[o.eng].append(o.idx)
        for o in new:
            e = self.engs[o.eng]
            w = self.waited[o.eng]
            need = {}
            for d in o.deps:
                od = ops[d]
                if od.dma is not None:
                    s = self.dsem[od.dma]
                    v = 16 * od.dcount
                else:
                    s = self.esem[od.eng]
                    v = od.tick
                if need.get(s, 0) < v:
                    need[s] = v
            for s, v in need.items():
                if w.get(s, 0) < v:
                    e.wait_ge(s, v)
                    w[s] = v
                    self.n_wait += 1
            inst = o.fn()
            if o.dma is not None:
                inst.then_inc(self.dsem[o.dma], 16)
            elif o.sig:
                inst.then_inc(self.esem[o.eng], 1)
            o.fn = None
        self.flushed = len(ops)
        sp = self.engs["sp"]
        for k in final_wait_keys:
            sp.wait_ge(self.dsem[k], 16 * self.dcount[k])


class V:
    __slots__ = ("ap", "keys")

    def __init__(self, ap, keys):
        self.ap = ap
        self.keys = tuple(keys)


class Tile:
    def __init__(self, name, handle, shape, kaxes):
        self.name = name
        self.t = handle
        self.shape = list(shape)
        self.kaxes = kaxes

    def __getitem__(self, idx):
        if not isinstance(idx, tuple):
            idx = (idx,)
        ranges = []
        for a in range(self.kaxes):
            i = idx[1 + a] if len(idx) > 1 + a else slice(None)
            n = self.shape[1 + a]
            if isinstance(i, int):
                ranges.append((i,))
            else:
                ranges.append(tuple(range(*i.indices(n))))
        keys = [(self.name,) + c for c in itertools.product(*ranges)]
        return V(self.t[idx], keys)


def _k(x):
    return x.keys if isinstance(x, V) else ()


def _a(x):
    return x.ap if isinstance(x, V) else x


class Builder:
    def __init__(self, nc):
        self.nc = nc
        self.gstack = contextlib.ExitStack()
        self.S = Sched(nc, self.gstack)
        self.pstack = None
        self.uid = 0
        self.nkey = 0
        self.rr = {}

    def phase_begin(self):
        self.pstack = contextlib.ExitStack()
        self.nkey = 0

    def fresh(self):
        self.nkey += 1
        return "u%d" % self.nkey

    def phase_end(self):
        self.S.barrier()
        self.pstack.close()
        self.pstack = None

    def tile(self, name, shape, dtype=F32, kaxes=0, space="sbuf", persistent=False):
        self.uid += 1
        nm = "%s_%d" % (name, self.uid)
        st = self.gstack if persistent else self.pstack
        if space == "sbuf":
            h = st.enter_context(self.nc.sbuf_tensor(nm, list(shape), dtype))
        else:
            h = st.enter_context(self.nc.psum_tensor(nm, list(shape), dtype))
        return Tile(nm, h, shape, kaxes)

    def pool(self, name, n, shape, dtype=F32, kaxes=0, space="sbuf"):
        tiles = [self.tile("%s%d" % (name, i), shape, dtype, kaxes, space) for i in range(n)]
        self.rr[id(tiles)] = 0
        return tiles

    def nxt(self, tiles):
        i = self.rr[id(tiles)]
        self.rr[id(tiles)] = (i + 1) % len(tiles)
        return tiles[i]

    def mm(self, out, lhsT, rhs, start=True, stop=True, **kw):
        nc = self.nc
        rd = lhsT.keys + rhs.keys + (() if start else out.keys)
        self.S.op("pe", lambda: nc.tensor.matmul(out.ap, lhsT=lhsT.ap, rhs=rhs.ap, start=start,
                                                 stop=stop, **kw), rd, out.keys)

    def transpose(self, out, in_, ident):
        nc = self.nc
        self.S.op("pe", lambda: nc.tensor.transpose(out=out.ap, in_=in_.ap, identity=ident.ap),
                  in_.keys + ident.keys, out.keys)

    def act(self, out, in_, func, bias=None, scale=None, accum=None):
        nc = self.nc
        kw = {}
        if bias is not None:
            kw["bias"] = _a(bias)
        if scale is not None:
            kw["scale"] = _a(scale)
        if accum is not None:
            kw["accum_out"] = accum.ap
        self.S.op("act", lambda: nc.scalar.activation(out=out.ap, in_=in_.ap, func=func, **kw),
                  in_.keys + _k(bias) + _k(scale), out.keys + _k(accum))

    def _ve(self, eng):
        return self.nc.vector if eng == "dve" else self.nc.gpsimd

    def tt(self, eng, out, in0, in1, op):
        e = self._ve(eng)
        self.S.op(eng, lambda: e.tensor_tensor(out=out.ap, in0=in0.ap, in1=in1.ap, op=op),
                  in0.keys + in1.keys, out.keys)

    def ts(self, eng, out, in0, s1, s2=None, op0=ALU.mult, op1=None):
        e = self._ve(eng)
        kw = {}
        if op1 is not None:
            kw["op1"] = op1
        self.S.op(eng, lambda: e.tensor_scalar(out=out.ap, in0=in0.ap, scalar1=_a(s1),
                                               scalar2=_a(s2), op0=op0, **kw),
                  in0.keys + _k(s1) + _k(s2), out.keys)

    def stt(self, eng, out, in0, scalar, in1, op0, op1):
        e = self._ve(eng)
        self.S.op(eng, lambda: e.scalar_tensor_tensor(out=out.ap, in0=in0.ap, scalar=_a(scalar),
                                                      in1=in1.ap, op0=op0, op1=op1),
                  in0.keys + _k(scalar) + in1.keys, out.keys)

    def copy(self, eng, out, in_):
        if eng == "act":
            nc = self.nc
            self.S.op("act", lambda: nc.scalar.copy(out=out.ap, in_=in_.ap), in_.keys, out.keys)
        else:
            e = self._ve(eng)
            self.S.op(eng, lambda: e.tensor_copy(out=out.ap, in_=in_.ap), in_.keys, out.keys)

    def memset(self, eng, out, val):
        e = self._ve(eng)
        self.S.op(eng, lambda: e.memset(out.ap, val), (), out.keys)

    def recip(self, out, in_):
        nc = self.nc
        self.S.op("dve", lambda: nc.vector.reciprocal(out=out.ap, in_=in_.ap), in_.keys, out.keys)

    def scan(self, out, d0, d1, init, op0=ALU.mult, op1=ALU.add):
        nc = self.nc
        self.S.op("dve", lambda: nc.vector.tensor_tensor_scan(out=out.ap, data0=d0.ap, data1=d1.ap,
                                                              initial=_a(init), op0=op0, op1=op1),
                  d0.keys + d1.keys + _k(init), out.keys)

    def dma(self, eng, out, in_, key, **kw):
        e = self.S.engs[eng]
        self.S.op(eng, lambda: e.dma_start(out=_a(out), in_=_a(in_), **kw), _k(in_), _k(out),
                  dma=key)


def rows_view(dram, r0, nsub):
    return dram[r0:r0 + 128 * nsub, :].rearrange("(s p) d -> p s d", p=128)


def load_w_bf16(B, name, dram, kdim, ndim, key, c0=0, eng="pool"):
    kc = kdim // 128
    t = B.tile(name, [128, kc, ndim], BF16, kaxes=0)
    src = dram[:, c0:c0 + ndim].rearrange("(k p) n -> p k n", p=128)
    half = max(1, kc // 2)
    for i in range(0, kc, half):
        B.dma(eng, V(t.t[:, i:i + half, :], t[:].keys), src[:, i:i + half, :], key)
    return t


def bcast_row(B, name, dram_row, n, key, persistent=False):
    t = B.tile(name, [128, n], F32, persistent=persistent)
    B.dma("sp", t[:], dram_row.broadcast_to([128, n]), key)
    return t


def rmsnorm_tile(B, xt_v, ss_v, rstd_v, g_t, out_v, sq_scratch):
    B.act(sq_scratch, xt_v, AF.Square, accum=ss_v)
    B.act(rstd_v, ss_v, AF.Sqrt, bias=EPS, scale=1.0 / D)
    B.recip(rstd_v, rstd_v)
    B.stt("dve", out_v, xt_v, rstd_v, g_t[:], ALU.mult, ALU.mult)


def mem_kv(B, C, io, l, kt, vv):
    outer = B.pstack
    B.pstack = contextlib.ExitStack()
    ident = C["ident"]
    g = bcast_row(B, "memg", io["mem_norm_g"], D, B.fresh())
    mt = B.tile("memx", [128, 2, D], F32, kaxes=1)
    B.dma("sp", mt[:], rows_view(io["mem"], 0, 2), "ld_x0")
    sq = B.tile("memsq", [128, D], F32)
    ss = B.tile("memss", [128, 2], F32, kaxes=1)
    rstd = B.tile("memrs", [128, 2], F32, kaxes=1)
    mn = B.tile("memn", [128, 2, D], BF16, kaxes=1)
    memT = B.tile("memT", [128, 8, 2, 128], BF16, kaxes=2)
    pbig = B.pool("mpb", 4, [128, 512], F32, space="psum")
    ptr = B.pool("mptr", 2, [128, 4, 128], BF16, space="psum")
    for s in range(2):
        rmsnorm_tile(B, mt[:, s, :], ss[:, s:s + 1], rstd[:, s:s + 1], g, mn[:, s, :], sq[:])
        for kg in range(2):
            p = B.nxt(ptr)
            for j in range(4):
                kc = kg * 4 + j
                B.transpose(V(p.t[:, j, :], p[:].keys), mn[:, s, kc * 128:(kc + 1) * 128], ident[:])
            B.copy("act" if kg == 0 else "dve", memT[:, kg * 4:(kg + 1) * 4, s, :], p[:])
    wkv = load_w_bf16(B, "wkv%d" % l, io["xa_w_kv"][l], D, 2048, B.fresh())
    for j in range(8):
        p = B.nxt(pbig)
        for kc in range(8):
            B.mm(V(p.t[:, 0:256], p[:].keys), V(wkv.t[:, kc, j * 128:(j + 1) * 128], wkv[:].keys),
                 V(memT.t[:, kc, :, :], memT[:, kc].keys), start=(kc == 0), stop=(kc == 7))
        B.copy("act" if j % 2 == 0 else "dve", kt[:, j, :], V(p.t[:, 0:256], p[:].keys))
    for m in range(2):
        for n in range(2):
            p = B.nxt(pbig)
            for kc in range(8):
                B.mm(p[:], memT[:, kc, m, :],
                     V(wkv.t[:, kc, 1024 + n * 512:1024 + (n + 1) * 512], wkv[:].keys),
                     start=(kc == 0), stop=(kc == 7))
            B.copy("act" if n == 0 else "dve", V(vv.t[:, m, n * 512:(n + 1) * 512], vv[:, m].keys),
                   p[:])
    B.S.barrier()
    B.pstack.close()
    B.pstack = outer


def phase_xa(B, C, io, l, xin, xout, ntok, final):
    T = 512
    NS = T // 128
    nt = ntok // T
    B.phase_begin()
    ident, ones = C["ident"], C["ones"]
    KTl = B.tile("KT%d" % l, [128, 8, 256], BF16, kaxes=1)
    VVl = B.tile("VV%d" % l, [128, 2, 1024], BF16, kaxes=1)
    mem_kv(B, C, io, l, KTl, VVl)
    g = bcast_row(B, "xag", io["xa_norm_g"][l:l + 1, :], D, B.fresh())
    gf = bcast_row(B, "fng", io["final_norm_g"], D, B.fresh()) if final else None
    wqg = load_w_bf16(B, "wqg", io["xa_w_qg"][l], D, 2048, B.fresh())
    wo = load_w_bf16(B, "wo", io["xa_w_o"][l], D, D, B.fresh())
    xts = B.pool("xt", 2, [128, NS, D], F32, kaxes=1)
    sq = B.tile("sq", [128, D], F32)
    ss = B.tile("ss", [128, 2 * NS], F32, kaxes=1)
    rstd = B.tile("rstd", [128, 2 * NS], F32, kaxes=1)
    hn = B.pool("hn", 2, [128, D], BF16)
    hnTs = B.pool("hnT", 2, [128, 8, NS, 128], BF16, kaxes=2)
    qT = B.tile("qT", [128, 8, T], BF16, kaxes=1)
    sgT = B.tile("sgT", [128, 8, T], BF16, kaxes=1)
    Et = B.pool("E", 4, [128, T], BF16)
    rs = B.pool("rs", 2, [128, T], F32)
    wt = B.pool("wt", 2, [128, T], F32)
    ogT = B.tile("ogT", [128, 8, T], BF16, kaxes=1)
    pbig = B.pool("pb", 6, [128, 512], F32, space="psum")
    ptr = B.pool("ptr", 2, [128, 4, 128], BF16, space="psum")

    def load(i):
        xt = xts[i % 2]
        B.dma("sp", xt[:], rows_view(xin, i * T, NS), "ld_x%d" % (i % 2))

    def stage_a(i):
        xt = xts[i % 2]
        hT = hnTs[i % 2]
        for s in range(NS):
            h = B.nxt(hn)
            rmsnorm_tile(B, xt[:, s, :], ss[:, s:s + 1], rstd[:, s:s + 1], g, h[:], sq[:])
            for kg in range(2):
                p = B.nxt(ptr)
                for j in range(4):
                    kc = kg * 4 + j
                    B.transpose(V(p.t[:, j, :], p[:].keys), V(h.t[:, kc * 128:(kc + 1) * 128], h[:].keys),
                                ident[:])
                B.copy("act" if kg == 0 else "dve", hT[:, kg * 4:(kg + 1) * 4, s, :], p[:])

    def stage_b(i):
        hT = hnTs[i % 2]
        for j in range(16):
            p = B.nxt(pbig)
            for kc in range(8):
                B.mm(p[:], V(wqg.t[:, kc, j * 128:(j + 1) * 128], wqg[:].keys),
                     V(hT.t[:, kc, :, :], hT[:, kc].keys), start=(kc == 0), stop=(kc == 7))
            if j < 8:
                B.ts("dve", qT[:, j, :], p[:], 1.0 / 16.0, None, ALU.mult)
            else:
                B.act(sgT[:, j - 8, :], p[:], AF.Silu)

    def stage_c(i):
        for hh in range(4):
            es = []
            for m in range(2):
                p = B.nxt(pbig)
                for dc in range(2):
                    B.mm(p[:], V(KTl.t[:, 2 * hh + dc, m * 128:(m + 1) * 128], KTl[:, 2 * hh + dc].keys),
                         qT[:, 2 * hh + dc, :], start=(dc == 0), stop=(dc == 1))
                e = B.nxt(Et)
                B.act(e[:], p[:], AF.Exp)
                es.append(e)
            psum = B.nxt(pbig)
            for m in range(2):
                B.mm(psum[:], ones[:], es[m][:], start=(m == 0), stop=(m == 1))
            r = B.nxt(rs)
            B.recip(r[:], psum[:])
            for dc in range(2):
                p = B.nxt(pbig)
                for m in range(2):
                    c0 = hh * 256 + dc * 128
                    B.mm(p[:], V(VVl.t[:, m, c0:c0 + 128], VVl[:, m].keys), es[m][:],
                         start=(m == 0), stop=(m == 1))
                w = B.nxt(wt)
                B.tt("pool", w[:], r[:], sgT[:, 2 * hh + dc, :], ALU.mult)
                B.tt("dve", ogT[:, 2 * hh + dc, :], p[:], w[:], ALU.mult)

    def stage_d(i):
        xt = xts[i % 2]
        for s in range(NS):
            for n in range(2):
                p = B.nxt(pbig)
                for kc in range(8):
                    B.mm(p[:], V(ogT.t[:, kc, s * 128:(s + 1) * 128], ogT[:, kc].keys),
                         V(wo.t[:, kc, n * 512:(n + 1) * 512], wo[:].keys), start=(kc == 0), stop=(kc == 7))
                xv = V(xt.t[:, s, n * 512:(n + 1) * 512], xt[:, s].keys)
                B.tt("dve", xv, xv, p[:], ALU.add)
            if final:
                rmsnorm_tile(B, xt[:, s, :], ss[:, NS + s:NS + s + 1], rstd[:, NS + s:NS + s + 1], gf,
                             xt[:, s, :], sq[:])
        B.dma("sp", rows_view(xout, i * T, NS), xt[:], "st_o%d" % (i % 2))

    load(0)
    if nt > 1:
        load(1)
    stage_a(0)
    for i in range(nt):
        stage_b(i)
        if i + 1 < nt:
            stage_a(i + 1)
        stage_c(i)
        stage_d(i)
        if i + 2 < nt:
            load(i + 2)
    B.phase_end()


def phase_l1(B, C, io, xin, xout):
    T = 256
    NS = 2
    nt = EXT // T
    NH = HALO // T
    B.phase_begin()
    ident, ones = C["ident"], C["ones"]
    g = bcast_row(B, "l1g", io["od_norm_g"], D, B.fresh())
    win = load_w_bf16(B, "l1win", io["od_w_in"][0], D, 4096, B.fresh())
    wo = load_w_bf16(B, "l1wo", io["od_w_out"][0], D, D, B.fresh())
    bias = B.tile("l1bias", [128, 80, 128], BF16)
    bsrc = io["c_bias"].rearrange("t k q -> k t q")
    bkey = B.fresh()
    for t0 in range(0, 80, 8):
        B.dma("pool", V(bias.t[:, t0:t0 + 8, :], bias[:].keys), bsrc[:, t0:t0 + 8, :], bkey)
    hmask = B.tile("l1hm", [128, 128], BF16)
    B.dma("pool", hmask[:], io["c_hmask"][:, :], B.fresh())
    xts = B.pool("l1xt", 2, [128, NS, D], F32, kaxes=1)
    sq = B.tile("l1sq", [128, D], F32)
    ss = B.tile("l1ss", [128, NS], F32, kaxes=1)
    rstd = B.tile("l1rstd", [128, NS], F32, kaxes=1)
    hn = B.pool("l1hn", 2, [128, D], BF16)
    hnT = B.tile("l1hnT", [128, 8, NS, 128], BF16, kaxes=2)
    KTr = B.tile("l1KT", [128, 8, 8, 128], BF16, kaxes=2)
    Vr = B.tile("l1V", [128, 8, D], BF16, kaxes=1)
    qTA = B.tile("l1qA", [128, 8, T], BF16, kaxes=1)
    qTB = B.tile("l1qB", [128, 8, T], BF16, kaxes=1)
    sgT = B.tile("l1sg", [128, 8, T], BF16, kaxes=1)
    ogT = B.tile("l1og", [128, 8, T], BF16, kaxes=1)
    Ep = B.pool("l1E", 4, [128, 640], BF16)
    rp = B.pool("l1r", 2, [128, 256], F32)
    wp = B.pool("l1w", 2, [128, 128], F32)
    Sall = B.tile("l1S", [128, 2, 1024], F32, kaxes=1, space="psum")
    scnt = [0]
    pb = B.pool("l1pb", 2, [128, 512], F32, space="psum")
    OSp = B.pool("l1OS", 1, [128, 512], F32, space="psum")
    ptr2 = B.tile("l1ptr", [128, 1, 4, 128], BF16, kaxes=1, space="psum")
    pcnt = [0]
    B.memset("pool", qTA[:], 0.0)
    B.memset("pool", qTB[:], 0.0)

    def load(i):
        B.dma("sp", xts[i % 2][:], rows_view(xin, i * T, NS), "ld_x%d" % (i % 2))

    load(0)
    cnt = 0
    for i in range(nt):
        if i + 1 < nt:
            load(i + 1)
        xt = xts[i % 2]
        own = i >= NH
        for s in range(NS):
            h = B.nxt(hn)
            rmsnorm_tile(B, xt[:, s, :], ss[:, s:s + 1], rstd[:, s:s + 1], g, h[:], sq[:])
            for kg in range(2):
                pi = 0
                pcnt[0] += 1
                for j in range(4):
                    kc = kg * 4 + j
                    B.transpose(V(ptr2.t[:, pi, j, :], ptr2[:, pi].keys),
                                V(h.t[:, kc * 128:(kc + 1) * 128], h[:].keys), ident[:])
                B.copy("act" if kg == 0 else "dve", hnT[:, kg * 4:(kg + 1) * 4, s, :], ptr2[:, pi])
        sl0 = (2 * i) % 8
        for j in range(8):
            p = B.nxt(pb)
            pv = V(p.t[:, 0:T], p[:].keys)
            for kc in range(8):
                B.mm(pv, V(win.t[:, kc, 1024 + j * 128:1024 + (j + 1) * 128], win[:].keys),
                     V(hnT.t[:, kc, :, :], hnT[:, kc].keys), start=(kc == 0), stop=(kc == 7))
            cnt += 1
            B.copy("act" if cnt % 2 else "dve",
                   V(KTr.t[:, j, sl0:sl0 + 2, :], KTr[:, j, sl0:sl0 + 2].keys), pv)
        for s in range(NS):
            for n in range(2):
                p = B.nxt(pb)
                for kc in range(8):
                    B.mm(p[:], hnT[:, kc, s, :],
                         V(win.t[:, kc, 2048 + n * 512:2048 + (n + 1) * 512], win[:].keys),
                         start=(kc == 0), stop=(kc == 7))
                cnt += 1
                B.copy("act" if cnt % 2 else "dve",
                       V(Vr.t[:, sl0 + s, n * 512:(n + 1) * 512], Vr[:, sl0 + s].keys), p[:])
        if not own:
            continue
        for j in range(8):
            p = B.nxt(pb)
            pv = V(p.t[:, 0:T], p[:].keys)
            for kc in range(8):
                B.mm(pv, V(win.t[:, kc, j * 128:(j + 1) * 128], win[:].keys),
                     V(hnT.t[:, kc, :, :], hnT[:, kc].keys), start=(kc == 0), stop=(kc == 7))
            B.ts("dve", V(qTA.t[0:64, j, :], qTA[:, j].keys), V(p.t[0:64, 0:T], p[:].keys), 0.125, None,
                 ALU.mult)
            B.ts("dve", V(qTB.t[64:128, j, :], qTB[:, j].keys), V(p.t[64:128, 0:T], p[:].keys), 0.125, None,
                 ALU.mult)
        for j in range(8):
            p = B.nxt(pb)
            pv = V(p.t[:, 0:T], p[:].keys)
            for kc in range(8):
                B.mm(pv, V(win.t[:, kc, 3072 + j * 128:3072 + (j + 1) * 128], win[:].keys),
                     V(hnT.t[:, kc, :, :], hnT[:, kc].keys), start=(kc == 0), stop=(kc == 7))
            B.act(sgT[:, j, :], pv, AF.Silu)
        for s in range(NS):
            u = 2 * i + s
            qc = slice(s * 128, (s + 1) * 128)
            for j in range(8):
                Es = []
                for hb in range(2):
                    hd = 2 * j + hb
                    qq = qTA if hb == 0 else qTB
                    sb_ = scnt[0] % 2
                    scnt[0] += 1
                    Skeys = Sall[:, sb_].keys
                    for dl in range(5):
                        uk = u - dl
                        slot = uk % 8
                        blk = V(Sall.t[:, sb_, dl * 128:(dl + 1) * 128], Skeys)
                        halo = uk < HALO // 128
                        B.mm(blk, KTr[:, j, slot, :], V(qq.t[:, j, qc], qq[:, j].keys), start=True, stop=False)
                        B.mm(blk, ident[:], V(bias.t[:, hd * 5 + dl, :], bias[:].keys), start=False,
                             stop=not halo)
                        if halo:
                            B.mm(blk, ident[:], hmask[:], start=False, stop=True)
                    E = B.nxt(Ep)
                    B.act(E[:], V(Sall.t[:, sb_, 0:640], Skeys), AF.Exp)
                    Es.append(E)
                OS = B.nxt(OSp)
                for hb in range(2):
                    for dl in range(5):
                        slot = (u - dl) % 8
                        B.mm(V(OS.t[:, hb * 128:(hb + 1) * 128], OS[:].keys),
                             V(Vr.t[:, slot, j * 128:(j + 1) * 128], Vr[:, slot].keys),
                             V(Es[hb].t[:, dl * 128:(dl + 1) * 128], Es[hb][:].keys),
                             start=(dl == 0), stop=(dl == 4))
                for hb in range(2):
                    for dl in range(5):
                        B.mm(V(OS.t[:, 256 + hb * 128:256 + (hb + 1) * 128], OS[:].keys), ones[:],
                             V(Es[hb].t[:, dl * 128:(dl + 1) * 128], Es[hb][:].keys),
                             start=(dl == 0), stop=(dl == 4))
                r = B.nxt(rp)
                B.recip(r[:], V(OS.t[:, 256:512], OS[:].keys))
                w = B.nxt(wp)
                for hb in range(2):
                    pr = slice(64 * hb, 64 * hb + 64)
                    B.tt("pool", V(w.t[pr, :], w[:].keys), V(r.t[pr, hb * 128:(hb + 1) * 128], r[:].keys),
                         V(sgT.t[pr, j, qc], sgT[:, j].keys), ALU.mult)
                for hb in range(2):
                    pr = slice(64 * hb, 64 * hb + 64)
                    B.tt("dve", V(ogT.t[pr, j, qc], ogT[:, j].keys),
                         V(OS.t[pr, hb * 128:(hb + 1) * 128], OS[:].keys), V(w.t[pr, :], w[:].keys), ALU.mult)
        for s in range(NS):
            for n in range(2):
                p = B.nxt(pb)
                for kc in range(8):
                    B.mm(p[:], V(ogT.t[:, kc, s * 128:(s + 1) * 128], ogT[:, kc].keys),
                         V(wo.t[:, kc, n * 512:(n + 1) * 512], wo[:].keys), start=(kc == 0), stop=(kc == 7))
                xv = V(xt.t[:, s, n * 512:(n + 1) * 512], xt[:, s].keys)
                B.tt("dve", xv, xv, p[:], ALU.add)
        B.dma("sp", rows_view(xout, (i - NH) * T, NS), xt[:], "st_o%d" % (i % 2))
    B.phase_end()


TWO_PI = 6.283185307179586
DEBUG_STOP = None
DEBUG_NT = None


def load_cols(B, name, dram_row, key):
    t = B.tile(name, [128, 4], F32)
    B.dma("sp", t[:], dram_row.rearrange("o (c p) -> p (o c)", p=128), key,
          allow_slow_non_contiguous=True)
    return t


def phase_l0(B, C, io, xin, xout):
    T = 256
    NS = 2
    T5 = 128
    npre = PRE // T
    nt = (PRE + EXT) // T
    B.phase_begin()
    ident, ones = C["ident"], C["ones"]
    g = bcast_row(B, "l0g", io["ev_norm_g"], D, B.fresh())
    win = load_w_bf16(B, "l0win", io["ev_w_in"][0], D, 2560, B.fresh())
    wo = load_w_bf16(B, "l0wo", io["ev_w_out"][0], D, D, B.fresh())
    gluw = load_w_bf16(B, "l0gluw", io["ev_s5_glu_w"][0], 512, 512, B.fresh())
    vecs = B.tile("l0vecs", [128, 20], F32)
    lrli = B.tile("l0lrli", [128, 32], F32)

    class _Sub:
        def __init__(self, t, c0, n):
            self.t = _SubT(t, c0)
            self._t = t

        def __getitem__(self, idx):
            return self._t[:]

    class _SubT:
        def __init__(self, t, c0):
            self.tt_, self.c0 = t, c0

        def __getitem__(self, idx):
            p, c = idx
            return self.tt_.t[p, self.c0 + c.start:self.c0 + c.stop]

    Dt, glub, convb, lng, lnb = [_Sub(vecs, 4 * i, 4) for i in range(5)]
    onesM = B.tile("l0onesM", [128, 128], BF16)
    B.memset("pool", onesM[:], 1.0 / 512.0)
    CS = B.tile("l0CS", [128, 16, 2, T5], F32)
    BT = [B.tile("l0BT%d" % ri, [128, 16, 128], BF16) for ri in range(2)]
    CT = [B.tile("l0CT%d" % ri, [128, 16, 128], BF16) for ri in range(3)]
    DG = B.tile("l0DG", [128, 4, 31, 128], BF16)
    mag = B.tile("l0mag", [128, 16], F32)
    cT = B.tile("l0cT", [128, 16], F32)
    sT = B.tile("l0sT", [128, 16], F32)
    init = [B.tile("l0init%d" % ri, [128, 16], F32, kaxes=1) for ri in range(2)]
    B.memset("pool", init[0][:], 0.0)
    B.memset("pool", init[1][:], 0.0)

    outer = B.pstack
    B.pstack = contextlib.ExitStack()
    pps = B.tile("l0pps", [128, 4, 128], BF16, space="psum")
    cnt = [0]

    def small(name):
        return B.tile(name, [128, 16], F32)

    ident32 = B.tile("l0id32", [128, 128], F32)
    B.dma("sp", ident32[:], io["c_ident"][:, :], B.fresh())
    Vst = B.tile("l0Vst", [32, 128], F32)
    vkey = B.fresh()
    for vi, nm in enumerate(("ev_s5_d", "ev_s5_glu_b", "ev_conv_b", "ev_conv_ln_g", "ev_conv_ln_b")):
        B.dma("sp", V(Vst.t[4 * vi:4 * vi + 4, :], Vst[:].keys),
              io[nm].rearrange("o (c p) -> (o c) p", p=128), vkey)
    LL = B.tile("l0LL", [32, 128], F32)
    lkey2 = B.fresh()
    B.dma("sp", V(LL.t[0:16, :], LL[:].keys),
          io["ev_s5_lambda_re"][0].rearrange("(k g) p -> k (g p)", g=2), lkey2)
    B.dma("sp", V(LL.t[16:32, :], LL[:].keys),
          io["ev_s5_lambda_im"][0].rearrange("(k g) p -> k (g p)", g=2), lkey2)
    pv32 = B.tile("l0pv32", [128, 64], F32, space="psum")
    B.S.op("pe", lambda: B.nc.tensor.transpose(out=pv32.t[:, 0:32], in_=LL.t[0:32, :],
                                               identity=ident32.t[0:32, 0:32]),
           LL[:].keys + ident32[:].keys, pv32[:].keys)
    B.S.op("pe", lambda: B.nc.tensor.transpose(out=pv32.t[:, 32:52], in_=Vst.t[0:20, :],
                                               identity=ident32.t[0:20, 0:20]),
           Vst[:].keys + ident32[:].keys, pv32[:].keys)
    B.copy("dve", lrli[:], V(pv32.t[:, 0:32], pv32[:].keys))
    B.copy("dve", vecs[:], V(pv32.t[:, 32:52], pv32[:].keys))
    lr = _Sub(lrli, 0, 16)
    li = _Sub(lrli, 16, 16)
    lr = V(lrli.t[:, 0:16], lrli[:].keys)
    li = V(lrli.t[:, 16:32], lrli[:].keys)
    LB = bcast_row(B, "l0LB", io["ev_s5_log_dt"], 32, B.fresh())
    ldt = small("ldt")
    for gg in range(2):
        pr = slice(64 * gg, 64 * gg + 64)
        B.copy("dve", V(ldt.t[pr, :], ldt[:].keys),
               V(LB.t[pr, :].rearrange("p (k g) -> p k g", g=2)[:, :, gg], LB[:].keys))
    if DEBUG_STOP == "loads":
        B.S.mute = True
    dt = small("dt")
    B.act(dt[:], ldt[:], AF.Exp)
    lrdt = small("lrdt")
    B.tt("dve", lrdt[:], lr, dt[:], ALU.mult)
    B.act(mag[:], lrdt[:], AF.Exp)
    th = small("th")
    B.tt("dve", th[:], li, dt[:], ALU.mult)
    tq = small("tq")
    B.ts("dve", tq[:], th[:], 1.0 / TWO_PI, None, ALU.mult)
    tqi = B.tile("tqi", [128, 16], I32)
    B.copy("dve", tqi[:], tq[:])
    tqf = small("tqf")
    B.copy("dve", tqf[:], tqi[:])
    thr = small("thr")
    B.stt("dve", thr[:], tqf[:], -TWO_PI, th[:], ALU.mult, ALU.add)
    ath = small("ath")
    B.ts("dve", ath[:], thr[:], -1.0, None, ALU.mult)
    B.tt("dve", ath[:], ath[:], thr[:], ALU.max)
    cm, sm = small("c1"), small("s1")
    B.act(cm[:], ath[:], AF.Sin, bias=C["halfpi"][:, 0:1], scale=-1.0)
    B.act(sm[:], thr[:], AF.Sin)
    if DEBUG_STOP == "trig":
        B.S.mute = True
    abre, abim = small("abre"), small("abim")
    B.tt("dve", abre[:], mag[:], cm[:], ALU.mult)
    B.tt("dve", abim[:], mag[:], sm[:], ALU.mult)
    den, t0, t1 = small("den"), small("t0"), small("t1")
    B.tt("dve", den[:], lr, lr, ALU.mult)
    B.tt("dve", t0[:], li, li, ALU.mult)
    B.tt("dve", den[:], den[:], t0[:], ALU.add)
    B.recip(den[:], den[:])
    nr = small("nr")
    B.ts("dve", nr[:], abre[:], -1.0, None, ALU.add)
    cre, cim = small("cre"), small("cim")
    B.tt("dve", t0[:], nr[:], lr, ALU.mult)
    B.tt("dve", t1[:], abim[:], li, ALU.mult)
    B.tt("dve", t0[:], t0[:], t1[:], ALU.add)
    B.tt("dve", cre[:], t0[:], den[:], ALU.mult)
    B.tt("dve", t0[:], abim[:], lr, ALU.mult)
    B.tt("dve", t1[:], nr[:], li, ALU.mult)
    B.tt("dve", t0[:], t0[:], t1[:], ALU.subtract)
    B.tt("dve", cim[:], t0[:], den[:], ALU.mult)
    if DEBUG_STOP == "abar":
        B.S.mute = True
    B.memset("pool", V(CS.t[:, :, 0, 0:1], CS[:].keys), 1.0)
    B.memset("pool", V(CS.t[:, :, 1, 0:1], CS[:].keys), 0.0)
    tb = [B.tile("l0tb%d" % i, [128, 16, 64], F32) for i in range(4)]

    def bc(t, n):
        return V(t.t[:, :].rearrange("p (k o) -> p k o", o=1).broadcast_to([128, 16, n]), t[:].keys)

    for m in range(7):
        n = 1 << m
        cosn = V(CS.t[:, :, 0, 0:n], CS[:].keys)
        sinn = V(CS.t[:, :, 1, 0:n], CS[:].keys)
        tv = [V(t.t[:, :, 0:n], t[:].keys) for t in tb]
        B.tt("dve", tv[0], cosn, bc(cm, n), ALU.mult)
        B.tt("dve", tv[1], sinn, bc(sm, n), ALU.mult)
        B.tt("dve", tv[2], cosn, bc(sm, n), ALU.mult)
        B.tt("dve", tv[3], sinn, bc(cm, n), ALU.mult)
        B.tt("dve", V(CS.t[:, :, 0, n:2 * n], CS[:].keys), tv[0], tv[1], ALU.subtract)
        B.tt("dve", V(CS.t[:, :, 1, n:2 * n], CS[:].keys), tv[2], tv[3], ALU.add)
        c2, s2, cs2 = small("c2_%d" % m), small("s2_%d" % m), small("cs_%d" % m)
        B.tt("dve", c2[:], cm[:], cm[:], ALU.mult)
        B.tt("dve", s2[:], sm[:], sm[:], ALU.mult)
        B.tt("dve", cs2[:], cm[:], sm[:], ALU.mult)
        if m < 6:
            cmn, smn = small("cm_%d" % m), small("sm_%d" % m)
        else:
            cmn, smn = cT, sT
        B.tt("dve", cmn[:], c2[:], s2[:], ALU.subtract)
        B.ts("dve", smn[:], cs2[:], 2.0, None, ALU.mult)
        cm, sm = cmn, smn
    if DEBUG_STOP == "tables":
        B.S.mute = True
    bre = B.tile("l0bre", [128, 16, 16], F32)
    bim = B.tile("l0bim", [128, 16, 16], F32)
    for bt_, nm in ((bre, "ev_s5_b_re"), (bim, "ev_s5_b_im")):
        bsrc = io[nm][0].rearrange("(k g) p c -> (g p) k c", g=2)
        bk = B.fresh()
        for k0 in range(0, 16, 4):
            B.dma("sp", V(bt_.t[:, k0:k0 + 4, :], bt_[:].keys), bsrc[:, k0:k0 + 4, :], bk)
    bbre = B.tile("l0bbre", [128, 16, 16], F32)
    bbim = B.tile("l0bbim", [128, 16, 16], F32)
    tb16 = [V(t.t[:, :, 0:16], t[:].keys) for t in tb]
    B.tt("dve", tb16[0], bre[:], bc(cre, 16), ALU.mult)
    B.tt("dve", tb16[1], bim[:], bc(cim, 16), ALU.mult)
    B.tt("dve", bbre[:], tb16[0], tb16[1], ALU.subtract)
    B.tt("dve", tb16[2], bim[:], bc(cre, 16), ALU.mult)
    B.tt("dve", tb16[3], bre[:], bc(cim, 16), ALU.mult)
    B.tt("dve", bbim[:], tb16[2], tb16[3], ALU.add)
    BD = [B.tile("l0BD%d" % ri, [128, 16, 128], BF16) for ri in range(2)]
    for ri, (bd, bb) in enumerate(zip(BD, (bbre, bbim))):
        B.memset("pool", bd[:], 0.0)
        for k in range(16):
            r = k % 4
            for gg in range(2):
                pr = slice(64 * gg, 64 * gg + 64)
                c0 = 32 * r + 16 * gg
                B.copy("pool" if gg else "dve", V(bd.t[pr, k, c0:c0 + 16], bd[:].keys),
                       V(bb.t[pr, k, :], bb[:].keys))
        for kg in range(4):
            for j in range(4):
                B.transpose(V(pps.t[:, j, :], pps[:].keys), V(bd.t[:, kg * 4 + j, :], bd[:].keys), ident[:])
            B.copy("act", V(BT[ri].t[:, kg * 4:kg * 4 + 4, :], BT[ri][:].keys), pps[:])
    if DEBUG_STOP == "bbar":
        B.S.mute = True
    for ct in CT:
        B.memset("pool", ct[:], 0.0)
    Zs = [B.tile("l0Z%d" % ri, [64, 16, 128], F32) for ri in range(2)]
    for ri, nm in enumerate(("ev_s5_c_re", "ev_s5_c_im")):
        B.memset("pool", Zs[ri][:], 0.0)
        src = io[nm][0].rearrange("(k g) co p -> g co k p", g=2)
        ckey = B.fresh()
        B.dma("sp", V(Zs[ri].t[0:16, :, 0:64], Zs[ri][:].keys), src[0], ckey)
        B.dma("sp", V(Zs[ri].t[32:48, :, 64:128], Zs[ri][:].keys), src[1], ckey)
        for k in range(16):
            r = k % 4
            Zk = Zs[ri]
            B.S.op("pe", (lambda Zk=Zk, k=k: B.nc.tensor.transpose(out=pv32.t[:, 0:64], in_=Zk.t[0:64, k, :],
                                                                  identity=ident32.t[0:64, 0:64])),
                   Zk[:].keys + ident32[:].keys, pv32[:].keys)
            for gg in range(2):
                srcv = V(pv32.t[:, 32 * gg:32 * gg + 16], pv32[:].keys)
                c0 = 32 * r + 16 * gg
                if ri == 0:
                    B.copy("act", V(CT[0].t[:, k, c0:c0 + 16], CT[0][:].keys), srcv)
                    B.ts("dve", V(CT[1].t[:, k, c0:c0 + 16], CT[1][:].keys), srcv, -1.0, None, ALU.mult)
                else:
                    B.ts("dve", V(CT[2].t[:, k, c0:c0 + 16], CT[2][:].keys), srcv, -1.0, None, ALU.mult)
    if DEBUG_STOP == "cmat":
        B.S.mute = True
    cw = B.tile("l0cw", [31, 512], F32)
    B.dma("sp", cw[:], io["ev_conv_w"][0], B.fresh())
    wk = B.tile("l0wk", [128, 4, 31], F32)
    for cc in range(4):
        B.S.op("pe", (lambda cc=cc: B.nc.tensor.transpose(out=pv32.t[:, 0:31], in_=cw.t[0:31, cc * 128:(cc + 1) * 128],
                                                          identity=ident32.t[0:31, 0:31])),
               cw[:].keys + ident32[:].keys, pv32[:].keys)
        B.copy("act", V(wk.t[:, cc, :], wk[:].keys), V(pv32.t[:, 0:31], pv32[:].keys))
    for cc in range(4):
        for k in range(31):
            B.ts("dve" if (k % 2) else "pool", V(DG.t[:, cc, k, :], DG[:].keys), ident[:],
                 V(wk.t[:, cc, k:k + 1], wk[:].keys), None, ALU.mult)
    if DEBUG_STOP in ("loads", "trig", "abar", "tables", "bbar", "cmat"):
        B.S.mute = False
        B.phase_end()
        B.pstack = outer
        B.phase_end()
        return
    B.S.barrier()
    B.pstack.close()
    B.pstack = outer

    if DEBUG_STOP == "prep":
        B.phase_end()
        return
    xts = B.pool("l0xt", 2, [128, NS, D], F32, kaxes=1)
    ss = B.tile("l0ss", [128, NS], F32, kaxes=1)
    rstd = B.tile("l0rstd", [128, NS], F32, kaxes=1)
    hn = B.pool("l0hn", 2, [128, D], BF16)
    hnT = B.tile("l0hnT", [128, 8, NS, 128], BF16, kaxes=2)
    u32 = B.tile("l0u32", [128, 4, T], F32, kaxes=1)
    ubf = B.tile("l0ubf", [128, 4, T], BF16, kaxes=1)
    sga = B.tile("l0sga", [128, 4, T], BF16, kaxes=1)
    sgb = B.tile("l0sgb", [128, 4, T], BF16, kaxes=1)
    vbf = B.tile("l0vbf", [128, 4, 30 + T], BF16, kaxes=1)
    B.memset("pool", vbf[:], 0.0)
    sgl = B.pool("l0sgl", 2, [128, T], F32)
    A4 = B.pool("l0A4", 2, [128, 2, 2, T5], F32)
    nubf = B.tile("l0nubf", [128, 4, T], BF16, kaxes=1)
    rl_all = B.tile("l0rl", [128, 16, 2], F32)
    cr16 = B.tile("l0cr16", [128, 4, 16], F32)
    G = B.pool("l0G", 2, [128, 2, T5], F32)
    R = B.pool("l0R", 2, [128, 2, T5], F32)
    PA = B.pool("l0PA", 2, [128, 2, T5], BF16)
    PB = B.pool("l0PB", 2, [128, 2, T5], BF16)
    y32 = B.pool("l0y32", 1, [128, T], F32)
    z32 = B.tile("l0z32", [128, 4, T], F32, kaxes=1)
    zbf = B.tile("l0zbf", [128, 4, T], BF16, kaxes=1)
    sig = sgl
    c32 = B.tile("l0c32", [128, 4, T], F32, kaxes=1)
    cbf = B.tile("l0cbf", [128, 4, T], BF16, kaxes=1)
    csq = B.tile("l0csq", [128, 4, T], BF16, kaxes=1)
    mus = B.tile("l0mus", [128, T], F32)
    var = B.tile("l0var", [128, T], F32)
    rsd = B.tile("l0rsd", [128, T], F32)
    cn = B.pool("l0cn", 2, [128, T], F32)
    yab = B.tile("l0yab", [128, 8, T], BF16, kaxes=1)
    pb = B.pool("l0pb", 2, [128, 512], F32, space="psum")
    pS = B.pool("l0pS", 2, [128, 2, 2, T5], F32, space="psum")
    yps = B.pool("l0yps", 1, [128, T], F32, space="psum")
    pst = B.pool("l0pst", 2, [128, T], F32, space="psum")
    ptr = B.tile("l0ptr", [128, 4, 128], BF16, space="psum")

    def load(i):
        B.dma("sp", xts[i % 2][:], rows_view(xin, i * T, NS), "ld_x%d" % (i % 2))

    def proj(c):
        p = B.nxt(pb)
        pv = V(p.t[:, 0:T], p[:].keys)
        for kc in range(8):
            B.mm(pv, V(win.t[:, kc, c * 128:(c + 1) * 128], win[:].keys),
                 V(hnT.t[:, kc, :, :], hnT[:, kc].keys), start=(kc == 0), stop=(kc == 7))
        return pv

    load(0)
    for i in range(nt):
        if DEBUG_NT is not None and i >= DEBUG_NT:
            break
        if i + 1 < nt:
            load(i + 1)
        xt = xts[i % 2]
        pre = i < npre
        lastpre = i == npre - 1
        for s in range(NS):
            h = B.nxt(hn)
            rmsnorm_tile(B, xt[:, s, :], ss[:, s:s + 1], rstd[:, s:s + 1], g, h[:], h[:])
            for kg in range(2):
                for j in range(4):
                    kc = kg * 4 + j
                    B.transpose(V(ptr.t[:, j, :], ptr[:].keys), V(h.t[:, kc * 128:(kc + 1) * 128], h[:].keys),
                                ident[:])
                B.copy("act" if kg == 0 else "dve", hnT[:, kg * 4:(kg + 1) * 4, s, :], ptr[:])
        for cc in range(4):
            pv = proj(cc)
            if pre:
                B.copy("act", ubf[:, cc, :], pv)
            else:
                B.copy("act", u32[:, cc, :], pv)
                B.copy("pool", ubf[:, cc, :], u32[:, cc, :])
        side = []
        if lastpre:
            for cc in range(4):
                sg = B.nxt(sgl)
                B.act(sg[:], proj(12 + cc), AF.Sigmoid)
                B.tt("dve", V(vbf.t[:, cc, 30:30 + T], vbf[:, cc].keys), proj(8 + cc), sg[:], ALU.mult)
        if not pre:
            def w_sga(cc):
                B.act(sga[:, cc, :], proj(4 + cc), AF.Silu)

            def w_v(cc):
                sg = B.nxt(sgl)
                B.act(sg[:], proj(12 + cc), AF.Sigmoid)
                B.tt("dve", V(vbf.t[:, cc, 30:30 + T], vbf[:, cc].keys), proj(8 + cc), sg[:], ALU.mult)

            def w_sgb(cc):
                B.act(sgb[:, cc, :], proj(16 + cc), AF.Silu)

            def w_conv(cc):
                p = B.nxt(pb)
                pv = V(p.t[:, 0:T], p[:].keys)
                for k in range(31):
                    B.mm(pv, V(DG.t[:, cc, k, :], DG[:].keys), V(vbf.t[:, cc, k:k + T], vbf[:, cc].keys),
                         start=(k == 0), stop=(k == 30))
                bcc = V(convb.t[:, cc:cc + 1], convb[:].keys)
                B.act(c32[:, cc, :], pv, AF.Identity, bias=bcc)
                B.act(cbf[:, cc, :], pv, AF.Identity, bias=bcc)
                B.act(csq[:, cc, :], pv, AF.Square, bias=bcc)

            def w_ln():
                B.copy("pool", V(vbf.t[:, :, 0:30], vbf[:].keys), V(vbf.t[:, :, T:T + 30], vbf[:].keys))
                pmu, pm2 = B.nxt(pst), B.nxt(pst)
                for cc in range(4):
                    B.mm(pmu[:], onesM[:], cbf[:, cc, :], start=(cc == 0), stop=(cc == 3))
                for cc in range(4):
                    B.mm(pm2[:], onesM[:], csq[:, cc, :], start=(cc == 0), stop=(cc == 3))
                B.copy("act", mus[:], pmu[:])
                B.tt("dve", var[:], mus[:], mus[:], ALU.mult)
                B.tt("dve", var[:], pm2[:], var[:], ALU.subtract)
                B.act(rsd[:], var[:], AF.Sqrt, bias=EPS)
                B.recip(rsd[:], rsd[:])

            def w_cn(cc):
                c_ = B.nxt(cn)
                B.tt("dve", c_[:], c32[:, cc, :], mus[:], ALU.subtract)
                B.tt("pool", c_[:], c_[:], rsd[:], ALU.mult)
                B.act(c_[:], c_[:], AF.Silu, bias=V(lnb.t[:, cc:cc + 1], lnb[:].keys),
                      scale=V(lng.t[:, cc:cc + 1], lng[:].keys))
                B.tt("dve", yab[:, 4 + cc, :], c_[:], sgb[:, cc, :], ALU.mult)

            for cc in range(4):
                side.append(lambda cc=cc: w_sga(cc))
            for cc in range(4):
                side.append(lambda cc=cc: w_v(cc))
            for cc in range(4):
                side.append(lambda cc=cc: w_sgb(cc))
            for cc in range(4):
                side.append(lambda cc=cc: w_conv(cc))
            side.append(w_ln)
            for cc in range(4):
                side.append(lambda cc=cc: w_cn(cc))
        for cc in range(4):
            B.ts("pool", nubf[:, cc, :], ubf[:, cc, :], -1.0, None, ALU.mult)
        its = [(h5, cc, r) for h5 in range(2) for cc in range(4) for r in range(4)]
        st = [dict() for _ in its]

        def s1(n):
            h5, cc, r = its[n]
            k = cc * 4 + r
            c5 = slice(h5 * T5, (h5 + 1) * T5)
            P = B.nxt(pS)
            uv = V(ubf.t[:, cc, c5], ubf[:, cc].keys)
            nuv = V(nubf.t[:, cc, c5], nubf[:, cc].keys)
            for (x_, y_, ri, rv) in ((0, 0, 0, uv), (0, 1, 1, nuv), (1, 0, 1, uv), (1, 1, 0, uv)):
                B.mm(V(P.t[:, x_, y_, :], P[:].keys), V(BT[ri].t[:, k, :], BT[ri][:].keys), rv)
            csb = V(CS.t[:, k:k + 1, :, :].broadcast_to([128, 2, 2, T5]), CS[:].keys)
            a_ = B.nxt(A4)
            B.tt("dve", a_[:], P[:], csb, ALU.mult)
            st[n]["a"] = a_

        def s2(n):
            a_ = st[n]["a"]
            gt = B.nxt(G)
            B.tt("pool", gt[:], V(a_.t[:, :, 0, :], a_[:].keys), V(a_.t[:, :, 1, :], a_[:].keys), ALU.subtract)
            st[n]["g"] = gt

        def s3(n):
            h5, cc, r = its[n]
            k = cc * 4 + r
            gt = st[n]["g"]
            rt = B.nxt(R)
            magk = V(mag.t[:, k:k + 1].broadcast_to([128, T5]), mag[:].keys)
            for ri in range(2):
                B.scan(V(rt.t[:, ri, :], rt[:].keys), magk, V(gt.t[:, ri, :], gt[:].keys),
                       V(init[ri].t[:, k:k + 1], init[ri][:, k].keys))
            st[n]["r"] = rt

        def s3b(n):
            h5, cc, r = its[n]
            k = cc * 4 + r
            rt = st[n]["r"]
            B.copy("pool", V(rl_all.t[:, k, :], rl_all[:].keys), V(rt.t[:, :, T5 - 1], rt[:].keys))
            if not pre:
                pa = B.nxt(PA)
                B.tt("dve", pa[:], rt[:], V(CS.t[:, k, :, :], CS[:].keys), ALU.mult)
                st[n]["pa"] = pa
                pb_ = B.nxt(PB)
                B.tt("pool", V(pb_.t[:, 0, :], pb_[:].keys), V(rt.t[:, 0, :], rt[:].keys),
                     V(CS.t[:, k, 1, :], CS[:].keys), ALU.mult)
                B.tt("pool", V(pb_.t[:, 1, :], pb_[:].keys), V(rt.t[:, 1, :], rt[:].keys),
                     V(CS.t[:, k, 0, :], CS[:].keys), ALU.mult)
                st[n]["pb"] = pb_
            if cc == 3 and r == 3:
                rlr = V(rl_all.t[:, :, 0], rl_all[:].keys)
                rli = V(rl_all.t[:, :, 1], rl_all[:].keys)
                crv = [V(cr16.t[:, j_, :], cr16[:].keys) for j_ in range(4)]
                B.tt("pool", crv[0], rli, sT[:], ALU.mult)
                B.tt("pool", crv[1], rli, cT[:], ALU.mult)
                B.tt("pool", crv[2], rlr, cT[:], ALU.mult)
                B.tt("pool", crv[3], rlr, sT[:], ALU.mult)
                B.tt("pool", init[0][:], crv[2], crv[0], ALU.subtract)
                B.tt("pool", init[1][:], crv[3], crv[1], ALU.add)

        def s4(n):
            if pre:
                return
            h5, cc, r = its[n]
            k = cc * 4 + r
            c5 = slice(h5 * T5, (h5 + 1) * T5)
            if r == 0:
                st[n]["yp"] = B.nxt(yps)
            else:
                st[n]["yp"] = st[n - 1]["yp"]
            yp = st[n]["yp"]
            pa, pb_ = st[n]["pa"], st[n]["pb"]
            yv = V(yp.t[:, 0:T5], yp[:].keys)
            B.mm(yv, V(CT[0].t[:, k, :], CT[0][:].keys), V(pa.t[:, 0, :], pa[:].keys), start=(r == 0), stop=False)
            B.mm(yv, V(CT[1].t[:, k, :], CT[1][:].keys), V(pa.t[:, 1, :], pa[:].keys), start=False, stop=False)
            B.mm(yv, V(CT[2].t[:, k, :], CT[2][:].keys), V(pb_.t[:, 0, :], pb_[:].keys), start=False, stop=False)
            B.mm(yv, V(CT[2].t[:, k, :], CT[2][:].keys), V(pb_.t[:, 1, :], pb_[:].keys), start=False,
                 stop=(r == 3))
            if r == 3:
                y = B.nxt(y32)
                yh = V(y.t[:, 0:T5], y[:].keys)
                B.stt("dve", yh, V(u32.t[:, cc, c5], u32[:, cc].keys), V(Dt.t[:, cc:cc + 1], Dt[:].keys),
                      yv, ALU.mult, ALU.add)
                B.act(V(z32.t[:, cc, c5], z32[:, cc].keys), yh, AF.Gelu_apprx_tanh)
                B.copy("pool", V(zbf.t[:, cc, c5], zbf[:, cc].keys), V(z32.t[:, cc, c5], z32[:, cc].keys))

        NI = len(its)
        for t in range(NI + 4):
            if t < NI:
                s1(t)
            if 0 <= t - 3 < NI:
                s3b(t - 3)
            if 0 <= t - 4 < NI:
                s4(t - 4)
            if 0 <= t - 1 < NI:
                s2(t - 1)
            if 0 <= t - 2 < NI:
                s3(t - 2)
            if side and t >= 1:
                side.pop(0)()
        while side:
            side.pop(0)()
        if pre:
            if lastpre:
                B.copy("pool", V(vbf.t[:, :, 0:30], vbf[:].keys), V(vbf.t[:, :, T:T + 30], vbf[:].keys))
            continue
        if DEBUG_STOP == "e_s5" and not pre:
            B.S.mute = True
        for co in range(4):
            p = B.nxt(pb)
            pv = V(p.t[:, 0:T], p[:].keys)
            for ci in range(4):
                B.mm(pv, V(gluw.t[:, ci, co * 128:(co + 1) * 128], gluw[:].keys), zbf[:, ci, :],
                     start=(ci == 0), stop=(ci == 3))
            sg = B.nxt(sig)
            B.act(sg[:], pv, AF.Sigmoid, bias=V(glub.t[:, co:co + 1], glub[:].keys))
            B.tt("pool", sg[:], sg[:], sga[:, co, :], ALU.mult)
            B.tt("dve", yab[:, co, :], z32[:, co, :], sg[:], ALU.mult)
        if DEBUG_STOP == "e_glu" and not pre:
            B.S.mute = True
        if DEBUG_STOP == "e_ln" and not pre:
            B.S.mute = True
        for s in range(NS):
            for n in range(2):
                p = B.nxt(pb)
                for kc in range(8):
                    B.mm(p[:], V(yab.t[:, kc, s * 128:(s + 1) * 128], yab[:, kc].keys),
                         V(wo.t[:, kc, n * 512:(n + 1) * 512], wo[:].keys), start=(kc == 0), stop=(kc == 7))
                xv = V(xt.t[:, s, n * 512:(n + 1) * 512], xt[:, s].keys)
                B.tt("dve", xv, xv, p[:], ALU.add)
        if DEBUG_STOP != "nostore":
            B.dma("pool" if DEBUG_STOP == "poolstore" else "sp", rows_view(xout, (i - npre) * T, NS), xt[:],
                  "st_o%d" % (i % 2))
    B.S.mute = False
    B.phase_end()


W_NAMES = ["mem_norm_g", "ev_norm_g", "ev_w_in", "ev_s5_lambda_re", "ev_s5_lambda_im", "ev_s5_log_dt",
           "ev_s5_b_re", "ev_s5_b_im", "ev_s5_c_re", "ev_s5_c_im", "ev_s5_d", "ev_s5_glu_w",
           "ev_s5_glu_b", "ev_conv_w", "ev_conv_b", "ev_conv_ln_g", "ev_conv_ln_b", "ev_w_out",
           "od_norm_g", "od_w_in", "od_rel_bias", "od_w_out", "xa_norm_g", "xa_w_qg", "xa_w_kv",
           "xa_w_o", "final_norm_g"]


def setup_consts(B, io):
    C = {}
    ident = B.tile("ident", [128, 128], BF16, persistent=True)
    B.dma("pool", ident[:], io["c_ident"][:, :], "ld_ident")
    ones = B.tile("ones", [128, 128], BF16, persistent=True)
    B.memset("pool", ones[:], 1.0)
    C["ident"] = ident
    C["ones"] = ones
    halfpi = B.tile("halfpi", [128, 1], F32, persistent=True)
    B.memset("pool", halfpi[:], 1.5707963267948966)
    C["halfpi"] = halfpi
    return C


def build_program(mode="full"):
    nc = bass.Bass("TRN2", target_bir_lowering=False)
    io = {}

    def din(name, shape):
        io[name] = nc.dram_tensor(name, list(shape), F32, kind="ExternalInput").ap()

    def dout(name, shape):
        io[name] = nc.dram_tensor(name, list(shape), F32, kind="ExternalOutput").ap()

    def dint(name, shape):
        io[name] = nc.dram_tensor(name, list(shape), F32, kind="Internal").ap()

    din("c_ident", [128, 128])
    din("mem", [MEM, D])
    din("mem_norm_g", [1, D])
    din("xa_norm_g", [2, D])
    din("xa_w_qg", [2, D, 2048])
    din("xa_w_kv", [2, D, 2048])
    din("xa_w_o", [2, D, D])
    din("final_norm_g", [1, D])
    B = Builder(nc)
    outs = []
    if mode == "xa_test":
        din("xin", [OWN, D])
        dout("out", [OWN, D])
        C = setup_consts(B, io)
        phase_xa(B, C, io, 1, io["xin"], io["out"], OWN, final=True)
        outs = ["st_o0", "st_o1"]
    if mode == "l0_test":
        din("xin", [PRE + EXT, D])
        dout("out", [EXT, D])
        for nm, shp in (("ev_norm_g", [1, D]), ("ev_w_in", [1, D, 2560]), ("ev_s5_lambda_re", [1, 32, 64]),
                        ("ev_s5_lambda_im", [1, 32, 64]), ("ev_s5_log_dt", [1, 32]),
                        ("ev_s5_b_re", [1, 32, 64, 16]), ("ev_s5_b_im", [1, 32, 64, 16]),
                        ("ev_s5_c_re", [1, 32, 16, 64]), ("ev_s5_c_im", [1, 32, 16, 64]),
                        ("ev_s5_d", [1, 512]), ("ev_s5_glu_w", [1, 512, 512]), ("ev_s5_glu_b", [1, 512]),
                        ("ev_conv_w", [1, 31, 512]), ("ev_conv_b", [1, 512]), ("ev_conv_ln_g", [1, 512]),
                        ("ev_conv_ln_b", [1, 512]), ("ev_w_out", [1, D, D])):
            din(nm, shp)
        C = setup_consts(B, io)
        phase_l0(B, C, io, io["xin"], io["out"])
        outs = [k for k in ("st_o0", "st_o1") if k in B.S.dsem]
    if mode == "l1_test":
        din("xin", [EXT, D])
        dout("out", [OWN, D])
        din("od_norm_g", [1, D])
        din("od_w_in", [1, D, 4096])
        din("od_w_out", [1, D, D])
        din("c_bias", [80, 128, 128])
        din("c_hmask", [128, 128])
        C = setup_consts(B, io)
        phase_l1(B, C, io, io["xin"], io["out"])
        outs = ["st_o0", "st_o1"]
    if mode == "full":
        din("x_ext", [PRE + EXT, D])
        dout("out", [OWN, D])
        dint("xa", [EXT, D])
        dint("xb", [EXT, D])
        dint("xc", [OWN, D])
        for nm, shp in (("ev_norm_g", [1, D]), ("ev_w_in", [1, D, 2560]), ("ev_s5_lambda_re", [1, 32, 64]),
                        ("ev_s5_lambda_im", [1, 32, 64]), ("ev_s5_log_dt", [1, 32]),
                        ("ev_s5_b_re", [1, 32, 64, 16]), ("ev_s5_b_im", [1, 32, 64, 16]),
                        ("ev_s5_c_re", [1, 32, 16, 64]), ("ev_s5_c_im", [1, 32, 16, 64]),
                        ("ev_s5_d", [1, 512]), ("ev_s5_glu_w", [1, 512, 512]), ("ev_s5_glu_b", [1, 512]),
                        ("ev_conv_w", [1, 31, 512]), ("ev_conv_b", [1, 512]), ("ev_conv_ln_g", [1, 512]),
                        ("ev_conv_ln_b", [1, 512]), ("ev_w_out", [1, D, D]),
                        ("od_norm_g", [1, D]), ("od_w_in", [1, D, 4096]), ("od_w_out", [1, D, D]),
                        ("c_bias", [80, 128, 128]), ("c_hmask", [128, 128])):
            din(nm, shp)
        C = setup_consts(B, io)
        phase_l0(B, C, io, io["x_ext"], io["xa"])
        phase_xa(B, C, io, 0, io["xa"], io["xb"], EXT, final=False)
        phase_l1(B, C, io, io["xb"], io["xc"])
        phase_xa(B, C, io, 1, io["xc"], io["out"], OWN, final=True)
        outs = ["st_o0", "st_o1"]
    B.S.flush(final_wait_keys=outs)
    B.gstack.close()
    return nc


def host_bias_tiles(rel_bias):
    k = np.arange(128)[:, None]
    q = np.arange(128)[None, :]
    out = np.empty((16, 5, 128, 128), np.float32)
    for dl in range(5):
        rel = np.clip(q - k + 128 * dl, -128, 128) + 128
        kb = (k >= 64).astype(np.int64)
        qb = (q >= 64).astype(np.int64)
        dc = -2 * dl + kb - qb
        masked = (dc > 0) | (dc < -8)
        for h in range(16):
            out[h, dl] = np.where(masked, np.float32(NEG), rel_bias[h][rel])
    return out.reshape(80, 128, 128)


_PROGRAM = None


def kernel(**inputs):
    global _PROGRAM
    if _PROGRAM is None:
        _PROGRAM = build_program("full")
    nc = _PROGRAM
    f32 = np.float32
    x = np.asarray(inputs["x"], f32)
    mem = np.asarray(inputs["mem"], f32)
    shared = {"c_ident": np.eye(128, dtype=f32),
              "c_bias": host_bias_tiles(np.asarray(inputs["od_rel_bias"], f32)[0]),
              "mem_norm_g": np.asarray(inputs["mem_norm_g"], f32)[None, :],
              "final_norm_g": np.asarray(inputs["final_norm_g"], f32)[None, :]}
    for k in W_NAMES:
        if k not in shared and k != "od_rel_bias":
            shared[k] = np.ascontiguousarray(np.asarray(inputs[k], f32))
    in_maps = []
    for c in range(NCORES):
        b, hh = divmod(c, 2)
        m = dict(shared)
        if hh == 0:
            m["x_ext"] = np.concatenate([np.zeros((OWN, D), f32), x[b, :OWN]], 0)
            m["c_hmask"] = np.full((128, 128), NEG, f32)
        else:
            m["x_ext"] = np.ascontiguousarray(x[b])
            m["c_hmask"] = np.zeros((128, 128), f32)
        m["mem"] = np.ascontiguousarray(mem[b])
        in_maps.append(m)
    res = run_bass_kernel_spmd(nc, in_maps, core_ids=list(range(NCORES)))
    out = np.empty((NB, SEQ, D), f32)
    for c in range(NCORES):
        b, hh = divmod(c, 2)
        out[b, hh * OWN:(hh + 1) * OWN] = res.results[c]["out"]
    return out
```

```python
import bisect
import contextlib
import itertools
import numpy as np
import concourse.bass as bass
import concourse.mybir as mybir
from concourse.bass_utils import run_bass_kernel_spmd

F32 = mybir.dt.float32
BF16 = mybir.dt.bfloat16
I32 = mybir.dt.int32
AF = mybir.ActivationFunctionType
ALU = mybir.AluOpType

D = 1024
SEQ = 8192
NB = 4
NCORES = 8
OWN = 4096
HALO = 512
EXT = OWN + HALO
PRE = SEQ // 2 - HALO
MEM = 256
EPS = 1e-6
NEG = -30000.0
RAW_ONLY_SAME_ENGINE = False


class _Op:
    __slots__ = ("eng", "fn", "reads", "writes", "dma", "deps", "sig", "tick", "idx", "dcount",
                 "after")


class Sched:
    NDSEM = 40

    def __init__(self, nc, stack, same_engine_sync=True):
        self.nc = nc
        self.engs = {"pe": nc.tensor, "act": nc.scalar, "dve": nc.vector,
                     "pool": nc.gpsimd, "sp": nc.sync}
        self.same = same_engine_sync
        self.ops = []
        self.flushed = 0
        self.last_write = {}
        self.readers = {}
        self.tick = {e: 0 for e in self.engs}
        self.dcount = {}
        self.waited = {e: {} for e in self.engs}
        self.esem = {e: stack.enter_context(nc.semaphore("sem_" + e)) for e in self.engs}
        self.dsem_pool = [stack.enter_context(nc.semaphore("dsem_%d" % i))
                          for i in range(self.NDSEM)]
        self.dsem = {}
        self.last_eng_op = {}
        self.last_dma_op = {}
        self.n_wait = 0
        self.sigidx = {e: [] for e in self.engs}

    mute = False

    def op(self, eng, fn, reads=(), writes=(), dma=None, after=()):
        if self.mute:
            return None
        o = _Op()
        o.eng = eng
        o.fn = fn
        o.reads = tuple(reads)
        o.writes = tuple(writes)
        o.dma = dma
        o.sig = False
        o.after = tuple(after)
        o.idx = len(self.ops)
        self.ops.append(o)
        return o

    def barrier(self):
        self.flush()
        after = [o for o in self.last_eng_op.values()] + [o for o in self.last_dma_op.values()]
        sp = self.engs["sp"]
        b = self.op("sp", lambda: sp.nop(), after=after)
        for e in ("pe", "act", "dve", "pool"):
            en = self.engs[e]
            self.op(e, (lambda en=en: en.nop()), after=[b])
        self.flush()

    def flush(self, final_wait_keys=()):
        ops = self.ops
        new = ops[self.flushed:]
        for o in new:
            deps = set()
            raw = set()
            for k in o.reads:
                if k in self.last_write:
                    deps.add(self.last_write[k])
                    raw.add(self.last_write[k])
            for k in o.writes:
                if k in self.last_write:
                    deps.add(self.last_write[k])
                for r in self.readers.get(k, ()):
                    deps.add(r)
            for a in o.after:
                deps.add(a.idx)
                raw.add(a.idx)
            deps.discard(o.idx)
            if RAW_ONLY_SAME_ENGINE:
                deps = {d for d in deps if ops[d].eng != o.eng or ops[d].dma is not None or d in raw}
            for k in o.reads:
                self.readers.setdefault(k, []).append(o.idx)
            for k in o.writes:
                self.last_write[k] = o.idx
                self.readers[k] = []
            best = {}
            dd = []
            for d in deps:
                od = ops[d]
                if od.dma is not None:
                    dd.append(d)
                    continue
                if od.eng == o.eng and (od.eng in ("pe", "sp") or not self.same):
                    continue
                if od.eng not in best or best[od.eng] < d:
                    best[od.eng] = d
            o.deps = dd
            for d in best.values():
                od = ops[d]
                if not od.sig:
                    if d >= self.flushed:
                        od.sig = True
                    else:
                        lst = self.sigidx[od.eng]
                        j = bisect.bisect_left(lst, d)
                        d = lst[j]
                o.deps.append(d)
            if o.dma is None:
                self.last_eng_op[o.eng] = o
            else:
                self.last_dma_op[o.dma] = o
        for e, o in self.last_eng_op.items():
            if o.idx >= self.flushed:
                o.sig = True
        for o in new:
            if o.dma is not None:
                if o.dma not in self.dsem:
                    self.dsem[o.dma] = self.dsem_pool[len(self.dsem)]
                self.dcount[o.dma] = self.dcount.get(o.dma, 0) + 1
                o.dcount = self.dcount[o.dma]
            elif o.sig:
                self.tick[o.eng] += 1
                o.tick = self.tick[o.eng]
                self.sigidx[o.eng].append(o.idx)
        for o in new:
            e = self.engs[o.eng]
            w = self.waited[o.eng]
            need = {}
            for d in o.deps:
                od = ops[d]
                if od.dma is not None:
                    s = self.dsem[od.dma]
                    v = 16 * od.dcount
                else:
                    s = self.esem[od.eng]
                    v = od.tick
                if need.get(s, 0) < v:
                    need[s] = v
            for s, v in need.items():
                if w.get(s, 0) < v:
                    e.wait_ge(s, v)
                    w[s] = v
                    self.n_wait += 1
            inst = o.fn()
            if o.dma is not None:
                inst.then_inc(self.dsem[o.dma], 16)
            elif o.sig:
                inst.then_inc(self.esem[o.eng], 1)
            o.fn = None
        self.flushed = len(ops)
        sp = self.engs["sp"]
        for k in final_wait_keys:
            sp.wait_ge(self.dsem[k], 16 * self.dcount[k])


class V:
    __slots__ = ("ap", "keys")

    def __init__(self, ap, keys):
        self.ap = ap
        self.keys = tuple(keys)


class Tile:
    def __init__(self, name, handle, shape, kaxes):
        self.name = name
        self.t = handle
        self.shape = list(shape)
        self.kaxes = kaxes

    def __getitem__(self, idx):
        if not isinstance(idx, tuple):
            idx = (idx,)
        ranges = []
        for a in range(self.kaxes):
            i = idx[1 + a] if len(idx) > 1 + a else slice(None)
            n = self.shape[1 + a]
            if isinstance(i, int):
                ranges.append((i,))
            else:
                ranges.append(tuple(range(*i.indices(n))))
        keys = [(self.name,) + c for c in itertools.product(*ranges)]
        return V(self.t[idx], keys)


def _k(x):
    return x.keys if isinstance(x, V) else ()


def _a(x):
    return x.ap if isinstance(x, V) else x


class Builder:
    def __init__(self, nc):
        self.nc = nc
        self.gstack = contextlib.ExitStack()
        self.S = Sched(nc, self.gstack)
        self.pstack = None
        self.uid = 0
        self.nkey = 0
        self.rr = {}

    def phase_begin(self):
        self.pstack = contextlib.ExitStack()
        self.nkey = 0

    def fresh(self):
        self.nkey += 1
        return "u%d" % self.nkey

    def phase_end(self):
        self.S.barrier()
        self.pstack.close()
        self.pstack = None

    def tile(self, name, shape, dtype=F32, kaxes=0, space="sbuf", persistent=False):
        self.uid += 1
        nm = "%s_%d" % (name, self.uid)
        st = self.gstack if persistent else self.pstack
        if space == "sbuf":
            h = st.enter_context(self.nc.sbuf_tensor(nm, list(shape), dtype))
        else:
            h = st.enter_context(self.nc.psum_tensor(nm, list(shape), dtype))
        return Tile(nm, h, shape, kaxes)

    def pool(self, name, n, shape, dtype=F32, kaxes=0, space="sbuf"):
        tiles = [self.tile("%s%d" % (name, i), shape, dtype, kaxes, space) for i in range(n)]
        self.rr[id(tiles)] = 0
        return tiles

    def nxt(self, tiles):
        i = self.rr[id(tiles)]
        self.rr[id(tiles)] = (i + 1) % len(tiles)
        return tiles[i]

    def mm(self, out, lhsT, rhs, start=True, stop=True, **kw):
        nc = self.nc
        rd = lhsT.keys + rhs.keys + (() if start else out.keys)
        self.S.op("pe", lambda: nc.tensor.matmul(out.ap, lhsT=lhsT.ap, rhs=rhs.ap, start=start,
                                                 stop=stop, **kw), rd, out.keys)

    def transpose(self, out, in_, ident):
        nc = self.nc
        self.S.op("pe", lambda: nc.tensor.transpose(out=out.ap, in_=in_.ap, identity=ident.ap),
                  in_.keys + ident.keys, out.keys)

    def act(self, out, in_, func, bias=None, scale=None, accum=None):
        nc = self.nc
        kw = {}
        if bias is not None:
            kw["bias"] = _a(bias)
        if scale is not None:
            kw["scale"] = _a(scale)
        if accum is not None:
            kw["accum_out"] = accum.ap
        self.S.op("act", lambda: nc.scalar.activation(out=out.ap, in_=in_.ap, func=func, **kw),
                  in_.keys + _k(bias) + _k(scale), out.keys + _k(accum))

    def _ve(self, eng):
        return self.nc.vector if eng == "dve" else self.nc.gpsimd

    def tt(self, eng, out, in0, in1, op):
        e = self._ve(eng)
        self.S.op(eng, lambda: e.tensor_tensor(out=out.ap, in0=in0.ap, in1=in1.ap, op=op),
                  in0.keys + in1.keys, out.keys)

    def ts(self, eng, out, in0, s1, s2=None, op0=ALU.mult, op1=None):
        e = self._ve(eng)
        kw = {}
        if op1 is not None:
            kw["op1"] = op1
        self.S.op(eng, lambda: e.tensor_scalar(out=out.ap, in0=in0.ap, scalar1=_a(s1),
                                               scalar2=_a(s2), op0=op0, **kw),
                  in0.keys + _k(s1) + _k(s2), out.keys)

    def stt(self, eng, out, in0, scalar, in1, op0, op1):
        e = self._ve(eng)
        self.S.op(eng, lambda: e.scalar_tensor_tensor(out=out.ap, in0=in0.ap, scalar=_a(scalar),
                                                      in1=in1.ap, op0=op0, op1=op1),
                  in0.keys + _k(scalar) + in1.keys, out.keys)

    def copy(self, eng, out, in_):
        if eng == "act":
            nc = self.nc
            self.S.op("act", lambda: nc.scalar.copy(out=out.ap, in_=in_.ap), in_.keys, out.keys)
        else:
            e = self._ve(eng)
            self.S.op(eng, lambda: e.tensor_copy(out=out.ap, in_=in_.ap), in_.keys, out.keys)

    def memset(self, eng, out, val):
        e = self._ve(eng)
        self.S.op(eng, lambda: e.memset(out.ap, val), (), out.keys)

    def recip(self, out, in_):
        nc = self.nc
        self.S.op("dve", lambda: nc.vector.reciprocal(out=out.ap, in_=in_.ap), in_.keys, out.keys)

    def scan(self, out, d0, d1, init, op0=ALU.mult, op1=ALU.add):
        nc = self.nc
        self.S.op("dve", lambda: nc.vector.tensor_tensor_scan(out=out.ap, data0=d0.ap, data1=d1.ap,
                                                              initial=_a(init), op0=op0, op1=op1),
                  d0.keys + d1.keys + _k(init), out.keys)

    def dma(self, eng, out, in_, key, **kw):
        e = self.S.engs[eng]
        self.S.op(eng, lambda: e.dma_start(out=_a(out), in_=_a(in_), **kw), _k(in_), _k(out),
                  dma=key)


def rows_view(dram, r0, nsub):
    return dram[r0:r0 + 128 * nsub, :].rearrange("(s p) d -> p s d", p=128)


def load_w_bf16(B, name, dram, kdim, ndim, key, c0=0, eng="pool"):
    kc = kdim // 128
    t = B.tile(name, [128, kc, ndim], BF16, kaxes=0)
    src = dram[:, c0:c0 + ndim].rearrange("(k p) n -> p k n", p=128)
    half = max(1, kc // 2)
    for i in range(0, kc, half):
        B.dma(eng, V(t.t[:, i:i + half, :], t[:].keys), src[:, i:i + half, :], key)
    return t


def bcast_row(B, name, dram_row, n, key, persistent=False):
    t = B.tile(name, [128, n], F32, persistent=persistent)
    B.dma("sp", t[:], dram_row.broadcast_to([128, n]), key)
    return t


def rmsnorm_tile(B, xt_v, ss_v, rstd_v, g_t, out_v, sq_scratch):
    B.act(sq_scratch, xt_v, AF.Square, accum=ss_v)
    B.act(rstd_v, ss_v, AF.Sqrt, bias=EPS, scale=1.0 / D)
    B.recip(rstd_v, rstd_v)
    B.stt("dve", out_v, xt_v, rstd_v, g_t[:], ALU.mult, ALU.mult)


def mem_kv(B, C, io, l, kt, vv):
    outer = B.pstack
    B.pstack = contextlib.ExitStack()
    ident = C["ident"]
    g = bcast_row(B, "memg", io["mem_norm_g"], D, B.fresh())
    mt = B.tile("memx", [128, 2, D], F32, kaxes=1)
    B.dma("sp", mt[:], rows_view(io["mem"], 0, 2), "ld_x0")
    sq = B.tile("memsq", [128, D], F32)
    ss = B.tile("memss", [128, 2], F32, kaxes=1)
    rstd = B.tile("memrs", [128, 2], F32, kaxes=1)
    mn = B.tile("memn", [128, 2, D], BF16, kaxes=1)
    memT = B.tile("memT", [128, 8, 2, 128], BF16, kaxes=2)
    pbig = B.pool("mpb", 4, [128, 512], F32, space="psum")
    ptr = B.pool("mptr", 2, [128, 4, 128], BF16, space="psum")
    for s in range(2):
        rmsnorm_tile(B, mt[:, s, :], ss[:, s:s + 1], rstd[:, s:s + 1], g, mn[:, s, :], sq[:])
        for kg in range(2):
            p = B.nxt(ptr)
            for j in range(4):
                kc = kg * 4 + j
                B.transpose(V(p.t[:, j, :], p[:].keys), mn[:, s, kc * 128:(kc + 1) * 128], ident[:])
            B.copy("act" if kg == 0 else "dve", memT[:, kg * 4:(kg + 1) * 4, s, :], p[:])
    wkv = load_w_bf16(B, "wkv%d" % l, io["xa_w_kv"][l], D, 2048, B.fresh())
    for j in range(8):
        p = B.nxt(pbig)
        for kc in range(8):
            B.mm(V(p.t[:, 0:256], p[:].keys), V(wkv.t[:, kc, j * 128:(j + 1) * 128], wkv[:].keys),
                 V(memT.t[:, kc, :, :], memT[:, kc].keys), start=(kc == 0), stop=(kc == 7))
        B.copy("act" if j % 2 == 0 else "dve", kt[:, j, :], V(p.t[:, 0:256], p[:].keys))
    for m in range(2):
        for n in range(2):
            p = B.nxt(pbig)
            for kc in range(8):
                B.mm(p[:], memT[:, kc, m, :],
                     V(wkv.t[:, kc, 1024 + n * 512:1024 + (n + 1) * 512], wkv[:].keys),
                     start=(kc == 0), stop=(kc == 7))
            B.copy("act" if n == 0 else "dve", V(vv.t[:, m, n * 512:(n + 1) * 512], vv[:, m].keys),
                   p[:])
    B.S.barrier()
    B.pstack.close()
    B.pstack = outer


def phase_xa(B, C, io, l, xin, xout, ntok, final):
    T = 512
    NS = T // 128
    nt = ntok // T
    B.phase_begin()
    ident, ones = C["ident"], C["ones"]
    KTl = B.tile("KT%d" % l, [128, 8, 256], BF16, kaxes=1)
    VVl = B.tile("VV%d" % l, [128, 2, 1024], BF16, kaxes=1)
    mem_kv(B, C, io, l, KTl, VVl)
    g = bcast_row(B, "xag", io["xa_norm_g"][l:l + 1, :], D, B.fresh())
    gf = bcast_row(B, "fng", io["final_norm_g"], D, B.fresh()) if final else None
    wqg = load_w_bf16(B, "wqg", io["xa_w_qg"][l], D, 2048, B.fresh())
    wo = load_w_bf16(B, "wo", io["xa_w_o"][l], D, D, B.fresh())
    xts = B.pool("xt", 2, [128, NS, D], F32, kaxes=1)
    sq = B.tile("sq", [128, D], F32)
    ss = B.tile("ss", [128, 2 * NS], F32, kaxes=1)
    rstd = B.tile("rstd", [128, 2 * NS], F32, kaxes=1)
    hn = B.pool("hn", 2, [128, D], BF16)
    hnTs = B.pool("hnT", 2, [128, 8, NS, 128], BF16, kaxes=2)
    qT = B.tile("qT", [128, 8, T], BF16, kaxes=1)
    sgT = B.tile("sgT", [128, 8, T], BF16, kaxes=1)
    Et = B.pool("E", 4, [128, T], BF16)
    rs = B.pool("rs", 2, [128, T], F32)
    wt = B.pool("wt", 2, [128, T], F32)
    ogT = B.tile("ogT", [128, 8, T], BF16, kaxes=1)
    pbig = B.pool("pb", 6, [128, 512], F32, space="psum")
    ptr = B.pool("ptr", 2, [128, 4, 128], BF16, space="psum")

    def load(i):
        xt = xts[i % 2]
        B.dma("sp", xt[:], rows_view(xin, i * T, NS), "ld_x%d" % (i % 2))

    def stage_a(i):
        xt = xts[i % 2]
        hT = hnTs[i % 2]
        for s in range(NS):
            h = B.nxt(hn)
            rmsnorm_tile(B, xt[:, s, :], ss[:, s:s + 1], rstd[:, s:s + 1], g, h[:], sq[:])
            for kg in range(2):
                p = B.nxt(ptr)
                for j in range(4):
                    kc = kg * 4 + j
                    B.transpose(V(p.t[:, j, :], p[:].keys), V(h.t[:, kc * 128:(kc + 1) * 128], h[:].keys),
                                ident[:])
                B.copy("act" if kg == 0 else "dve", hT[:, kg * 4:(kg + 1) * 4, s, :], p[:])

    def stage_b(i):
        hT = hnTs[i % 2]
        for j in range(16):
            p = B.nxt(pbig)
            for kc in range(8):
                B.mm(p[:], V(wqg.t[:, kc, j * 128:(j + 1) * 128], wqg[:].keys),
                     V(hT.t[:, kc, :, :], hT[:, kc].keys), start=(kc == 0), stop=(kc == 7))
            if j < 8:
                B.ts("dve", qT[:, j, :], p[:], 1.0 / 16.0, None, ALU.mult)
            else:
                B.act(sgT[:, j - 8, :], p[:], AF.Silu)

    def stage_c(i):
        for hh in range(4):
            es = []
            for m in range(2):
                p = B.nxt(pbig)
                for dc in range(2):
                    B.mm(p[:], V(KTl.t[:, 2 * hh + dc, m * 128:(m + 1) * 128], KTl[:, 2 * hh + dc].keys),
                         qT[:, 2 * hh + dc, :], start=(dc == 0), stop=(dc == 1))
                e = B.nxt(Et)
                B.act(e[:], p[:], AF.Exp)
                es.append(e)
            psum = B.nxt(pbig)
            for m in range(2):
                B.mm(psum[:], ones[:], es[m][:], start=(m == 0), stop=(m == 1))
            r = B.nxt(rs)
            B.recip(r[:], psum[:])
            for dc in range(2):
                p = B.nxt(pbig)
                for m in range(2):
                    c0 = hh * 256 + dc * 128
                    B.mm(p[:], V(VVl.t[:, m, c0:c0 + 128], VVl[:, m].keys), es[m][:],
                         start=(m == 0), stop=(m == 1))
                w = B.nxt(wt)
                B.tt("pool", w[:], r[:], sgT[:, 2 * hh + dc, :], ALU.mult)
                B.tt("dve", ogT[:, 2 * hh + dc, :], p[:], w[:], ALU.mult)

    def stage_d(i):
        xt = xts[i % 2]
        for s in range(NS):
            for n in range(2):
                p = B.nxt(pbig)
                for kc in range(8):
                    B.mm(p[:], V(ogT.t[:, kc, s * 128:(s + 1) * 128], ogT[:, kc].keys),
                         V(wo.t[:, kc, n * 512:(n + 1) * 512], wo[:].keys), start=(kc == 0), stop=(kc == 7))
                xv = V(xt.t[:, s, n * 512:(n + 1) * 512], xt[:, s].keys)
                B.tt("dve", xv, xv, p[:], ALU.add)
            if final:
                rmsnorm_tile(B, xt[:, s, :], ss[:, NS + s:NS + s + 1], rstd[:, NS + s:NS + s + 1], gf,
                             xt[:, s, :], sq[:])
        B.dma("sp", rows_view(xout, i * T, NS), xt[:], "st_o%d" % (i % 2))

    load(0)
    if nt > 1:
        load(1)
    stage_a(0)
    for i in range(nt):
        stage_b(i)
        if i + 1 < nt:
            stage_a(i + 1)
        stage_c(i)
        stage_d(i)
        if i + 2 < nt:
            load(i + 2)
    B.phase_end()


def phase_l1(B, C, io, xin, xout):
    T = 256
    NS = 2
    nt = EXT // T
    NH = HALO // T
    B.phase_begin()
    ident, ones = C["ident"], C["ones"]
    g = bcast_row(B, "l1g", io["od_norm_g"], D, B.fresh())
    win = load_w_bf16(B, "l1win", io["od_w_in"][0], D, 4096, B.fresh())
    wo = load_w_bf16(B, "l1wo", io["od_w_out"][0], D, D, B.fresh())
    bias = B.tile("l1bias", [128, 80, 128], BF16)
    bsrc = io["c_bias"].rearrange("t k q -> k t q")
    bkey = B.fresh()
    for t0 in range(0, 80, 8):
        B.dma("pool", V(bias.t[:, t0:t0 + 8, :], bias[:].keys), bsrc[:, t0:t0 + 8, :], bkey)
    hmask = B.tile("l1hm", [128, 2, 128], BF16)
    hkey = B.fresh()
    for hb in range(2):
        B.dma("pool", V(hmask.t[:, hb, :], hmask[:].keys), io["c_hmask"][:, :], hkey)
    xts = B.pool("l1xt", 2, [128, NS, D], F32, kaxes=1)
    sq = B.tile("l1sq", [128, D], F32)
    ss = B.tile("l1ss", [128, NS], F32, kaxes=1)
    rstd = B.tile("l1rstd", [128, NS], F32, kaxes=1)
    hn = B.pool("l1hn", 2, [128, D], BF16)
    hnT = B.tile("l1hnT", [128, 8, NS, 128], BF16, kaxes=2)
    KTr = B.tile("l1KT", [128, 8, 8, 128], BF16, kaxes=2)
    Vr = B.tile("l1V", [128, 8, D], BF16, kaxes=1)
    qAB = B.tile("l1qAB", [128, 8, NS, 2, 128], BF16, kaxes=1)
    sgT = B.tile("l1sg", [128, 8, T], BF16, kaxes=1)
    ogT = B.tile("l1og", [128, 8, T], BF16, kaxes=1)
    Ep = B.pool("l1E", 3, [128, 5, 2, 128], BF16)
    rp = B.pool("l1r", 2, [128, 256], F32)
    wp = B.pool("l1w", 2, [128, 128], F32)
    Sp = B.pool("l1S", 4, [128, 2, 2, 128], F32, space="psum")
    pb = B.pool("l1pb", 2, [128, 512], F32, space="psum")
    OSp = B.pool("l1OS", 1, [128, 512], F32, space="psum")
    OSp = OSp + pb
    B.rr[id(OSp)] = 0
    ptr2 = B.tile("l1ptr", [128, 1, 4, 128], BF16, kaxes=1, space="psum")
    pcnt = [0]
    B.memset("pool", qAB[:], 0.0)

    def load(i):
        B.dma("sp", xts[i % 2][:], rows_view(xin, i * T, NS), "ld_x%d" % (i % 2))

    load(0)
    cnt = 0
    for i in range(nt):
        if i + 1 < nt:
            load(i + 1)
        xt = xts[i % 2]
        own = i >= NH
        for s in range(NS):
            h = B.nxt(hn)
            rmsnorm_tile(B, xt[:, s, :], ss[:, s:s + 1], rstd[:, s:s + 1], g, h[:], sq[:])
            for kg in range(2):
                pi = 0
                pcnt[0] += 1
                for j in range(4):
                    kc = kg * 4 + j
                    B.transpose(V(ptr2.t[:, pi, j, :], ptr2[:, pi].keys),
                                V(h.t[:, kc * 128:(kc + 1) * 128], h[:].keys), ident[:])
                B.copy("act" if kg == 0 else "dve", hnT[:, kg * 4:(kg + 1) * 4, s, :], ptr2[:, pi])
        sl0 = (2 * i) % 8
        for j in range(8):
            p = B.nxt(pb)
            pv = V(p.t[:, 0:T], p[:].keys)
            for kc in range(8):
                B.mm(pv, V(win.t[:, kc, 1024 + j * 128:1024 + (j + 1) * 128], win[:].keys),
                     V(hnT.t[:, kc, :, :], hnT[:, kc].keys), start=(kc == 0), stop=(kc == 7))
            cnt += 1
            B.copy("act" if cnt % 2 else "dve",
                   V(KTr.t[:, j, sl0:sl0 + 2, :], KTr[:, j, sl0:sl0 + 2].keys), pv)
        for s in range(NS):
            for n in range(2):
                p = B.nxt(pb)
                for kc in range(8):
                    B.mm(p[:], hnT[:, kc, s, :],
                         V(win.t[:, kc, 2048 + n * 512:2048 + (n + 1) * 512], win[:].keys),
                         start=(kc == 0), stop=(kc == 7))
                cnt += 1
                B.copy("act" if cnt % 2 else "dve",
                       V(Vr.t[:, sl0 + s, n * 512:(n + 1) * 512], Vr[:, sl0 + s].keys), p[:])
        if not own:
            continue
        for j in range(8):
            p = B.nxt(pb)
            pv = V(p.t[:, 0:T], p[:].keys)
            for kc in range(8):
                B.mm(pv, V(win.t[:, kc, j * 128:(j + 1) * 128], win[:].keys),
                     V(hnT.t[:, kc, :, :], hnT[:, kc].keys), start=(kc == 0), stop=(kc == 7))
            for hb in range(2):
                pr = slice(64 * hb, 64 * hb + 64)
                B.ts("dve", V(qAB.t[pr, j, :, hb, :], qAB[:, j].keys),
                     V(p.t[pr, 0:T].rearrange("p (s q) -> p s q", s=NS), p[:].keys), 0.125, None, ALU.mult)
        for j in range(8):
            p = B.nxt(pb)
            pv = V(p.t[:, 0:T], p[:].keys)
            for kc in range(8):
                B.mm(pv, V(win.t[:, kc, 3072 + j * 128:3072 + (j + 1) * 128], win[:].keys),
                     V(hnT.t[:, kc, :, :], hnT[:, kc].keys), start=(kc == 0), stop=(kc == 7))
            B.act(sgT[:, j, :], pv, AF.Silu)
        for s in range(NS):
            u = 2 * i + s
            qc = slice(s * 128, (s + 1) * 128)
            for j in range(8):
                E = B.nxt(Ep)
                qv = V(qAB.t[:, j, s, :, :], qAB[:, j].keys)
                for grp in ((0, 1), (2, 3), (4,)):
                    St = B.nxt(Sp)
                    for sl_, dl in enumerate(grp):
                        uk = u - dl
                        slot = uk % 8
                        blk = V(St.t[:, sl_, :, :], St[:].keys)
                        halo = uk < HALO // 128
                        B.mm(blk, KTr[:, j, slot, :], qv, start=True, stop=False)
                        B.mm(blk, ident[:], V(bias.t[:, dl * 16 + 2 * j:dl * 16 + 2 * j + 2, :], bias[:].keys),
                             start=False, stop=not halo)
                        if halo:
                            B.mm(blk, ident[:], hmask[:], start=False, stop=True)
                    n_ = len(grp)
                    B.act(V(E.t[:, grp[0]:grp[0] + n_, :, :], E[:].keys), V(St.t[:, 0:n_, :, :], St[:].keys),
                          AF.Exp)
                OS = B.nxt(OSp)
                for dl in range(5):
                    slot = (u - dl) % 8
                    B.mm(V(OS.t[:, 0:256], OS[:].keys), V(Vr.t[:, slot, j * 128:(j + 1) * 128], Vr[:, slot].keys),
                         V(E.t[:, dl, :, :], E[:].keys), start=(dl == 0), stop=(dl == 4))
                for dl in range(5):
                    B.mm(V(OS.t[:, 256:512], OS[:].keys), ones[:], V(E.t[:, dl, :, :], E[:].keys),
                         start=(dl == 0), stop=(dl == 4))
                r = B.nxt(rp)
                B.recip(r[:], V(OS.t[:, 256:512], OS[:].keys))
                w = B.nxt(wp)
                for hb in range(2):
                    pr = slice(64 * hb, 64 * hb + 64)
                    B.tt("pool", V(w.t[pr, :], w[:].keys), V(r.t[pr, hb * 128:(hb + 1) * 128], r[:].keys),
                         V(sgT.t[pr, j, qc], sgT[:, j].keys), ALU.mult)
                for hb in range(2):
                    pr = slice(64 * hb, 64 * hb + 64)
                    B.tt("dve", V(ogT.t[pr, j, qc], ogT[:, j].keys),
                         V(OS.t[pr, hb * 128:(hb + 1) * 128], OS[:].keys), V(w.t[pr, :], w[:].keys), ALU.mult)
        for s in range(NS):
            for n in range(2):
                p = B.nxt(pb)
                for kc in range(8):
                    B.mm(p[:], V(ogT.t[:, kc, s * 128:(s + 1) * 128], ogT[:, kc].keys),
                         V(wo.t[:, kc, n * 512:(n + 1) * 512], wo[:].keys), start=(kc == 0), stop=(kc == 7))
                xv = V(xt.t[:, s, n * 512:(n + 1) * 512], xt[:, s].keys)
                B.tt("dve", xv, xv, p[:], ALU.add)
        B.dma("sp", rows_view(xout, (i - NH) * T, NS), xt[:], "st_o%d" % (i % 2))
    B.phase_end()


TWO_PI = 6.283185307179586
DEBUG_STOP = None
DEBUG_NT = None


def load_cols(B, name, dram_row, key):
    t = B.tile(name, [128, 4], F32)
    B.dma("sp", t[:], dram_row.rearrange("o (c p) -> p (o c)", p=128), key,
          allow_slow_non_contiguous=True)
    return t


def phase_l0(B, C, io, xin, xout):
    T = 256
    NS = 2
    T5 = 128
    npre = PRE // T
    nt = (PRE + EXT) // T
    B.phase_begin()
    ident, ones = C["ident"], C["ones"]
    g = bcast_row(B, "l0g", io["ev_norm_g"], D, B.fresh())
    win = load_w_bf16(B, "l0win", io["ev_w_in"][0], D, 2560, B.fresh())
    wo = load_w_bf16(B, "l0wo", io["ev_w_out"][0], D, D, B.fresh())
    gluw = load_w_bf16(B, "l0gluw", io["ev_s5_glu_w"][0], 512, 512, B.fresh())
    vecs = B.tile("l0vecs", [128, 20], F32)
    lrli = B.tile("l0lrli", [128, 32], F32)

    class _Sub:
        def __init__(self, t, c0, n):
            self.t = _SubT(t, c0)
            self._t = t

        def __getitem__(self, idx):
            return self._t[:]

    class _SubT:
        def __init__(self, t, c0):
            self.tt_, self.c0 = t, c0

        def __getitem__(self, idx):
            p, c = idx
            return self.tt_.t[p, self.c0 + c.start:self.c0 + c.stop]

    Dt, glub, convb, lng, lnb = [_Sub(vecs, 4 * i, 4) for i in range(5)]
    onesM = B.tile("l0onesM", [128, 128], BF16)
    B.memset("pool", onesM[:], 1.0 / 512.0)
    CS = B.tile("l0CS", [128, 16, 2, T5], F32)
    BT = [B.tile("l0BT%d" % ri, [128, 16, 128], BF16) for ri in range(2)]
    CT = [B.tile("l0CT%d" % ri, [128, 16, 128], BF16) for ri in range(3)]
    DG = B.tile("l0DG", [128, 4, 31, 128], BF16)
    mag = B.tile("l0mag", [128, 16], F32)
    cT = B.tile("l0cT", [128, 16], F32)
    sT = B.tile("l0sT", [128, 16], F32)
    init = [B.tile("l0init%d" % ri, [128, 16], F32, kaxes=1) for ri in range(2)]
    B.memset("pool", init[0][:], 0.0)
    B.memset("pool", init[1][:], 0.0)

    outer = B.pstack
    B.pstack = contextlib.ExitStack()
    pps = B.tile("l0pps", [128, 4, 128], BF16, space="psum")
    cnt = [0]

    def small(name):
        return B.tile(name, [128, 16], F32)

    ident32 = B.tile("l0id32", [128, 128], F32)
    B.dma("sp", ident32[:], io["c_ident"][:, :], B.fresh())
    Vst = B.tile("l0Vst", [32, 128], F32)
    vkey = B.fresh()
    for vi, nm in enumerate(("ev_s5_d", "ev_s5_glu_b", "ev_conv_b", "ev_conv_ln_g", "ev_conv_ln_b")):
        B.dma("sp", V(Vst.t[4 * vi:4 * vi + 4, :], Vst[:].keys),
              io[nm].rearrange("o (c p) -> (o c) p", p=128), vkey)
    LL = B.tile("l0LL", [32, 128], F32)
    lkey2 = B.fresh()
    B.dma("sp", V(LL.t[0:16, :], LL[:].keys),
          io["ev_s5_lambda_re"][0].rearrange("(k g) p -> k (g p)", g=2), lkey2)
    B.dma("sp", V(LL.t[16:32, :], LL[:].keys),
          io["ev_s5_lambda_im"][0].rearrange("(k g) p -> k (g p)", g=2), lkey2)
    pv32 = B.tile("l0pv32", [128, 64], F32, space="psum")
    B.S.op("pe", lambda: B.nc.tensor.transpose(out=pv32.t[:, 0:32], in_=LL.t[0:32, :],
                                               identity=ident32.t[0:32, 0:32]),
           LL[:].keys + ident32[:].keys, pv32[:].keys)
    B.S.op("pe", lambda: B.nc.tensor.transpose(out=pv32.t[:, 32:52], in_=Vst.t[0:20, :],
                                               identity=ident32.t[0:20, 0:20]),
           Vst[:].keys + ident32[:].keys, pv32[:].keys)
    B.copy("dve", lrli[:], V(pv32.t[:, 0:32], pv32[:].keys))
    B.copy("dve", vecs[:], V(pv32.t[:, 32:52], pv32[:].keys))
    lr = _Sub(lrli, 0, 16)
    li = _Sub(lrli, 16, 16)
    lr = V(lrli.t[:, 0:16], lrli[:].keys)
    li = V(lrli.t[:, 16:32], lrli[:].keys)
    LB = bcast_row(B, "l0LB", io["ev_s5_log_dt"], 32, B.fresh())
    ldt = small("ldt")
    for gg in range(2):
        pr = slice(64 * gg, 64 * gg + 64)
        B.copy("dve", V(ldt.t[pr, :], ldt[:].keys),
               V(LB.t[pr, :].rearrange("p (k g) -> p k g", g=2)[:, :, gg], LB[:].keys))
    if DEBUG_STOP == "loads":
        B.S.mute = True
    dt = small("dt")
    B.act(dt[:], ldt[:], AF.Exp)
    lrdt = small("lrdt")
    B.tt("dve", lrdt[:], lr, dt[:], ALU.mult)
    B.act(mag[:], lrdt[:], AF.Exp)
    th = small("th")
    B.tt("dve", th[:], li, dt[:], ALU.mult)
    tq = small("tq")
    B.ts("dve", tq[:], th[:], 1.0 / TWO_PI, None, ALU.mult)
    tqi = B.tile("tqi", [128, 16], I32)
    B.copy("dve", tqi[:], tq[:])
    tqf = small("tqf")
    B.copy("dve", tqf[:], tqi[:])
    thr = small("thr")
    B.stt("dve", thr[:], tqf[:], -TWO_PI, th[:], ALU.mult, ALU.add)
    ath = small("ath")
    B.ts("dve", ath[:], thr[:], -1.0, None, ALU.mult)
    B.tt("dve", ath[:], ath[:], thr[:], ALU.max)
    cm, sm = small("c1"), small("s1")
    B.act(cm[:], ath[:], AF.Sin, bias=C["halfpi"][:, 0:1], scale=-1.0)
    B.act(sm[:], thr[:], AF.Sin)
    if DEBUG_STOP == "trig":
        B.S.mute = True
    abre, abim = small("abre"), small("abim")
    B.tt("dve", abre[:], mag[:], cm[:], ALU.mult)
    B.tt("dve", abim[:], mag[:], sm[:], ALU.mult)
    den, t0, t1 = small("den"), small("t0"), small("t1")
    B.tt("dve", den[:], lr, lr, ALU.mult)
    B.tt("dve", t0[:], li, li, ALU.mult)
    B.tt("dve", den[:], den[:], t0[:], ALU.add)
    B.recip(den[:], den[:])
    nr = small("nr")
    B.ts("dve", nr[:], abre[:], -1.0, None, ALU.add)
    cre, cim = small("cre"), small("cim")
    B.tt("dve", t0[:], nr[:], lr, ALU.mult)
    B.tt("dve", t1[:], abim[:], li, ALU.mult)
    B.tt("dve", t0[:], t0[:], t1[:], ALU.add)
    B.tt("dve", cre[:], t0[:], den[:], ALU.mult)
    B.tt("dve", t0[:], abim[:], lr, ALU.mult)
    B.tt("dve", t1[:], nr[:], li, ALU.mult)
    B.tt("dve", t0[:], t0[:], t1[:], ALU.subtract)
    B.tt("dve", cim[:], t0[:], den[:], ALU.mult)
    if DEBUG_STOP == "abar":
        B.S.mute = True
    B.memset("pool", V(CS.t[:, :, 0, 0:1], CS[:].keys), 1.0)
    B.memset("pool", V(CS.t[:, :, 1, 0:1], CS[:].keys), 0.0)
    tb = [B.tile("l0tb%d" % i, [128, 16, 64], F32) for i in range(4)]

    def bc(t, n):
        return V(t.t[:, :].rearrange("p (k o) -> p k o", o=1).broadcast_to([128, 16, n]), t[:].keys)

    for m in range(7):
        n = 1 << m
        cosn = V(CS.t[:, :, 0, 0:n], CS[:].keys)
        sinn = V(CS.t[:, :, 1, 0:n], CS[:].keys)
        tv = [V(t.t[:, :, 0:n], t[:].keys) for t in tb]
        B.tt("dve", tv[0], cosn, bc(cm, n), ALU.mult)
        B.tt("dve", tv[1], sinn, bc(sm, n), ALU.mult)
        B.tt("dve", tv[2], cosn, bc(sm, n), ALU.mult)
        B.tt("dve", tv[3], sinn, bc(cm, n), ALU.mult)
        B.tt("dve", V(CS.t[:, :, 0, n:2 * n], CS[:].keys), tv[0], tv[1], ALU.subtract)
        B.tt("dve", V(CS.t[:, :, 1, n:2 * n], CS[:].keys), tv[2], tv[3], ALU.add)
        c2, s2, cs2 = small("c2_%d" % m), small("s2_%d" % m), small("cs_%d" % m)
        B.tt("dve", c2[:], cm[:], cm[:], ALU.mult)
        B.tt("dve", s2[:], sm[:], sm[:], ALU.mult)
        B.tt("dve", cs2[:], cm[:], sm[:], ALU.mult)
        if m < 6:
            cmn, smn = small("cm_%d" % m), small("sm_%d" % m)
        else:
            cmn, smn = cT, sT
        B.tt("dve", cmn[:], c2[:], s2[:], ALU.subtract)
        B.ts("dve", smn[:], cs2[:], 2.0, None, ALU.mult)
        cm, sm = cmn, smn
    if DEBUG_STOP == "tables":
        B.S.mute = True
    bre = B.tile("l0bre", [128, 16, 16], F32)
    bim = B.tile("l0bim", [128, 16, 16], F32)
    for bt_, nm in ((bre, "ev_s5_b_re"), (bim, "ev_s5_b_im")):
        bsrc = io[nm][0].rearrange("(k g) p c -> (g p) k c", g=2)
        bk = B.fresh()
        for k0 in range(0, 16, 4):
            B.dma("sp", V(bt_.t[:, k0:k0 + 4, :], bt_[:].keys), bsrc[:, k0:k0 + 4, :], bk)
    bbre = B.tile("l0bbre", [128, 16, 16], F32)
    bbim = B.tile("l0bbim", [128, 16, 16], F32)
    tb16 = [V(t.t[:, :, 0:16], t[:].keys) for t in tb]
    B.tt("dve", tb16[0], bre[:], bc(cre, 16), ALU.mult)
    B.tt("dve", tb16[1], bim[:], bc(cim, 16), ALU.mult)
    B.tt("dve", bbre[:], tb16[0], tb16[1], ALU.subtract)
    B.tt("dve", tb16[2], bim[:], bc(cre, 16), ALU.mult)
    B.tt("dve", tb16[3], bre[:], bc(cim, 16), ALU.mult)
    B.tt("dve", bbim[:], tb16[2], tb16[3], ALU.add)
    BD = [B.tile("l0BD%d" % ri, [128, 16, 128], BF16) for ri in range(2)]
    for ri, (bd, bb) in enumerate(zip(BD, (bbre, bbim))):
        B.memset("pool", bd[:], 0.0)
        for k in range(16):
            r = k % 4
            for gg in range(2):
                pr = slice(64 * gg, 64 * gg + 64)
                c0 = 32 * r + 16 * gg
                B.copy("pool" if gg else "dve", V(bd.t[pr, k, c0:c0 + 16], bd[:].keys),
                       V(bb.t[pr, k, :], bb[:].keys))
        for kg in range(4):
            for j in range(4):
                B.transpose(V(pps.t[:, j, :], pps[:].keys), V(bd.t[:, kg * 4 + j, :], bd[:].keys), ident[:])
            B.copy("act", V(BT[ri].t[:, kg * 4:kg * 4 + 4, :], BT[ri][:].keys), pps[:])
    if DEBUG_STOP == "bbar":
        B.S.mute = True
    for ct in CT:
        B.memset("pool", ct[:], 0.0)
    Zs = [B.tile("l0Z%d" % ri, [64, 16, 128], F32) for ri in range(2)]
    for ri, nm in enumerate(("ev_s5_c_re", "ev_s5_c_im")):
        B.memset("pool", Zs[ri][:], 0.0)
        src = io[nm][0].rearrange("(k g) co p -> g co k p", g=2)
        ckey = B.fresh()
        B.dma("sp", V(Zs[ri].t[0:16, :, 0:64], Zs[ri][:].keys), src[0], ckey)
        B.dma("sp", V(Zs[ri].t[32:48, :, 64:128], Zs[ri][:].keys), src[1], ckey)
        for k in range(16):
            r = k % 4
            Zk = Zs[ri]
            B.S.op("pe", (lambda Zk=Zk, k=k: B.nc.tensor.transpose(out=pv32.t[:, 0:64], in_=Zk.t[0:64, k, :],
                                                                  identity=ident32.t[0:64, 0:64])),
                   Zk[:].keys + ident32[:].keys, pv32[:].keys)
            for gg in range(2):
                srcv = V(pv32.t[:, 32 * gg:32 * gg + 16], pv32[:].keys)
                c0 = 32 * r + 16 * gg
                if ri == 0:
                    B.copy("act", V(CT[0].t[:, k, c0:c0 + 16], CT[0][:].keys), srcv)
                    B.ts("dve", V(CT[1].t[:, k, c0:c0 + 16], CT[1][:].keys), srcv, -1.0, None, ALU.mult)
                else:
                    B.ts("dve", V(CT[2].t[:, k, c0:c0 + 16], CT[2][:].keys), srcv, -1.0, None, ALU.mult)
    if DEBUG_STOP == "cmat":
        B.S.mute = True
    cw = B.tile("l0cw", [31, 512], F32)
    B.dma("sp", cw[:], io["ev_conv_w"][0], B.fresh())
    wk = B.tile("l0wk", [128, 4, 31], F32)
    for cc in range(4):
        B.S.op("pe", (lambda cc=cc: B.nc.tensor.transpose(out=pv32.t[:, 0:31], in_=cw.t[0:31, cc * 128:(cc + 1) * 128],
                                                          identity=ident32.t[0:31, 0:31])),
               cw[:].keys + ident32[:].keys, pv32[:].keys)
        B.copy("act", V(wk.t[:, cc, :], wk[:].keys), V(pv32.t[:, 0:31], pv32[:].keys))
    for cc in range(4):
        for k in range(31):
            B.ts("dve" if (k % 2) else "pool", V(DG.t[:, cc, k, :], DG[:].keys), ident[:],
                 V(wk.t[:, cc, k:k + 1], wk[:].keys), None, ALU.mult)
    if DEBUG_STOP in ("loads", "trig", "abar", "tables", "bbar", "cmat"):
        B.S.mute = False
        B.phase_end()
        B.pstack = outer
        B.phase_end()
        return
    B.S.barrier()
    B.pstack.close()
    B.pstack = outer

    if DEBUG_STOP == "prep":
        B.phase_end()
        return
    xts = B.pool("l0xt", 2, [128, NS, D], F32, kaxes=1)
    ss = B.tile("l0ss", [128, NS], F32, kaxes=1)
    rstd = B.tile("l0rstd", [128, NS], F32, kaxes=1)
    hn = B.pool("l0hn", 2, [128, D], BF16)
    hnT = B.tile("l0hnT", [128, 8, NS, 128], BF16, kaxes=2)
    u32 = B.tile("l0u32", [128, 4, T], F32, kaxes=1)
    ubf = B.tile("l0ubf", [128, 4, T], BF16, kaxes=1)
    sga = B.tile("l0sga", [128, 4, T], BF16, kaxes=1)
    sgb = B.tile("l0sgb", [128, 4, T], BF16, kaxes=1)
    vbf = B.tile("l0vbf", [128, 4, 30 + T], BF16, kaxes=1)
    B.memset("pool", vbf[:], 0.0)
    sgl = B.pool("l0sgl", 2, [128, T], F32)
    A4 = B.pool("l0A4", 2, [128, 2, 2, T5], F32)
    rl_all = B.tile("l0rl", [128, 16, 2], F32)
    cr16 = B.tile("l0cr16", [128, 4, 16], F32)
    G = B.pool("l0G", 2, [128, 2, T5], F32)
    R = B.pool("l0R", 2, [128, 2, T5], F32)
    PA = B.pool("l0PA", 2, [128, 2, T5], BF16)
    PB = B.pool("l0PB", 2, [128, 2, T5], BF16)
    y32 = B.pool("l0y32", 1, [128, T], F32)
    z32 = B.tile("l0z32", [128, 4, T], F32, kaxes=1)
    zbf = B.tile("l0zbf", [128, 4, T], BF16, kaxes=1)
    sig = sgl
    c32 = B.tile("l0c32", [128, 4, T], F32, kaxes=1)
    cbf = B.tile("l0cbf", [128, 4, T], BF16, kaxes=1)
    csq = B.tile("l0csq", [128, 4, T], BF16, kaxes=1)
    mus = B.tile("l0mus", [128, T], F32)
    var = B.tile("l0var", [128, T], F32)
    rsd = B.tile("l0rsd", [128, T], F32)
    cn = B.pool("l0cn", 2, [128, T], F32)
    yab = B.tile("l0yab", [128, 8, T], BF16, kaxes=1)
    pb = B.pool("l0pb", 2, [128, 512], F32, space="psum")
    pS = B.pool("l0pS", 2, [128, 2, T5], F32, space="psum")
    yps = B.pool("l0yps", 1, [128, T], F32, space="psum")
    pst = B.pool("l0pst", 2, [128, T], F32, space="psum")
    ptr = B.tile("l0ptr", [128, 4, 128], BF16, space="psum")

    def load(i):
        B.dma("sp", xts[i % 2][:], rows_view(xin, i * T, NS), "ld_x%d" % (i % 2))

    def proj(c):
        p = B.nxt(pb)
        pv = V(p.t[:, 0:T], p[:].keys)
        for kc in range(8):
            B.mm(pv, V(win.t[:, kc, c * 128:(c + 1) * 128], win[:].keys),
                 V(hnT.t[:, kc, :, :], hnT[:, kc].keys), start=(kc == 0), stop=(kc == 7))
        return pv

    load(0)
    for i in range(nt):
        if DEBUG_NT is not None and i >= DEBUG_NT:
            break
        if i + 1 < nt:
            load(i + 1)
        xt = xts[i % 2]
        pre = i < npre
        lastpre = i == npre - 1
        for s in range(NS):
            h = B.nxt(hn)
            rmsnorm_tile(B, xt[:, s, :], ss[:, s:s + 1], rstd[:, s:s + 1], g, h[:], h[:])
            for kg in range(2):
                for j in range(4):
                    kc = kg * 4 + j
                    B.transpose(V(ptr.t[:, j, :], ptr[:].keys), V(h.t[:, kc * 128:(kc + 1) * 128], h[:].keys),
                                ident[:])
                B.copy("act" if kg == 0 else "dve", hnT[:, kg * 4:(kg + 1) * 4, s, :], ptr[:])
        for cc in range(4):
            pv = proj(cc)
            if pre:
                B.copy("act", ubf[:, cc, :], pv)
            else:
                B.copy("act", u32[:, cc, :], pv)
                B.copy("pool", ubf[:, cc, :], u32[:, cc, :])
        side = []
        if lastpre:
            for cc in range(4):
                sg = B.nxt(sgl)
                B.act(sg[:], proj(12 + cc), AF.Sigmoid)
                B.tt("dve", V(vbf.t[:, cc, 30:30 + T], vbf[:, cc].keys), proj(8 + cc), sg[:], ALU.mult)
        if not pre:
            def w_sga(cc):
                B.act(sga[:, cc, :], proj(4 + cc), AF.Silu)

            def w_v(cc):
                sg = B.nxt(sgl)
                B.act(sg[:], proj(12 + cc), AF.Sigmoid)
                B.tt("dve", V(vbf.t[:, cc, 30:30 + T], vbf[:, cc].keys), proj(8 + cc), sg[:], ALU.mult)

            def w_sgb(cc):
                B.act(sgb[:, cc, :], proj(16 + cc), AF.Silu)

            def w_conv(cc):
                p = B.nxt(pb)
                pv = V(p.t[:, 0:T], p[:].keys)
                for k in range(31):
                    B.mm(pv, V(DG.t[:, cc, k, :], DG[:].keys), V(vbf.t[:, cc, k:k + T], vbf[:, cc].keys),
                         start=(k == 0), stop=(k == 30))
                bcc = V(convb.t[:, cc:cc + 1], convb[:].keys)
                B.act(c32[:, cc, :], pv, AF.Identity, bias=bcc)
                B.act(cbf[:, cc, :], pv, AF.Identity, bias=bcc)
                B.act(csq[:, cc, :], pv, AF.Square, bias=bcc)

            def w_ln():
                B.copy("pool", V(vbf.t[:, :, 0:30], vbf[:].keys), V(vbf.t[:, :, T:T + 30], vbf[:].keys))
                pmu, pm2 = B.nxt(pst), B.nxt(pst)
                for cc in range(4):
                    B.mm(pmu[:], onesM[:], cbf[:, cc, :], start=(cc == 0), stop=(cc == 3))
                for cc in range(4):
                    B.mm(pm2[:], onesM[:], csq[:, cc, :], start=(cc == 0), stop=(cc == 3))
                B.copy("act", mus[:], pmu[:])
                B.tt("dve", var[:], mus[:], mus[:], ALU.mult)
                B.tt("dve", var[:], pm2[:], var[:], ALU.subtract)
                B.act(rsd[:], var[:], AF.Sqrt, bias=EPS)
                B.recip(rsd[:], rsd[:])

            def w_cn(cc):
                c_ = B.nxt(cn)
                B.tt("dve", c_[:], c32[:, cc, :], mus[:], ALU.subtract)
                B.tt("pool", c_[:], c_[:], rsd[:], ALU.mult)
                B.act(c_[:], c_[:], AF.Silu, bias=V(lnb.t[:, cc:cc + 1], lnb[:].keys),
                      scale=V(lng.t[:, cc:cc + 1], lng[:].keys))
                B.tt("dve", yab[:, 4 + cc, :], c_[:], sgb[:, cc, :], ALU.mult)

            for cc in range(4):
                side.append(lambda cc=cc: w_sga(cc))
            for cc in range(4):
                side.append(lambda cc=cc: w_v(cc))
            for cc in range(4):
                side.append(lambda cc=cc: w_sgb(cc))
            for cc in range(4):
                side.append(lambda cc=cc: w_conv(cc))
            side.append(w_ln)
            for cc in range(4):
                side.append(lambda cc=cc: w_cn(cc))
        its = [(h5, cc, r) for h5 in range(2) for cc in range(4) for r in range(4)]
        st = [dict() for _ in its]

        def s1(n):
            h5, cc, r = its[n]
            k = cc * 4 + r
            c5 = slice(h5 * T5, (h5 + 1) * T5)
            P = B.nxt(pS)
            uv = V(ubf.t[:, cc, c5], ubf[:, cc].keys)
            for x_ in range(2):
                B.mm(V(P.t[:, x_, :], P[:].keys), V(BT[x_].t[:, k, :], BT[x_][:].keys), uv)
            pbc = V(P.t[:, :, :].rearrange("p x (o j) -> p x o j", o=1).broadcast_to([128, 2, 2, T5]), P[:].keys)
            csb = V(CS.t[:, k:k + 1, :, :].broadcast_to([128, 2, 2, T5]), CS[:].keys)
            a_ = B.nxt(A4)
            B.tt("dve", a_[:], pbc, csb, ALU.mult)
            st[n]["a"] = a_

        def s2(n):
            a_ = st[n]["a"]
            gt = B.nxt(G)
            B.tt("pool", V(gt.t[:, 0, :], gt[:].keys), V(a_.t[:, 0, 0, :], a_[:].keys),
                 V(a_.t[:, 1, 1, :], a_[:].keys), ALU.add)
            B.tt("pool", V(gt.t[:, 1, :], gt[:].keys), V(a_.t[:, 1, 0, :], a_[:].keys),
                 V(a_.t[:, 0, 1, :], a_[:].keys), ALU.subtract)
            st[n]["g"] = gt

        def s3(n):
            h5, cc, r = its[n]
            k = cc * 4 + r
            gt = st[n]["g"]
            rt = B.nxt(R)
            magk = V(mag.t[:, k:k + 1].broadcast_to([128, T5]), mag[:].keys)
            for ri in range(2):
                B.scan(V(rt.t[:, ri, :], rt[:].keys), magk, V(gt.t[:, ri, :], gt[:].keys),
                       V(init[ri].t[:, k:k + 1], init[ri][:, k].keys))
            st[n]["r"] = rt

        def s3b(n):
            h5, cc, r = its[n]
            k = cc * 4 + r
            rt = st[n]["r"]
            B.copy("pool", V(rl_all.t[:, k, :], rl_all[:].keys), V(rt.t[:, :, T5 - 1], rt[:].keys))
            if not pre:
                pa = B.nxt(PA)
                B.tt("dve", pa[:], rt[:], V(CS.t[:, k, :, :], CS[:].keys), ALU.mult)
                st[n]["pa"] = pa
                pb_ = B.nxt(PB)
                B.tt("pool", V(pb_.t[:, 0, :], pb_[:].keys), V(rt.t[:, 0, :], rt[:].keys),
                     V(CS.t[:, k, 1, :], CS[:].keys), ALU.mult)
                B.tt("pool", V(pb_.t[:, 1, :], pb_[:].keys), V(rt.t[:, 1, :], rt[:].keys),
                     V(CS.t[:, k, 0, :], CS[:].keys), ALU.mult)
                st[n]["pb"] = pb_
            if cc == 3 and r == 3:
                rlr = V(rl_all.t[:, :, 0], rl_all[:].keys)
                rli = V(rl_all.t[:, :, 1], rl_all[:].keys)
                crv = [V(cr16.t[:, j_, :], cr16[:].keys) for j_ in range(4)]
                B.tt("pool", crv[0], rli, sT[:], ALU.mult)
                B.tt("pool", crv[1], rli, cT[:], ALU.mult)
                B.tt("pool", crv[2], rlr, cT[:], ALU.mult)
                B.tt("pool", crv[3], rlr, sT[:], ALU.mult)
                B.tt("pool", init[0][:], crv[2], crv[0], ALU.subtract)
                B.tt("pool", init[1][:], crv[3], crv[1], ALU.add)

        def s4(n):
            if pre:
                return
            h5, cc, r = its[n]
            k = cc * 4 + r
            c5 = slice(h5 * T5, (h5 + 1) * T5)
            if r == 0:
                st[n]["yp"] = B.nxt(yps)
            else:
                st[n]["yp"] = st[n - 1]["yp"]
            yp = st[n]["yp"]
            pa, pb_ = st[n]["pa"], st[n]["pb"]
            yv = V(yp.t[:, 0:T5], yp[:].keys)
            B.mm(yv, V(CT[0].t[:, k, :], CT[0][:].keys), V(pa.t[:, 0, :], pa[:].keys), start=(r == 0), stop=False)
            B.mm(yv, V(CT[1].t[:, k, :], CT[1][:].keys), V(pa.t[:, 1, :], pa[:].keys), start=False, stop=False)
            B.mm(yv, V(CT[2].t[:, k, :], CT[2][:].keys), V(pb_.t[:, 0, :], pb_[:].keys), start=False, stop=False)
            B.mm(yv, V(CT[2].t[:, k, :], CT[2][:].keys), V(pb_.t[:, 1, :], pb_[:].keys), start=False,
                 stop=(r == 3))
            if r == 3:
                y = B.nxt(y32)
                yh = V(y.t[:, 0:T5], y[:].keys)
                B.stt("dve", yh, V(u32.t[:, cc, c5], u32[:, cc].keys), V(Dt.t[:, cc:cc + 1], Dt[:].keys),
                      yv, ALU.mult, ALU.add)
                B.act(V(z32.t[:, cc, c5], z32[:, cc].keys), yh, AF.Gelu_apprx_tanh)
                B.copy("pool", V(zbf.t[:, cc, c5], zbf[:, cc].keys), V(z32.t[:, cc, c5], z32[:, cc].keys))

        NI = len(its)
        for t in range(NI + 4):
            if t < NI:
                s1(t)
            if 0 <= t - 3 < NI:
                s3b(t - 3)
            if 0 <= t - 4 < NI:
                s4(t - 4)
            if 0 <= t - 1 < NI:
                s2(t - 1)
            if 0 <= t - 2 < NI:
                s3(t - 2)
            if side and t >= 1:
                side.pop(0)()
        while side:
            side.pop(0)()
        if pre:
            if lastpre:
                B.copy("pool", V(vbf.t[:, :, 0:30], vbf[:].keys), V(vbf.t[:, :, T:T + 30], vbf[:].keys))
            continue
        if DEBUG_STOP == "e_s5" and not pre:
            B.S.mute = True
        for co in range(4):
            p = B.nxt(pb)
            pv = V(p.t[:, 0:T], p[:].keys)
            for ci in range(4):
                B.mm(pv, V(gluw.t[:, ci, co * 128:(co + 1) * 128], gluw[:].keys), zbf[:, ci, :],
                     start=(ci == 0), stop=(ci == 3))
            sg = B.nxt(sig)
            B.act(sg[:], pv, AF.Sigmoid, bias=V(glub.t[:, co:co + 1], glub[:].keys))
            B.tt("pool", sg[:], sg[:], sga[:, co, :], ALU.mult)
            B.tt("dve", yab[:, co, :], z32[:, co, :], sg[:], ALU.mult)
        if DEBUG_STOP == "e_glu" and not pre:
            B.S.mute = True
        if DEBUG_STOP == "e_ln" and not pre:
            B.S.mute = True
        for s in range(NS):
            for n in range(2):
                p = B.nxt(pb)
                for kc in range(8):
                    B.mm(p[:], V(yab.t[:, kc, s * 128:(s + 1) * 128], yab[:, kc].keys),
                         V(wo.t[:, kc, n * 512:(n + 1) * 512], wo[:].keys), start=(kc == 0), stop=(kc == 7))
                xv = V(xt.t[:, s, n * 512:(n + 1) * 512], xt[:, s].keys)
                B.tt("dve", xv, xv, p[:], ALU.add)
        if DEBUG_STOP != "nostore":
            B.dma("pool" if DEBUG_STOP == "poolstore" else "sp", rows_view(xout, (i - npre) * T, NS), xt[:],
                  "st_o%d" % (i % 2))
    B.S.mute = False
    B.phase_end()


W_NAMES = ["mem_norm_g", "ev_norm_g", "ev_w_in", "ev_s5_lambda_re", "ev_s5_lambda_im", "ev_s5_log_dt",
           "ev_s5_b_re", "ev_s5_b_im", "ev_s5_c_re", "ev_s5_c_im", "ev_s5_d", "ev_s5_glu_w",
           "ev_s5_glu_b", "ev_conv_w", "ev_conv_b", "ev_conv_ln_g", "ev_conv_ln_b", "ev_w_out",
           "od_norm_g", "od_w_in", "od_rel_bias", "od_w_out", "xa_norm_g", "xa_w_qg", "xa_w_kv",
           "xa_w_o", "final_norm_g"]


def setup_consts(B, io):
    C = {}
    ident = B.tile("ident", [128, 128], BF16, persistent=True)
    B.dma("pool", ident[:], io["c_ident"][:, :], "ld_ident")
    ones = B.tile("ones", [128, 128], BF16, persistent=True)
    B.memset("pool", ones[:], 1.0)
    C["ident"] = ident
    C["ones"] = ones
    halfpi = B.tile("halfpi", [128, 1], F32, persistent=True)
    B.memset("pool", halfpi[:], 1.5707963267948966)
    C["halfpi"] = halfpi
    return C


def build_program(mode="full"):
    nc = bass.Bass("TRN2", target_bir_lowering=False)
    io = {}

    def din(name, shape):
        io[name] = nc.dram_tensor(name, list(shape), F32, kind="ExternalInput").ap()

    def dout(name, shape):
        io[name] = nc.dram_tensor(name, list(shape), F32, kind="ExternalOutput").ap()

    def dint(name, shape):
        io[name] = nc.dram_tensor(name, list(shape), F32, kind="Internal").ap()

    din("c_ident", [128, 128])
    din("mem", [MEM, D])
    din("mem_norm_g", [1, D])
    din("xa_norm_g", [2, D])
    din("xa_w_qg", [2, D, 2048])
    din("xa_w_kv", [2, D, 2048])
    din("xa_w_o", [2, D, D])
    din("final_norm_g", [1, D])
    B = Builder(nc)
    outs = []
    if mode == "xa_test":
        din("xin", [OWN, D])
        dout("out", [OWN, D])
        C = setup_consts(B, io)
        phase_xa(B, C, io, 1, io["xin"], io["out"], OWN, final=True)
        outs = ["st_o0", "st_o1"]
    if mode == "l0_test":
        din("xin", [PRE + EXT, D])
        dout("out", [EXT, D])
        for nm, shp in (("ev_norm_g", [1, D]), ("ev_w_in", [1, D, 2560]), ("ev_s5_lambda_re", [1, 32, 64]),
                        ("ev_s5_lambda_im", [1, 32, 64]), ("ev_s5_log_dt", [1, 32]),
                        ("ev_s5_b_re", [1, 32, 64, 16]), ("ev_s5_b_im", [1, 32, 64, 16]),
                        ("ev_s5_c_re", [1, 32, 16, 64]), ("ev_s5_c_im", [1, 32, 16, 64]),
                        ("ev_s5_d", [1, 512]), ("ev_s5_glu_w", [1, 512, 512]), ("ev_s5_glu_b", [1, 512]),
                        ("ev_conv_w", [1, 31, 512]), ("ev_conv_b", [1, 512]), ("ev_conv_ln_g", [1, 512]),
                        ("ev_conv_ln_b", [1, 512]), ("ev_w_out", [1, D, D])):
            din(nm, shp)
        C = setup_consts(B, io)
        phase_l0(B, C, io, io["xin"], io["out"])
        outs = [k for k in ("st_o0", "st_o1") if k in B.S.dsem]
    if mode == "l1_test":
        din("xin", [EXT, D])
        dout("out", [OWN, D])
        din("od_norm_g", [1, D])
        din("od_w_in", [1, D, 4096])
        din("od_w_out", [1, D, D])
        din("c_bias", [80, 128, 128])
        din("c_hmask", [128, 128])
        C = setup_consts(B, io)
        phase_l1(B, C, io, io["xin"], io["out"])
        outs = ["st_o0", "st_o1"]
    if mode == "full":
        din("x_ext", [PRE + EXT, D])
        dout("out", [OWN, D])
        dint("xa", [EXT, D])
        dint("xb", [EXT, D])
        dint("xc", [OWN, D])
        for nm, shp in (("ev_norm_g", [1, D]), ("ev_w_in", [1, D, 2560]), ("ev_s5_lambda_re", [1, 32, 64]),
                        ("ev_s5_lambda_im", [1, 32, 64]), ("ev_s5_log_dt", [1, 32]),
                        ("ev_s5_b_re", [1, 32, 64, 16]), ("ev_s5_b_im", [1, 32, 64, 16]),
                        ("ev_s5_c_re", [1, 32, 16, 64]), ("ev_s5_c_im", [1, 32, 16, 64]),
                        ("ev_s5_d", [1, 512]), ("ev_s5_glu_w", [1, 512, 512]), ("ev_s5_glu_b", [1, 512]),
                        ("ev_conv_w", [1, 31, 512]), ("ev_conv_b", [1, 512]), ("ev_conv_ln_g", [1, 512]),
                        ("ev_conv_ln_b", [1, 512]), ("ev_w_out", [1, D, D]),
                        ("od_norm_g", [1, D]), ("od_w_in", [1, D, 4096]), ("od_w_out", [1, D, D]),
                        ("c_bias", [80, 128, 128]), ("c_hmask", [128, 128])):
            din(nm, shp)
        C = setup_consts(B, io)
        phase_l0(B, C, io, io["x_ext"], io["xa"])
        phase_xa(B, C, io, 0, io["xa"], io["xb"], EXT, final=False)
        phase_l1(B, C, io, io["xb"], io["xc"])
        phase_xa(B, C, io, 1, io["xc"], io["out"], OWN, final=True)
        outs = ["st_o0", "st_o1"]
    B.S.flush(final_wait_keys=outs)
    B.gstack.close()
    return nc


def host_bias_tiles(rel_bias):
    k = np.arange(128)[:, None]
    q = np.arange(128)[None, :]
    out = np.empty((16, 5, 128, 128), np.float32)
    for dl in range(5):
        rel = np.clip(q - k + 128 * dl, -128, 128) + 128
        kb = (k >= 64).astype(np.int64)
        qb = (q >= 64).astype(np.int64)
        dc = -2 * dl + kb - qb
        masked = (dc > 0) | (dc < -8)
        for h in range(16):
            out[h, dl] = np.where(masked, np.float32(NEG), rel_bias[h][rel])
    return np.ascontiguousarray(out.transpose(1, 0, 2, 3)).reshape(80, 128, 128)


_PROGRAM = None


def kernel(**inputs):
    global _PROGRAM
    if _PROGRAM is None:
        _PROGRAM = build_program("full")
    nc = _PROGRAM
    f32 = np.float32
    x = np.asarray(inputs["x"], f32)
    mem = np.asarray(inputs["mem"], f32)
    shared = {"c_ident": np.eye(128, dtype=f32),
              "c_bias": host_bias_tiles(np.asarray(inputs["od_rel_bias"], f32)[0]),
              "mem_norm_g": np.asarray(inputs["mem_norm_g"], f32)[None, :],
              "final_norm_g": np.asarray(inputs["final_norm_g"], f32)[None, :]}
    for k in W_NAMES:
        if k not in shared and k != "od_rel_bias":
            shared[k] = np.ascontiguousarray(np.asarray(inputs[k], f32))
    in_maps = []
    for c in range(NCORES):
        b, hh = divmod(c, 2)
        m = dict(shared)
        if hh == 0:
            m["x_ext"] = np.concatenate([np.zeros((OWN, D), f32), x[b, :OWN]], 0)
            m["c_hmask"] = np.full((128, 128), NEG, f32)
        else:
            m["x_ext"] = np.ascontiguousarray(x[b])
            m["c_hmask"] = np.zeros((128, 128), f32)
        m["mem"] = np.ascontiguousarray(mem[b])
        in_maps.append(m)
    res = run_bass_kernel_spmd(nc, in_maps, core_ids=list(range(NCORES)))
    out = np.empty((NB, SEQ, D), f32)
    for c in range(NCORES):
        b, hh = divmod(c, 2)
        out[b, hh * OWN:(hh + 1) * OWN] = res.results[c]["out"]
    return out
```
